# Optimizing a Trainium2 kernel written in Bass

```python
import math
import jax, jax.numpy as jnp
from jax import lax
import numpy as np

D_MODEL = 1024
BATCH = 2
SEQ = 8192
DEPTH = 1

N_HEADS = 8
HEAD_DIM = 64
N_KV_HEADS = 2
GQA_REP = N_HEADS // N_KV_HEADS
D_ATTN = N_HEADS * HEAD_DIM
D_KV = N_KV_HEADS * HEAD_DIM
D_CONV = D_MODEL - D_ATTN
N_CONV_GROUPS = 8
CONV_K = 3
D_MIX = D_ATTN + D_CONV
CMP_BLOCK = 32
CMP_STRIDE = 16
CMP_HIDDEN = 2 * HEAD_DIM
SEL_BLOCK = 64
N_SEL = 16
WINDOW = 512
Q_BLOCK = 128
N_BUCKETS = 32
MAX_DISTANCE = 128
EPS = 1e-6
NEG = -1e30
FORCE_SCORE = 1e3
IN_SPLITS = (D_ATTN, 6 * D_KV, 3 * N_HEADS, D_ATTN, D_CONV, D_CONV, D_CONV, D_CONV)
IN_COLS = sum(IN_SPLITS)

kernel_name = "hymba_nsa_shortconv_hybrid"


def rms_norm(x, w):
    xf = x.astype(jnp.float32)
    y = xf * lax.rsqrt(jnp.mean(xf * xf, axis=-1, keepdims=True) + EPS)
    return (y * w.astype(jnp.float32)).astype(x.dtype)


def t5_bucket(dist):
    n = jnp.maximum(dist, 0)
    max_exact = N_BUCKETS // 2
    nf = jnp.maximum(n, 1).astype(jnp.float32)
    large = max_exact + (jnp.log(nf / max_exact) / math.log(MAX_DISTANCE / max_exact)
                         * (N_BUCKETS - max_exact)).astype(jnp.int32)
    large = jnp.minimum(large, N_BUCKETS - 1)
    return jnp.where(n < max_exact, n, large)


def masked_softmax(s, mask):
    s = jnp.where(mask, s, NEG)
    m = jnp.max(s, axis=-1, keepdims=True)
    p = jnp.where(mask, jnp.exp(s - m), 0.0)
    return p / jnp.maximum(jnp.sum(p, axis=-1, keepdims=True), 1e-30)


def compress(kv, w1, w2, pe):
    b, s, g, dk = kv.shape
    n_chunks = s // CMP_STRIDE
    per_block = CMP_BLOCK // CMP_STRIDE
    n_cmp = n_chunks - per_block + 1
    chunks = kv.reshape(b, n_chunks, CMP_STRIDE, g, dk)
    hid = 0.0
    for m in range(per_block):
        lo, hi = m * CMP_STRIDE, (m + 1) * CMP_STRIDE
        seg = chunks[:, m:m + n_cmp] + pe[lo:hi][None, None, :, None, :]
        hid = hid + jnp.einsum('bcjgd,jdh->bgch', seg, w1[lo:hi])
    return jnp.einsum('bgch,hd->bgcd', jax.nn.silu(hid), w2)


def cmp_to_sel_overlap(n_cmp, n_sel):
    start = np.arange(n_cmp) * CMP_STRIDE
    end = start + CMP_BLOCK
    sel_start = np.arange(n_sel) * SEL_BLOCK
    ov = (start[:, None] < sel_start[None, :] + SEL_BLOCK) & (end[:, None] > sel_start[None, :])
    return jnp.asarray(ov.astype(np.float32))


def nsa_mixer(q, k_cmp, v_cmp, k_slc, v_slc, k_win, v_win, gates,
              w_ck1, w_ck2, pe_k, w_cv1, w_cv2, pe_v, rel_bias):
    b, s = q.shape[:2]
    G, R, dk = N_KV_HEADS, GQA_REP, HEAD_DIM
    qh = (q * (HEAD_DIM ** -0.5)).reshape(b, s, G, R, dk)
    kc = compress(k_cmp, w_ck1, w_ck2, pe_k)
    vc = compress(v_cmp, w_cv1, w_cv2, pe_v)
    n_cmp = kc.shape[2]
    n_sel = s // SEL_BLOCK
    k_top = min(N_SEL, n_sel)
    cmp_end = jnp.arange(n_cmp, dtype=jnp.int32) * CMP_STRIDE + CMP_BLOCK - 1
    overlap = cmp_to_sel_overlap(n_cmp, n_sel)
    ks_blocks = k_slc.transpose(0, 2, 1, 3).reshape(b, G, n_sel, SEL_BLOCK, dk)
    vs_blocks = v_slc.transpose(0, 2, 1, 3).reshape(b, G, n_sel, SEL_BLOCK, dk)
    kw = jnp.pad(k_win.transpose(0, 2, 1, 3), ((0, 0), (0, 0), (WINDOW, 0), (0, 0)))
    vw = jnp.pad(v_win.transpose(0, 2, 1, 3), ((0, 0), (0, 0), (WINDOW, 0), (0, 0)))
    tbl_g = rel_bias.reshape(N_BUCKETS, G, R).transpose(1, 0, 2)
    bi = jnp.arange(b)[:, None, None, None]
    gi = jnp.arange(G)[None, :, None, None]
    sel_ids = jnp.arange(n_sel, dtype=jnp.int32)

    n_qb = s // Q_BLOCK
    q_blocks = qh.reshape(b, n_qb, Q_BLOCK, G, R, dk).transpose(1, 0, 3, 4, 2, 5)
    g_blocks = gates.reshape(b, n_qb, Q_BLOCK, 3, G, R).transpose(1, 3, 0, 4, 5, 2)

    def head_bias(bucket):
        bias = rel_bias[bucket].astype(jnp.float32)
        return bias.transpose(2, 0, 1).reshape(G, R, bucket.shape[0], bucket.shape[1])

    def block_fn(args):
        qb, gb, i = args
        q0 = i * Q_BLOCK
        t = q0 + jnp.arange(Q_BLOCK, dtype=jnp.int32)

        s_c = jnp.einsum('bgrqd,bgcd->bgrqc', qb, kc).astype(jnp.float32)
        dist_c = t[:, None] - cmp_end[None, :]
        p_c = masked_softmax(s_c + head_bias(t5_bucket(dist_c)), dist_c >= 0)
        o_c = jnp.einsum('bgrqc,bgcd->bgrqd', p_c.astype(vc.dtype), vc)

        imp = jnp.einsum('bgrqc,cn->bgqn', p_c, overlap)
        blk_t = t // SEL_BLOCK
        causal_blk = sel_ids[None, :] <= blk_t[:, None]
        forced = (sel_ids[None, :] == 0) | (sel_ids[None, :] == blk_t[:, None]) | (sel_ids[None, :] == blk_t[:, None] - 1)
        score = jnp.where(forced, FORCE_SCORE, jnp.where(causal_blk, imp, -1.0))
        _, idx = lax.top_k(score, k_top)
        k_g = ks_blocks[bi, gi, idx]
        v_g = vs_blocks[bi, gi, idx].reshape(b, G, Q_BLOCK, k_top * SEL_BLOCK, dk)
        s_s = jnp.einsum('bgrqd,bgqnkd->bgrqnk', qb, k_g).astype(jnp.float32)
        pos_s = idx[..., None] * SEL_BLOCK + jnp.arange(SEL_BLOCK, dtype=jnp.int32)
        dist_s = t[None, None, :, None, None] - pos_s
        bias_s = tbl_g[gi[..., None], t5_bucket(dist_s)].astype(jnp.float32)
        bias_s = bias_s.transpose(0, 1, 5, 2, 3, 4)
        flat = (b, G, R, Q_BLOCK, k_top * SEL_BLOCK)
        mask_s = (dist_s >= 0)[:, :, None].reshape(b, G, 1, Q_BLOCK, k_top * SEL_BLOCK)
        p_s = masked_softmax((s_s + bias_s).reshape(flat), mask_s)
        o_s = jnp.einsum('bgrqm,bgqmd->bgrqd', p_s.astype(v_g.dtype), v_g)

        kw_blk = lax.dynamic_slice_in_dim(kw, q0, WINDOW + Q_BLOCK, axis=2)
        vw_blk = lax.dynamic_slice_in_dim(vw, q0, WINDOW + Q_BLOCK, axis=2)
        pos_w = q0 - WINDOW + jnp.arange(WINDOW + Q_BLOCK, dtype=jnp.int32)
        dist_w = t[:, None] - pos_w[None, :]
        mask_w = (dist_w >= 0) & (dist_w < WINDOW) & (pos_w[None, :] >= 0)
        s_w = jnp.einsum('bgrqd,bgkd->bgrqk', qb, kw_blk).astype(jnp.float32)
        p_w = masked_softmax(s_w + head_bias(t5_bucket(dist_w)), mask_w)
        o_w = jnp.einsum('bgrqk,bgkd->bgrqd', p_w.astype(vw_blk.dtype), vw_blk)

        return gb[0][..., None] * o_c + gb[1][..., None] * o_s + gb[2][..., None] * o_w

    out = lax.map(block_fn, (q_blocks, g_blocks, jnp.arange(n_qb, dtype=jnp.int32)))
    return out.transpose(1, 0, 4, 2, 3, 5).reshape(b, s, D_ATTN)


def short_conv_mixer(h, b_gate, c_gate, conv_w):
    u = c_gate * h
    y = lax.conv_general_dilated(u, conv_w[:, None, :].astype(u.dtype), window_strides=(1,),
                                 padding=[(CONV_K - 1, 0)],
                                 dimension_numbers=('NWC', 'WIO', 'NWC'),
                                 feature_group_count=D_CONV)
    return b_gate * y


def setup_inputs(seed: int = 0) -> dict:
    key = jax.random.key(seed)
    ks = jax.random.split(key, 16)
    f32 = jnp.float32
    nrm = lambda k, shape, scale: jax.random.normal(k, shape, f32) * scale
    return {
        "x": nrm(ks[0], (BATCH, SEQ, D_MODEL), 1.0),
        "norm_w": 1.0 + nrm(ks[1], (DEPTH, D_MODEL), 0.01),
        "w_in": nrm(ks[2], (DEPTH, D_MODEL, IN_COLS), D_MODEL ** -0.5),
        "w_ck1": nrm(ks[3], (DEPTH, CMP_BLOCK, HEAD_DIM, CMP_HIDDEN), (CMP_BLOCK * HEAD_DIM) ** -0.5),
        "w_ck2": nrm(ks[4], (DEPTH, CMP_HIDDEN, HEAD_DIM), CMP_HIDDEN ** -0.5),
        "pe_k": nrm(ks[5], (DEPTH, CMP_BLOCK, HEAD_DIM), 0.02),
        "w_cv1": nrm(ks[6], (DEPTH, CMP_BLOCK, HEAD_DIM, CMP_HIDDEN), (CMP_BLOCK * HEAD_DIM) ** -0.5),
        "w_cv2": nrm(ks[7], (DEPTH, CMP_HIDDEN, HEAD_DIM), CMP_HIDDEN ** -0.5),
        "pe_v": nrm(ks[8], (DEPTH, CMP_BLOCK, HEAD_DIM), 0.02),
        "conv_w": nrm(ks[9], (DEPTH, CONV_K, D_CONV), CONV_K ** -0.5),
        "w_out": nrm(ks[10], (DEPTH, D_MIX, D_MODEL), D_MIX ** -0.5),
        "rel_bias": nrm(ks[11], (N_BUCKETS, N_HEADS), 0.5),
        "final_norm_w": 1.0 + nrm(ks[12], (D_MODEL,), 0.01),
    }


def reference(x, norm_w, w_in, w_ck1, w_ck2, pe_k, w_cv1, w_cv2, pe_v, conv_w, w_out,
              rel_bias, final_norm_w):
    b, s, _ = x.shape
    split_at = [int(c) for c in np.cumsum(IN_SPLITS)[:-1]]
    for l in range(DEPTH):
        h = rms_norm(x, norm_w[l])
        proj = h @ w_in[l]
        q, kv, gate_logits, z_attn, conv_h, conv_b, conv_c, z_conv = jnp.split(proj, split_at, axis=-1)
        k_cmp, v_cmp, k_slc, v_slc, k_win, v_win = [
            a.reshape(b, s, N_KV_HEADS, HEAD_DIM) for a in jnp.split(kv, 6, axis=-1)]
        gates = jax.nn.sigmoid(gate_logits).reshape(b, s, 3, N_HEADS)
        attn = nsa_mixer(q.reshape(b, s, N_HEADS, HEAD_DIM), k_cmp, v_cmp, k_slc, v_slc,
                         k_win, v_win, gates, w_ck1[l], w_ck2[l], pe_k[l],
                         w_cv1[l], w_cv2[l], pe_v[l], rel_bias)
        conv = short_conv_mixer(conv_h, conv_b, conv_c, conv_w[l])
        mixed = jnp.concatenate([attn * jax.nn.silu(z_attn), conv * jax.nn.silu(z_conv)], axis=-1)
        x = x + mixed @ w_out[l]
    return rms_norm(x, final_norm_w)
```

```python
import os
from contextlib import ExitStack

import numpy as np
import ml_dtypes

import concourse.bass as bass
import concourse.mybir as mybir
from concourse.bass_utils import run_bass_kernel_spmd

F32 = mybir.dt.float32
BF16 = mybir.dt.bfloat16
AF = mybir.ActivationFunctionType
ALU = mybir.AluOpType
NPBF = ml_dtypes.bfloat16

NEGBIG = -30000.0
EPS = 1e-6
SEQ = 8192
DM = 1024
NCOLS = 3864
LEN_S, LEN_W, LEN_C = 1151, 1535, 2544
LEN_F = LEN_S + LEN_W + LEN_C
OFF_W = LEN_S
OFF_C = LEN_S + LEN_W

DEBUG = os.environ.get("KDEBUG", "")
KCUT = int(os.environ.get("KCUT", "0"))


class Buf:
    __slots__ = ("name", "w", "r")

    def __init__(self, name=""):
        self.name = name
        self.w = None
        self.r = []


class Chan:
    __slots__ = ("sem", "n")

    def __init__(self, sem):
        self.sem = sem
        self.n = 0


class Op:
    __slots__ = ("eng", "fn", "deps", "chan", "chan_n", "sig", "signo", "is_dma", "chan_waits")

    def __init__(self, eng, fn, is_dma):
        self.eng = eng
        self.fn = fn
        self.deps = []
        self.chan = None
        self.chan_n = 0
        self.sig = False
        self.signo = 0
        self.is_dma = is_dma
        self.chan_waits = None


class Sched:
    ENGS = ("pe", "act", "dve", "pool", "sp")

    def __init__(self, nc):
        self.nc = nc
        self.ops = {e: [] for e in self.ENGS}
        self.all_ops = []
        self.cur_barrier = None
        self.chans = []

    def new_chan(self, sem):
        c = Chan(sem)
        self.chans.append(c)
        return c

    def _dep(self, op, other):
        if other is None or other is op:
            return
        if (not other.is_dma) and (not op.is_dma) and other.eng == op.eng and op.eng == "pe":
            return
        if other not in op.deps:
            op.deps.append(other)

    def op(self, eng, fn, reads=(), writes=(), chan=None):
        is_dma = chan is not None
        o = Op(eng, fn, is_dma)
        if self.cur_barrier is not None:
            o.deps.append(self.cur_barrier)
        for b in reads:
            self._dep(o, b.w)
        for b in writes:
            self._dep(o, b.w)
            for r in b.r:
                self._dep(o, r)
        for b in reads:
            b.r.append(o)
        for b in writes:
            b.w = o
            b.r = []
        if is_dma:
            chan.n += 1
            o.chan = chan
            o.chan_n = chan.n
        self.ops[eng].append(o)
        self.all_ops.append(o)
        return o

    def barrier(self, fn, chan):
        o = Op("sp", fn, True)
        for e in self.ENGS:
            for prev in reversed(self.ops[e]):
                if not prev.is_dma:
                    o.deps.append(prev)
                    break
        o.chan_waits = [(c, c.n) for c in self.chans if c.n > 0]
        chan.n += 1
        o.chan = chan
        o.chan_n = chan.n
        self.ops["sp"].append(o)
        self.all_ops.append(o)
        self.cur_barrier = o
        return o

    def emit(self, sems):
        nc = self.nc
        for o in self.all_ops:
            for d in o.deps:
                if not d.is_dma:
                    d.sig = True
        for e in self.ENGS:
            n = 0
            for o in self.ops[e]:
                if o.sig and not o.is_dma:
                    n += 1
                    o.signo = n
        all_chans = self.chans

        def run_engine(ename, eng):
            seen = {}
            for o in self.ops[ename]:
                need = {}
                for d in o.deps:
                    if d.is_dma:
                        key = ("c", id(d.chan))
                        val = 16 * d.chan_n
                        sem = d.chan.sem
                    else:
                        key = ("e", d.eng)
                        val = d.signo
                        sem = sems[d.eng]
                    if val > need.get(key, (0, None))[0]:
                        need[key] = (val, sem)
                if o.chan_waits:
                    for c, n in o.chan_waits:
                        key = ("c", id(c))
                        if 16 * n > need.get(key, (0, None))[0]:
                            need[key] = (16 * n, c.sem)
                for key, (val, sem) in need.items():
                    if seen.get(key, 0) >= val:
                        continue
                    seen[key] = val
                    eng.wait_ge(sem, val)
                ins = o.fn(eng)
                if o.is_dma:
                    ins.then_inc(o.chan.sem, 16)
                elif o.sig:
                    ins.then_inc(sems[ename], 1)
            if ename == "sp":
                for c in all_chans:
                    if c.n > 0 and seen.get(("c", id(c)), 0) < 16 * c.n:
                        eng.wait_ge(c.sem, 16 * c.n)
                for e2 in self.ENGS:
                    last = 0
                    for o2 in self.ops[e2]:
                        if o2.sig and not o2.is_dma:
                            last = o2.signo
                    if last > 0:
                        eng.wait_ge(sems[e2], last)

        with nc.Block() as block:
            @block.tensor
            def _(eng):
                run_engine("pe", eng)

            @block.scalar
            def _(eng):
                run_engine("act", eng)

            @block.vector
            def _(eng):
                run_engine("dve", eng)

            @block.gpsimd
            def _(eng):
                run_engine("pool", eng)

            @block.sync
            def _(eng):
                run_engine("sp", eng)


def _t5_bucket(d):
    n = np.maximum(d, 0)
    nf = np.maximum(n, 1).astype(np.float32)
    large = 16 + (np.log(nf / np.float32(16.0)) / np.float32(np.log(8.0)) * np.float32(16.0)).astype(np.int32)
    large = np.minimum(large, 31)
    return np.where(n < 16, n, large)


def _onehot_rows():
    oh = np.zeros((33, LEN_F), np.float32)

    def fill(off, length, i0, wmax):
        d = np.arange(length) - i0
        masked = d < 0
        if wmax is not None:
            masked |= d >= wmax
        bk = _t5_bucket(d)
        for i in range(length):
            if masked[i]:
                oh[32, off + i] = 1.0
            else:
                oh[bk[i], off + i] = 1.0

    fill(0, LEN_S, 511, None)
    fill(OFF_W, LEN_W, 511, 512)
    fill(OFF_C, LEN_C, 527, None)
    return oh


_CONST_CACHE = {}


def _shared_consts():
    if "c" in _CONST_CACHE:
        return _CONST_CACHE["c"]
    ident = np.eye(128, dtype=np.float32).astype(NPBF)
    antiI = np.eye(128, dtype=np.float32)[::-1].copy().astype(NPBF)
    L = np.arange(SEQ)
    eall = np.zeros((64, SEQ), np.float32)
    eall[(L // 64) % 64, L] = 1.0
    c = np.arange(512)
    j = np.arange(128)
    ov = ((16 * c[:, None] < 64 * j[None, :] + 64) & (16 * c[:, None] + 32 > 64 * j[None, :])).astype(np.float32)
    ov[511, :] = 0.0
    ov = ov.reshape(4, 128, 128).transpose(1, 0, 2).copy()
    sel = np.zeros((67, 3, 64), np.float32)
    for br in range(3):
        sel[64 + br, br, :] = 1.0
    d = dict(ident=ident, antiI=antiI, eall=eall.astype(NPBF), ov_base=ov, oh=_onehot_rows(), selm=sel)
    _CONST_CACHE["c"] = d
    return d


def _core_consts(i):
    pad = 1536 - 512 * i
    jmin = pad // 64
    kinv = (np.arange(1536) < pad).astype(np.float32)[None].astype(NPBF)
    cinv = (16 * np.arange(128) < pad).astype(np.float32)[None].astype(NPBF)
    vcm = np.zeros((4, 128, 4, 128), np.float32)
    addm = np.zeros((4, 128, 4, 128), np.float32)
    jj = np.arange(128)[None, :]
    for s in range(4):
        T = 3 + 4 * s
        for u in range(4):
            tL = 512 * T + 128 * u + np.arange(128)
            blk = (tL // 64)[:, None]
            valid = jj >= jmin
            vc = valid & (jj <= blk)
            forced = ((jj == jmin) | (jj == blk) | (jj == blk - 1)) & vc
            vcm[s, :, u, :] = vc
            addm[s, :, u, :] = (vc.astype(np.float32) - 1.0) + 2000.0 * forced
    cval = ((16 * np.arange(512) >= pad) & (np.arange(512) <= 510)).astype(np.float32)
    cval_pc = cval.reshape(4, 128).T
    ov = _shared_consts()["ov_base"].copy()
    ov[:, :, 127] = 1.0
    ov = ov * cval_pc[:, :, None]
    kvw = np.zeros((128, 32), np.float32)
    for s in range(4):
        for wi in range(8):
            kt = 8 + 16 * s + wi
            kvw[:, 8 * s + wi] = (128 * kt + np.arange(128)) >= pad
    return dict(vcm=vcm, addm=addm, ov=ov.astype(NPBF), kvw=kvw, cvc=np.ascontiguousarray(cval_pc))


P1 = 0
O_KAUG = 0
O_VS = 32768
O_KWIN = 50176
O_VW = 66560
O_KCT = 75264
O_VC = 77312
P2 = 78464
P3 = P2 + 57344
P3_SIZE = 72704
PC = P3 + P3_SIZE
ARENA_BYTES = PC + 3584


def build_program(stop_after="D"):
    nc = bass.Bass("TRN2", target_bir_lowering=False)

    def din(name, shape, dt=F32):
        return nc.dram_tensor(name, list(shape), dt, kind="ExternalInput").ap()

    x_d = din("x", [SEQ, DM])
    win_d = din("w_in", [DM, NCOLS])
    wgate_d = din("w_gate", [DM, 24])
    nw_d = din("nw", [128, 8])
    w1k_d = din("w1k", [64, 32, 128])
    w1v_d = din("w1v", [64, 32, 128])
    w2k_d = din("w2k", [128, 64])
    w2v_d = din("w2v", [128, 64])
    pek_d = din("pekT", [64, 32])
    pev_d = din("pevT", [64, 32])
    cw_d = din("cw", [128, 12])
    wout_d = din("w_out", [DM, DM])
    relb_d = din("rel_bias", [32, 8])
    fnw_d = din("fnw", [1, DM])
    ident_d = din("ident", [128, 128], BF16)
    anti_d = din("antiI", [128, 128], BF16)
    eall_d = din("eall", [64, SEQ], BF16)
    ov_d = din("ov", [128, 4, 128], BF16)
    oh_d = din("oh", [33, LEN_F])
    selm_d = din("selm", [67, 3, 64])
    kvw_d = din("kvw", [128, 32])
    cvc_d = din("cvc", [128, 4])
    vcm_d = din("vcm", [4, 128, 4, 128])
    addm_d = din("addm", [4, 128, 4, 128])
    out_d = nc.dram_tensor("out", [2048, DM], F32, kind="ExternalOutput").ap()
    fd_d = nc.dram_tensor("fd_scr", [8, LEN_F], BF16, kind="Internal").ap()
    bar_d = nc.dram_tensor("bar_scr", [1, 64], F32, kind="Internal").ap()
    dbg = {}

    def dbg_out(name, shape, dt):
        t = nc.dram_tensor("dbg_" + name, list(shape), dt, kind="ExternalOutput").ap()
        dbg[name] = t
        return t

    with ExitStack() as es:
        arena = es.enter_context(nc.sbuf_tensor("arena", [128, ARENA_BYTES // 2], BF16))
        ps_sc = [es.enter_context(nc.psum_tensor("ps_sc%d" % k, [128, 1024], F32)) for k in range(2)]
        ps_m = [es.enter_context(nc.psum_tensor("ps_m%d" % k, [128, 512], F32)) for k in range(4)]
        sems = {e: es.enter_context(nc.semaphore("s_" + e)) for e in Sched.ENGS}
        S = Sched(nc)

        def newchan(name):
            return S.new_chan(es.enter_context(nc.semaphore(name)))

        bar_chan = newchan("c_bar")

        def carve(off, shape, dt, p0=0, p1=128):
            n = 1
            for d_ in shape:
                n *= d_
            nb = n * (4 if dt == F32 else 2)
            assert off % 4 == 0 and off + nb <= ARENA_BYTES, (off, nb)
            v = arena[p0:p1, off // 2: (off + nb) // 2]
            if dt == F32:
                v = v.bitcast(F32)
            if len(shape) == 2:
                v = v.rearrange("p (a b) -> p a b", a=shape[0])
            elif len(shape) == 3:
                v = v.rearrange("p (a b c) -> p a b c", a=shape[0], b=shape[1])
            return v

        def barrier():
            S.barrier(lambda e: e.dma_start(out=bar_d[0:1, 0:16], in_=fnw_d[0:1, 0:16]), bar_chan)

        def mm(out, lhsT, rhs, start, stop, reads=(), writes=()):
            return S.op("pe", lambda e: e.matmul(out, lhsT=lhsT, rhs=rhs, start=start, stop=stop), reads, writes)

        def tr(out, in_, reads=(), writes=()):
            return S.op("pe", lambda e: e.transpose(out=out, in_=in_, identity=IDENT), reads, writes)

        def act(out, in_, func, reads=(), writes=(), bias=None, scale=None, accum_out=None):
            kw = {}
            if bias is not None:
                kw["bias"] = bias
            if scale is not None:
                kw["scale"] = scale
            if accum_out is not None:
                kw["accum_out"] = accum_out
            return S.op("act", lambda e: e.activation(out=out, in_=in_, func=func, **kw), reads, writes)

        def vcopy(eng, out, in_, reads=(), writes=()):
            return S.op(eng, lambda e: e.tensor_copy(out=out, in_=in_), reads, writes)

        def tt(eng, out, in0, in1, op, reads=(), writes=()):
            return S.op(eng, lambda e: e.tensor_tensor(out=out, in0=in0, in1=in1, op=op), reads, writes)

        def ts(eng, out, in0, s1, op0, s2=None, op1=None, reads=(), writes=()):
            if op1 is None:
                return S.op(eng, lambda e: e.tensor_scalar(out=out, in0=in0, scalar1=s1, scalar2=None, op0=op0), reads, writes)
            return S.op(eng, lambda e: e.tensor_scalar(out=out, in0=in0, scalar1=s1, scalar2=s2, op0=op0, op1=op1), reads, writes)

        def stt(eng, out, in0, scalar, in1, op0, op1, reads=(), writes=()):
            return S.op(eng, lambda e: e.scalar_tensor_tensor(out=out, in0=in0, scalar=scalar, in1=in1, op0=op0, op1=op1), reads, writes)

        def memset(eng, ap, val, reads=(), writes=()):
            return S.op(eng, lambda e: e.memset(ap, val), reads, writes)

        def dma(eng, out, in_, chan, reads=(), writes=()):
            return S.op(eng, lambda e: e.dma_start(out=out, in_=in_), reads, writes, chan=chan)

        KAUG = carve(O_KAUG, [2, 8192], BF16)
        VS = carve(O_VS, [64, 2, 68], BF16)
        KWIN = carve(O_KWIN, [2, 4096], BF16)
        VW = carve(O_VW, [32, 2, 68], BF16)
        KCT = carve(O_KCT, [2, 512], BF16)
        VC = carve(O_VC, [4, 2, 68], BF16)
        QTP = carve(P2 + 0, [4, 2048], BF16)
        SZP = carve(P2 + 16384, [4, 2048], BF16)
        MC = carve(P2 + 32768, [4, 2048], BF16)
        G24 = carve(P2 + 49152, [2048], F32, 0, 24)
        IDENT = carve(PC + 0, [128], BF16)
        ANTI = carve(PC + 256, [128], BF16)
        NW = carve(PC + 512, [8], F32)
        B31 = carve(PC + 544, [8], F32)
        OV = carve(PC + 576, [4, 128], BF16)
        SELM = carve(PC + 1600, [3, 64], F32, 0, 67)
        CW = carve(PC + 2368, [12], F32)
        HIDB = carve(PC + 2416, [2], F32)
        RELB = carve(PC + 2432, [8], F32, 0, 33)
        RB31 = carve(PC + 2464, [8], F32, 0, 32)
        ONES = carve(PC + 2496, [2], BF16)
        SMALL = carve(PC + 2560, [64], F32)
        c_const = newchan("c_const")
        c_dbg = newchan("c_dbg")
        Bps = [Buf("psm%d" % k) for k in range(4)]
        Bsc = [Buf("pssc%d" % k) for k in range(2)]
        Bsc4 = [Buf("pssch%d" % k) for k in range(4)]
        Bc = Buf("consts")
        Bsmall = Buf("small")

        def finish():
            S.emit(sems)

        def dump(name, ap, dt):
            t = dbg_out(name, list(ap.shape), dt)
            idx = tuple(slice(None) for _ in ap.shape)
            dma("sp", t[idx], ap, c_dbg)

        KVW = carve(PC + 2816, [32], F32)
        CVC = carve(PC + 2944, [4], F32)
        b31_src = bass.AP(relb_d.tensor, 31 * 8, [[0, 128], [1, 8]])
        rb31_src = bass.AP(relb_d.tensor, 31 * 8, [[0, 32], [1, 8]])
        const_loads = [(IDENT, ident_d[:, :]), (ANTI, anti_d[:, :]), (NW, nw_d[:, :]), (OV, ov_d[:, :, :]),
                       (SELM, selm_d[:, :, :]), (CW, cw_d[:, :]), (RELB[0:32, :], relb_d[:, :]),
                       (B31, b31_src), (RB31, rb31_src), (KVW, kvw_d[:, :]), (CVC, cvc_d[:, :]),
                       (KAUG[64:128, 0, :], eall_d[:, :]), (KAUG[64:128, 1, :], eall_d[:, :])]
        for i_, (dst, src_) in enumerate(const_loads):
            dma("sp", dst, src_, c_const, writes=[Bc] if i_ == len(const_loads) - 1 else [])
        Bvi = Buf("vind")
        memset("pool", ONES, 1.0, writes=[Bvi])
        for (V_, one) in ((VS, 65), (VW, 66), (VC, 64)):
            memset("pool", V_[:, :, :, 64:68], 0.0, writes=[Bvi])
            memset("pool", V_[:, :, :, one:one + 1], 1.0, writes=[Bvi])
        memset("pool", VS[:, :, :, 64:65], 0.5, writes=[Bvi])
        memset("pool", VW[:, :, :, 64:65], 0.5, writes=[Bvi])
        for g in range(2):
            tt("dve", VW[:, :, g, 64], VW[:, :, g, 64], KVW, ALU.mult, reads=[Bc, Bvi], writes=[Bvi])
            tt("dve", VW[:, :, g, 66], VW[:, :, g, 66], KVW, ALU.mult, reads=[Bc, Bvi], writes=[Bvi])
            tt("dve", VC[:, :, g, 64], VC[:, :, g, 64], CVC, ALU.mult, reads=[Bc, Bvi], writes=[Bvi])
        KCMPT = carve(P2 + 0, [8192], BF16)
        VCMPT = carve(P2 + 16384, [8192], BF16)
        XNTOWN = carve(P3 + 0, [8, 4, 514], BF16)
        XNTW = [carve(P3 + 32896 + 8192 * k, [8, 512], BF16) for k in range(2)]
        WKV = carve(P3 + 49280, [8, 1024], BF16)
        WSTG = [carve(P3 + 65664 + 2048 * k, [8, 64], F32) for k in range(2)]
        win_v = win_d.rearrange("(c p) n -> p c n", p=128)
        c_w = [newchan("c_w%d" % k) for k in range(2)]
        Bwstg = [Buf("wstg%d" % k) for k in range(2)]
        Bwkv = Buf("wkv")
        wcnt = [0]

        def load_w_block(dst_fn, src_v, col0, ncol, stg, Bdst):
            k_ = wcnt[0] % 2
            wcnt[0] += 1
            dma("sp", stg[k_][:, :, 0:ncol], src_v[:, :, col0:col0 + ncol], c_w[k_], writes=[Bwstg[k_]])
            for c in range(8):
                ts("dve", dst_fn(c), stg[k_][:, c, 0:ncol], NW[:, c:c + 1], ALU.mult,
                   reads=[Bwstg[k_], Bc], writes=[Bdst] if c in (0, 7) else [])

        XT = [carve(P2 + 32768 + 4096 * k, [1024], F32) for k in range(4)]
        Bxt = [Buf("xt%d" % k) for k in range(4)]
        c_x = [newchan("c_x%d" % k) for k in range(4)]
        wmap = [(0, 512, 128), (128, 640, 128), (6 * 128, 896, 128), (7 * 128, 1152, 128)]
        wmap.append((2 * 128, 768, 128))
        wmap.append((3 * 128, 1024, 128))
        wmap.append((4 * 128, 1024, 128))
        wmap.append((5 * 128, 1024, 128))
        for c in range(8):
            k_ = c % 4
            dma("sp", XT[k_][:, 0:768], win_d[128 * c:128 * c + 128, 512:1280], c_x[k_], writes=[Bxt[k_]])
            for wi_, (dcol, scol, n) in enumerate(wmap):
                ts("dve", WKV[:, c, dcol:dcol + n], XT[k_][:, scol - 512:scol - 512 + n], NW[:, c:c + 1], ALU.mult,
                   reads=[Bxt[k_], Bc], writes=[Bwkv] if (c == 7 and wi_ == len(wmap) - 1) else [])

        OHS = carve(P3 + 0, [LEN_F], F32, 0, 33)
        FROW = carve(P3 + 20992, [LEN_F], BF16, 0, 8)
        Boh = Buf("oh")
        Brelb = Buf("relb")
        c_oh = newchan("c_oh")
        dma("sp", OHS, oh_d[:, :], c_oh, writes=[Boh])
        memset("dve", RELB[32:33, :], NEGBIG, reads=[Bc], writes=[Brelb])
        tt("dve", RELB[0:32, :], RELB[0:32, :], RB31, ALU.subtract, reads=[Bc, Brelb], writes=[Brelb])
        Bfrow = Buf("frow")
        c0 = 0
        k = 0
        while c0 < LEN_F:
            n = min(512, LEN_F - c0)
            bank = k % 2
            mm(ps_m[bank][0:8, 0:n], RELB[0:33, :], OHS[:, c0:c0 + n], True, True, reads=[Brelb, Boh], writes=[Bps[bank]])
            vcopy("dve", FROW[:, c0:c0 + n], ps_m[bank][0:8, 0:n], reads=[Bps[bank]], writes=[Bfrow])
            c0 += n
            k += 1
        Bfd = Buf("fd")
        dma("sp", fd_d[:, :], FROW, c_const, reads=[Bfrow], writes=[Bfd])
        if DEBUG == "0":
            dump("frow", FROW, BF16)
            dump("kaug", KAUG[:, :, 0:2048], BF16)
            finish()
            return nc, dbg

        if DEBUG == "w":
            dump("wkv", WKV, BF16)
            finish()
            return nc, dbg
        XN = [carve(P2 + 49152 + 2048 * k, [1024], BF16) for k in range(4)]
        JUNK = carve(P3 + 69760, [1024], BF16)
        Bxn = [Buf("xn%d" % k) for k in range(4)]
        Bss = [Buf("ss%d" % k) for k in range(8)]
        Bxntw = [Buf("xntw%d" % k) for k in range(2)]
        Bxown = [Buf("xown%d" % k) for k in range(4)]

        Bkw = [Buf("kw%d" % k) for k in range(2)]
        c_kw = [newchan("c_kw%d" % k) for k in range(2)]
        SLST = [carve(P3 + 67712 + 1024 * k, [512], BF16) for k in range(2)]
        Bsl = [Buf("sl%d" % k) for k in range(2)]
        c_sl = [newchan("c_sl%d" % k) for k in range(2)]

        def tile_dst(n):
            T = n // 4
            if T % 4 == 3:
                return XNTOWN[:, :, T // 4, 2:514], Bxown[T // 4]
            return XNTW[T % 2], Bxntw[T % 2]

        def stA(n):
            kx = n % 4
            ss = SMALL[:, (n % 8) * 2:(n % 8) * 2 + 1]
            rs = SMALL[:, (n % 8) * 2 + 1:(n % 8) * 2 + 2]
            dma("sp", XT[kx], x_d[128 * n:128 * n + 128, :], c_x[kx], writes=[Bxt[kx]])
            act(JUNK, XT[kx], AF.Square, reads=[Bxt[kx]], writes=[Bss[n % 8]], accum_out=ss)
            ts("dve", rs, ss, 1.0 / DM, ALU.mult, EPS, ALU.add, reads=[Bss[n % 8]], writes=[Bss[n % 8]])

        def stB(n):
            rs = SMALL[:, (n % 8) * 2 + 1:(n % 8) * 2 + 2]
            act(rs, rs, AF.Sqrt, reads=[Bss[n % 8]], writes=[Bss[n % 8]])
            S.op("dve", lambda e, rs=rs: e.reciprocal(out=rs, in_=rs), reads=[Bss[n % 8]], writes=[Bss[n % 8]])

        def stC(n):
            kx = n % 4
            sub = n % 4
            rs = SMALL[:, (n % 8) * 2 + 1:(n % 8) * 2 + 2]
            dstT, dstB = tile_dst(n)
            act(XN[kx], XT[kx], AF.Copy, reads=[Bxt[kx], Bss[n % 8]], writes=[Bxn[kx]], scale=rs)
            pb = 2 + (n % 2)
            pst = ps_m[pb][:, 0:512].bitcast(BF16)
            for c in range(8):
                tr(pst[:, c * 128:(c + 1) * 128], XN[kx][:, c * 128:(c + 1) * 128], reads=[Bxn[kx], Bc], writes=[Bps[pb]])
            vcopy("dve", dstT[:, :, sub * 128:(sub + 1) * 128], pst.rearrange("p (c n) -> p c n", c=8),
                  reads=[Bps[pb]] + ([Bfd] if (n // 4) % 4 == 3 else []), writes=[dstB])

        def projA(T):
            XT_T, BT = tile_dst(4 * T)
            if T % 4 == 2:
                vcopy("pool", XNTOWN[:, :, T // 4, 0:2], XT_T[:, :, 510:512], reads=[BT, Bfd], writes=[Bxown[T // 4]])
            cols = slice(512 * T, 512 * T + 512)
            fm = [(0, KCMPT[:, cols], 128), (1, VCMPT[:, cols], 128), (2, None, 128)]
            if T % 4 >= 2:
                w0_ = 512 * (2 * (T // 4) + (T % 4 - 2))
                fm += [(3, "kwin", 128)]
            for ii, (blk, dst, m) in enumerate(fm):
                pb = ii % 4
                pso = ps_sc[pb // 2][0:m, (pb % 2) * 512:(pb % 2) * 512 + 512]
                for c in range(8):
                    mm(pso, WKV[:, c, blk * 128:blk * 128 + m], XT_T[:, c, :], c == 0, c == 7,
                       reads=[BT, Bwkv], writes=[Bsc4[pb]])
                if isinstance(dst, str):
                    kw_ = T % 2
                    wsl = slice(w0_, w0_ + 512)
                    vcopy("dve", KWIN[0:64, 0, wsl], pso[0:64, :], reads=[Bsc4[pb]], writes=[Bkw[kw_]])
                    vcopy("dve", KWIN[64:128, 1, wsl], pso[64:128, :], reads=[Bsc4[pb]], writes=[Bkw[kw_]])
                    dma("pool", KWIN[64:128, 0, wsl], KWIN[0:64, 0, wsl], c_kw[kw_], reads=[Bkw[kw_]])
                    dma("pool", KWIN[0:64, 1, wsl], KWIN[64:128, 1, wsl], c_kw[kw_], reads=[Bkw[kw_]])
                elif dst is not None:
                    vcopy("dve", dst, pso, reads=[Bsc4[pb]])
                else:
                    ks = T % 2
                    vcopy("dve", KAUG[0:64, 0, cols], pso[0:64, :], reads=[Bsc4[pb]])
                    vcopy("dve", SLST[ks][64:128, :], pso[64:128, :], reads=[Bsc4[pb]], writes=[Bsl[ks]])
                    dma("pool", KAUG[0:64, 1, cols], SLST[ks][64:128, :], c_sl[ks], reads=[Bsl[ks]])
            for sub in range(4):
                kt = 4 * T + sub
                pb = sub % 2
                pso = ps_m[pb][:, 0:256]
                for c in range(8):
                    mm(pso, XT_T[:, c, sub * 128:(sub + 1) * 128], WKV[:, c, 768:1024], c == 0, c == 7,
                       reads=[BT, Bwkv], writes=[Bps[pb]])
                vcopy("dve", VS[:, kt, :, 0:64], pso[:, 0:128].rearrange("p (g d) -> p g d", g=2), reads=[Bps[pb]])
                if T % 4 >= 2:
                    slot = 8 * (T // 4) + 4 * (T % 4 - 2) + sub
                    vcopy("dve", VW[:, slot, :, 0:64], pso[:, 128:256].rearrange("p (g d) -> p g d", g=2),
                          reads=[Bps[pb]])

        for n in range(4):
            stA(n)
            stB(n)
        for T in range(16):
            for n in range(4 * T, 4 * T + 4):
                stC(n)
            if T + 1 < 16:
                for n in range(4 * T + 4, 4 * T + 8):
                    stA(n)
                    stB(n)
            projA(T)
        W1SX = carve(P2 + 32768, [32, 128], F32)
        W2S = carve(P3 + 66176, [2, 64], F32)
        PETS = carve(P3 + 66816, [2, 32], F32)
        c_a2 = newchan("c_a2")
        c_a2b = newchan("c_a2b")
        Bw1s = Buf("w1s")
        Bw1sx = Buf("w1sx")
        for hf in range(2):
            dma("sp", W1SX[64 * hf:64 * hf + 64, :, :], w1k_d[:, :, :], c_a2b, writes=Bxt + [Bw1sx])
        for kv, (w2d, ped) in enumerate(((w2k_d, pek_d), (w2v_d, pev_d))):
            for hf in range(2):
                dma("sp", PETS[64 * hf:64 * hf + 64, kv, :], ped[:, :], c_a2b, writes=[Bw1sx])
            dma("sp", W2S[:, kv, :], w2d[:, :], c_a2b, writes=[Bw1sx])
        barrier()
        if DEBUG == "A1":
            dump("kaug", KAUG[:, :, 0:2048], BF16)
            dump("vs", VS, BF16)
            dump("kwin", KWIN, BF16)
            dump("vw", VW, BF16)
            finish()
            return nc, dbg

        W1 = [carve(P3 + 32896 + 8192 * k, [32, 128], BF16) for k in range(2)]
        W1S = carve(P3 + 49280, [32, 128], F32)
        W2 = carve(P3 + 65664, [2, 128], BF16)
        PET = carve(P3 + 66688, [2, 32], BF16)
        HSIL = [carve(P3 + 67072 + 1024 * k, [512], BF16) for k in range(2)]
        Bw1 = Buf("w1")
        Bhs = [Buf("hs0"), Buf("hs1")]
        for hf in range(2):
            dma("sp", W1S[64 * hf:64 * hf + 64, :, :], w1v_d[:, :, :], c_a2, writes=[Bw1s])
        for kv in range(2):
            stg, Bstg = (W1SX, Bw1sx) if kv == 0 else (W1S, Bw1s)
            vcopy("dve", W1[kv], stg, reads=[Bstg], writes=[Bw1])
            vcopy("pool", PET[:, kv, :], PETS[:, kv, :], reads=[Bw1sx], writes=[Bw1])
            vcopy("pool", W2[:, kv, 0:64], W2S[:, kv, :], reads=[Bw1sx], writes=[Bw1])
            vcopy("pool", W2[:, kv, 64:128], W2S[:, kv, :], reads=[Bw1sx], writes=[Bw1])
            for j in range(32):
                mm(ps_m[2][:, 0:1], W1[kv][0:64, j, :], PET[0:64, kv, j:j + 1], j == 0, j == 31, reads=[Bw1], writes=[Bps[2]])
            vcopy("dve", HIDB[:, kv:kv + 1], ps_m[2][:, 0:1], reads=[Bps[2]], writes=[Bw1])
        Bvc = Buf("vc")
        memset("pool", HSIL[0][:, 511:512], 0.0, writes=[Bhs[0]])
        memset("pool", HSIL[1][:, 511:512], 0.0, writes=[Bhs[1]])
        for g in range(2):
            memset("pool", KCT[:, g, 511:512], 0.0)
            for kv in range(2):
                src = (KCMPT if kv == 0 else VCMPT)
                pr = slice(64 * g, 64 * g + 64)
                hid = ps_sc[kv][:, 0:511]
                for j in range(32):
                    mm(hid, W1[kv][pr, j, :], src[pr, j:j + 8161:16], j == 0, j == 31, reads=[Bw1], writes=[Bsc[kv]])
                act(HSIL[kv][:, 0:511], hid, AF.Silu, reads=[Bsc[kv], Bw1], writes=[Bhs[kv]], bias=HIDB[:, kv:kv + 1])
                if kv == 0:
                    mm(ps_m[0][:, 0:511], W2[:, 0, :], HSIL[0][:, 0:511], True, True, reads=[Bhs[0], Bw1], writes=[Bps[0]])
                    vcopy("dve", KCT[:, g, 0:511], ps_m[0][:, 0:511], reads=[Bps[0]])
                else:
                    for ct in range(4):
                        mm(ps_m[1][:, ct * 64:(ct + 1) * 64], HSIL[1][:, ct * 128:(ct + 1) * 128], W2[:, 1, 0:64], True, True,
                           reads=[Bhs[1], Bw1], writes=[Bps[1]])
                    vcopy("dve", VC[:, :, g, 0:64], ps_m[1][:, 0:256].rearrange("p (c d) -> p c d", c=4),
                          reads=[Bps[1]], writes=[Bvc])
                    for ct in range(4):
                        ts("dve", VC[:, ct, g, 0:64], VC[:, ct, g, 0:64], CVC[:, ct:ct + 1], ALU.mult,
                           reads=[Bvc, Bc], writes=[Bvc])
        barrier()
        if DEBUG == "A":
            dump("kaug", KAUG[:, :, 0:2048], BF16)
            dump("vs", VS, BF16)
            dump("kwin", KWIN, BF16)
            dump("vw", VW, BF16)
            dump("kct", KCT, BF16)
            dump("vc", VC, BF16)
            finish()
            return nc, dbg

        WSTG2 = [carve(P3 + 32896 + 4096 * k, [8, 128], F32) for k in range(2)]
        WG = [carve(P3 + 41088 + 2048 * k, [8, 128], BF16) for k in range(6)]
        HSB = carve(P3 + 53376, [514], F32)
        UU = carve(P3 + 55440, [514], F32)
        YY = carve(P3 + 57504, [512], F32)
        SZC = carve(P3 + 59552, [512], F32)
        BS = carve(P3 + 61600, [512], F32)
        Bwg = [Buf("wg%d" % k) for k in range(6)]
        Bout = Buf("projout")
        Btmp = {n: Buf(n) for n in ("hsb", "uu", "yy", "szc", "bs")}
        wgate_v = wgate_d.rearrange("(c p) n -> p c n", p=128)

        def own_cols(s):
            return XNTOWN[:, :, s, 2:514]

        def proj4(wg, nm, dst_fn):
            for s in range(4):
                pb = s % 4
                pso = ps_sc[pb // 2][0:nm, (pb % 2) * 512:(pb % 2) * 512 + 512]
                for c in range(8):
                    mm(pso, WG[wg][:, c, 0:nm], XNTOWN[:, c, s, 2:514], c == 0, c == 7, reads=[Bwg[wg]], writes=[Bsc4[pb]])
                dst_fn(s, pso, Bsc4[pb])

        wgi = 0
        for qb in range(4):
            w_ = wgi % 6
            wgi += 1
            load_w_block(lambda c, w_=w_: WG[w_][:, c, :], win_v, 128 * qb, 128, WSTG2, Bwg[w_])
            proj4(w_, 128, lambda s, pso, B_, qb=qb: act(QTP[:, qb, s * 512:(s + 1) * 512], pso, AF.Copy,
                                                       reads=[B_], scale=0.125))
        for qb in range(4):
            w_ = wgi % 6
            wgi += 1
            load_w_block(lambda c, w_=w_: WG[w_][:, c, :], win_v, 1304 + 128 * qb, 128, WSTG2, Bwg[w_])
            proj4(w_, 128, lambda s, pso, B_, qb=qb: act(SZP[:, qb, s * 512:(s + 1) * 512], pso, AF.Silu,
                                                       reads=[B_]))
        w_ = wgi % 6
        wgi += 1
        load_w_block(lambda c, w_=w_: WG[w_][:, c, 0:24], wgate_v, 0, 24, WSTG2, Bwg[w_])
        proj4(w_, 24, lambda s, pso, B_: act(G24[:, s * 512:(s + 1) * 512], pso, AF.Sigmoid, reads=[B_]))
        for ch in range(4):
            ws = []
            for (nm_, c0_) in (("h", 1816), ("b", 2328), ("c", 2840), ("z", 3352)):
                w_ = wgi % 6
                wgi += 1
                load_w_block(lambda c, w_=w_: WG[w_][:, c, :], win_v, c0_ + 128 * ch, 128, WSTG2, Bwg[w_])
                ws.append(w_)
            wh, wb, wc_, wz = ws
            for s in range(4):
                qcols = slice(s * 512, (s + 1) * 512)
                ps_h = ps_sc[0][:, 0:512]
                ps_c = ps_sc[0][:, 512:1024]
                ps_z = ps_sc[1][:, 0:512]
                ps_b = ps_sc[1][:, 512:1024]
                ps_h2 = ps_m[0][:, 0:2]
                ps_c2 = ps_m[1][:, 0:2]
                for c in range(8):
                    mm(ps_h, WG[wh][:, c, :], XNTOWN[:, c, s, 2:514], c == 0, c == 7, reads=[Bwg[wh]], writes=[Bsc4[0]])
                for c in range(8):
                    mm(ps_h2, WG[wh][:, c, :], XNTOWN[:, c, s, 0:2], c == 0, c == 7, reads=[Bwg[wh]], writes=[Bps[0]])
                for c in range(8):
                    mm(ps_c, WG[wc_][:, c, :], XNTOWN[:, c, s, 2:514], c == 0, c == 7, reads=[Bwg[wc_]], writes=[Bsc4[1]])
                for c in range(8):
                    mm(ps_c2, WG[wc_][:, c, :], XNTOWN[:, c, s, 0:2], c == 0, c == 7, reads=[Bwg[wc_]], writes=[Bps[1]])
                for c in range(8):
                    mm(ps_z, WG[wz][:, c, :], XNTOWN[:, c, s, 2:514], c == 0, c == 7, reads=[Bwg[wz]], writes=[Bsc4[2]])
                for c in range(8):
                    mm(ps_b, WG[wb][:, c, :], XNTOWN[:, c, s, 2:514], c == 0, c == 7, reads=[Bwg[wb]], writes=[Bsc4[3]])
                act(HSB[:, 2:514], ps_h, AF.Copy, reads=[Bsc4[0]], writes=[Btmp["hsb"]])
                vcopy("dve", HSB[:, 0:2], ps_h2, reads=[Bps[0]], writes=[Btmp["hsb"]])
                tt("dve", UU[:, 2:514], ps_c, HSB[:, 2:514], ALU.mult, reads=[Bsc4[1], Btmp["hsb"]], writes=[Btmp["uu"]])
                tt("dve", UU[:, 0:2], ps_c2, HSB[:, 0:2], ALU.mult, reads=[Bps[1], Btmp["hsb"]], writes=[Btmp["uu"]])
                ts("dve", YY, UU[:, 2:514], CW[:, 3 * ch + 2:3 * ch + 3], ALU.mult, reads=[Btmp["uu"], Bc], writes=[Btmp["yy"]])
                stt("dve", YY, UU[:, 1:513], CW[:, 3 * ch + 1:3 * ch + 2], YY, ALU.mult, ALU.add,
                    reads=[Btmp["uu"], Btmp["yy"], Bc], writes=[Btmp["yy"]])
                stt("dve", YY, UU[:, 0:512], CW[:, 3 * ch:3 * ch + 1], YY, ALU.mult, ALU.add,
                    reads=[Btmp["uu"], Btmp["yy"], Bc], writes=[Btmp["yy"]])
                act(SZC, ps_z, AF.Silu, reads=[Bsc4[2]], writes=[Btmp["szc"]])
                tt("dve", BS, ps_b, YY, ALU.mult, reads=[Bsc4[3], Btmp["yy"]], writes=[Btmp["bs"]])
                tt("pool", MC[:, ch, qcols], BS, SZC, ALU.mult, reads=[Btmp["bs"], Btmp["szc"]])
        barrier()
        if DEBUG == "B":
            dump("qtp", QTP, BF16)
            dump("szp", SZP, BF16)
            dump("mc", MC, BF16)
            dump("g24", G24, F32)
            finish()
            return nc, dbg

        def c3(off, shape, dt, p0=0, p1=128):
            return carve(P3 + off, shape, dt, p0, p1)
        AC = [c3(4096 * k, [4, 512], BF16, 0, 64) for k in range(2)]
        MT0 = [c3(1024 * k, [512], BF16, 64, 128) for k in range(2)]
        MT1 = [c3(2048 + 1024 * k, [512], BF16, 64, 128) for k in range(2)]
        ZSW1 = c3(4096, [512], F32, 64, 65)
        CF1H = c3(6144, [512], BF16, 64, 65)
        CF1L = c3(7168, [512], BF16, 64, 65)
        X0 = [c3(8192 + 1024 * k, [512], BF16) for k in range(2)]
        X1 = [c3(10240 + 1024 * k, [512], BF16) for k in range(2)]
        HS = [c3(12288 + 2048 * k, [1024], BF16) for k in range(2)]
        HW = [c3(16384 + 2816 * k, [1408], BF16) for k in range(2)]
        HC = c3(22016, [8, 512], BF16)
        IMP = c3(30208, [4, 128], F32)
        VCM = c3(32256, [4, 128], F32)
        ADDM = c3(34304, [4, 128], F32)
        PB = [c3(65728 + 2048 * k, [1024], BF16) for k in range(3)]
        OSB_S = c3(40448, [512], F32, 0, 67)
        OSB_W = c3(42496, [512], F32, 0, 67)
        OCSB = c3(44544, [512], F32, 0, 67)
        COEF = c3(46592, [512], F32, 64, 67)
        ZSW = c3(48640, [512], F32, 64, 67)
        SZO = [c3(50688 + 2048 * k, [512], BF16, 0, 64) for k in range(2)]
        MXO = [c3(50688 + 2048 * k + 1024, [512], BF16, 0, 64) for k in range(2)]
        GST = [c3(50688 + 2048 * k, [512], F32, 64, 67) for k in range(2)]
        GST1 = c3(54784, [512], F32, 64, 65)
        SCORE = c3(56832, [4, 128], F32)
        WORK = c3(58880, [4, 128], F32)
        TOP = c3(60928, [4, 16], F32)
        MNB_N = c3(61184, [4, 128], BF16)
        MNB_S = c3(62208, [4, 128], BF16)
        RZT = c3(63232, [4], F32)
        COEFH = c3(63296, [512], BF16, 64, 67)
        COEFL = c3(64320, [512], BF16, 64, 67)
        SELMB = c3(65344, [3, 64], BF16, 0, 67)

        SC = [ps_sc[0], ps_sc[1]]
        OCP = ps_m[2][0:67, :]
        IMPP = ps_m[3][:, :]
        Bsc2 = [Buf("sc0"), Buf("sc1")]
        Bocp = Buf("ocp")
        Bimpp = Buf("impp")

        Bcc = Buf("cconst")
        c_c = newchan("c_cconst")
        vcopy("pool", SELMB, SELM, reads=[Bc], writes=[Bcc])
        Bhc = Buf("hc")
        c_hc = newchan("c_hc")
        for h in range(8):
            dma("sp", HC[:, h, :], bass.AP(fd_d.tensor, h * LEN_F + OFF_C, [[16, 128], [1, 512]]), c_hc, writes=[Bhc])

        Bp = [Buf("p%d" % k) for k in range(3)]
        Bx = [Buf("x%d" % k) for k in range(2)]
        Bhsw = [Buf("hsw%d" % k) for k in range(2)]
        Bimp = Buf("imp")
        Bmask = Buf("mask")
        Bmt = [Buf("mt0"), Buf("mt1")]
        Bac = [[Buf("ac%d_%d" % (p_, k)) for k in range(4)] for p_ in range(2)]
        Bgst = [Buf("gst%d" % k) for k in range(2)]
        Bgst1 = Buf("gst1")
        Bszo = [Buf("szo%d" % k) for k in range(2)]
        Bmxo = [Buf("mxo%d" % k) for k in range(2)]
        Bosb = {n: Buf(n) for n in ("s", "w", "c", "coef", "coefh", "coefl", "zsw", "zsw1", "cf1h", "cf1l",
                                    "score", "work", "top", "mnb", "mnbs", "rzt")}
        Bszp = [Buf("szp%d" % k) for k in range(8)]
        c_tab = [newchan("c_tab%d" % k) for k in range(2)]
        c_mask = newchan("c_mask")
        c_gst = [newchan("c_gst%d" % k) for k in range(2)]
        c_gst1 = newchan("c_gstone")
        c_x2 = [newchan("c_xq%d" % k) for k in range(2)]
        c_szo = [newchan("c_szo%d" % k) for k in range(2)]
        c_mxo = [newchan("c_mxo%d" % k) for k in range(2)]

        class Pipe:
            def __init__(self):
                self.q = []
                self.tick = 0

            def _run_due(self):
                due = [f for (t_, f) in self.q if t_ <= self.tick]
                self.q = [(t_, f) for (t_, f) in self.q if t_ > self.tick]
                for f in due:
                    f()

            def push(self, qk_fn, later=()):
                self.tick += 1
                qk_fn()
                self._run_due()
                for (d_, f) in later:
                    self.q.append((self.tick + d_, f))

            def flush(self):
                while self.q:
                    self.tick += 1
                    self._run_due()

        pipe = Pipe()
        gctr = [0]

        pctr = [0]

        def push_group(it):
            grp = it["group"]
            n = gctr[0]
            gctr[0] += 1
            sc = SC[n % 2]
            Bs_ = Bsc2[n % 2]
            pn = pctr[0]
            pctr[0] += 1
            P_ = PB[pn % 3]
            Bp_ = Bp[pn % 3]
            h = grp[0]["h"]
            width = 512 * len(grp)

            rng = [tl.get("cols", (0, 512)) for tl in grp]
            if len(grp) == 2:
                assert rng[0][1] == 512 and rng[1][0] == 0, rng
            e0 = rng[0][0]
            e1 = 512 * (len(grp) - 1) + rng[-1][1]

            def qk():
                for idx, tl in enumerate(grp):
                    lo, hi = rng[idx]
                    o = sc[:, idx * 512 + lo:idx * 512 + hi]
                    nm = len(tl["mms"])
                    for mi, (l_, r_) in enumerate(tl["mms"]):
                        mm(o, l_, r_[:, lo:hi], mi == 0, mi == nm - 1, reads=tl["reads"], writes=[Bs_])
                act(P_[:, e0:e1], sc[:, e0:e1], AF.Exp, reads=[Bs_, Bc], writes=[Bp_], bias=B31[:, h:h + 1])

            def pv():
                for idx, tl in enumerate(grp):
                    lo, hi = rng[idx]
                    tl["pv"](P_[:, idx * 512 + lo:idx * 512 + hi], Bp_, lo, hi)
            pipe.push(qk, [(1, pv)] + list(it.get("after", [])))

        def special(fn):
            n = gctr[0]
            gctr[0] += 1
            fn(SC[n % 2], Bsc2[n % 2])

        def to_groups(tiles, after):
            out = [dict(group=tiles[i:i + 2]) for i in range(0, len(tiles), 2)]
            out[-1]["after"] = after
            return out

        def run_seq(seq):
            for it in seq:
                if "call" in it:
                    it["call"]()
                else:
                    push_group(it)

        def merge(seq3, inserts):
            out = []
            ti = 0
            inserts = sorted(inserts, key=lambda x: x[0])
            ii = 0
            for it in seq3:
                while ii < len(inserts) and "call" not in it and inserts[ii][0] <= ti:
                    out.extend(inserts[ii][1])
                    ii += 1
                out.append(it)
                if "call" not in it:
                    ti += 1
            while ii < len(inserts):
                out.extend(inserts[ii][1])
                ii += 1
            return out

        insts = [(g, s) for g in range(2) for s in range(4)]
        hctr = [0]

        def stage1_head(idx, r, delays=(1, 3, 8), gst=None):
            GST1_, Bgst1_, c_gst1_ = gst if gst is not None else (GST1, Bgst1, c_gst1)
            g, s = insts[idx]
            p_ = idx % 2
            h = 4 * g + r
            q = h // 2
            pr = slice(64 * (h % 2), 64 * (h % 2) + 64)
            qcols = slice(s * 512, (s + 1) * 512)
            QT_h = QTP[pr, q, qcols]
            items = [dict(call=lambda: dma("sp", GST1_, G24[3 * h:3 * h + 1, qcols], c_gst1_, writes=[Bgst1_]))]
            tiles = []
            pend = []
            for ct in range(s + 1):
                mms = [(KCT[pr, g, ct * 128:(ct + 1) * 128], QT_h)]
                if ct == s:
                    mms.append((ANTI, HC[:, h, :]))

                def pvf(Ph, Bp_, lo, hi, ct=ct):
                    mm(OCP, VC[:, ct, g, 0:67], Ph, ct == 0, ct == s, reads=[Bp_], writes=[Bocp])
                    pend.append((ct, Ph, Bp_))
                    if ct == s:
                        for u in range(4):
                            for (c2, Ph2, Bp2) in pend:
                                mm(IMPP[:, u * 128:(u + 1) * 128], Ph2[:, u * 128:(u + 1) * 128], OV[:, c2, :],
                                   c2 == 0, c2 == s, reads=[Bp2], writes=[Bimpp])
                tiles.append(dict(mms=mms, reads=[Bhc], pv=pvf, h=h))

            def fin1():
                IMPRAW = WORK.rearrange("p u j -> p (u j)")
                vcopy("dve", OCSB, OCP, reads=[Bocp], writes=[Bosb["c"]])
                vcopy("dve", IMPRAW, IMPP, reads=[Bimpp], writes=[Bosb["work"]])
                ts("dve", RZT, WORK[:, :, 127], 1e-30, ALU.max, reads=[Bosb["work"]], writes=[Bosb["rzt"]])
                ts("dve", ZSW1, OCSB[64:65, :], 1e-30, ALU.max, reads=[Bosb["c"]], writes=[Bosb["zsw1"]])
                S.op("dve", lambda e: e.reciprocal(out=RZT, in_=RZT), reads=[Bosb["rzt"]], writes=[Bosb["rzt"]])
                for u in range(4):
                    if r == 0:
                        ts("dve", IMP[:, u, :], WORK[:, u, :], RZT[:, u:u + 1], ALU.mult,
                           reads=[Bosb["work"], Bosb["rzt"]], writes=[Bimp])
                    else:
                        stt("dve", IMP[:, u, :], WORK[:, u, :], RZT[:, u:u + 1], IMP[:, u, :],
                            ALU.mult, ALU.add, reads=[Bosb["work"], Bosb["rzt"], Bimp], writes=[Bimp])

            def fin1b():
                act(ZSW1, ZSW1, AF.Ln, reads=[Bosb["zsw1"]], writes=[Bosb["zsw1"]])
                act(ZSW1, ZSW1, AF.Exp, reads=[Bosb["zsw1"]], writes=[Bosb["zsw1"]], scale=-1.0)
                tt("dve", ZSW1, ZSW1, GST1_, ALU.mult, reads=[Bosb["zsw1"], Bgst1_], writes=[Bosb["zsw1"]])
                vcopy("dve", CF1H, ZSW1, reads=[Bosb["zsw1"]], writes=[Bosb["cf1h"]])
                tt("dve", CF1L, ZSW1, CF1H, ALU.subtract, reads=[Bosb["zsw1"], Bosb["cf1h"]], writes=[Bosb["cf1l"]])

            def fin2():
                bcp = ps_m[3][0:64, :]
                mm(bcp, SELMB[64:65, 0, :], CF1H, True, False, reads=[Bosb["cf1h"], Bcc], writes=[Bimpp])
                mm(bcp, SELMB[64:65, 0, :], CF1L, False, True, reads=[Bosb["cf1l"], Bcc], writes=[Bimpp])
                tt("dve", AC[p_][:, r, :], OCSB[0:64, :], bcp, ALU.mult,
                   reads=[Bosb["c"], Bimpp], writes=[Bac[p_][r]])
            items += to_groups(tiles, [(delays[0], fin1), (delays[1], fin1b), (delays[2], fin2)])
            return items

        def stage2_dve(idx):
            g, s = insts[idx]
            dma("sp", VCM, vcm_d[s, :, :, :], c_mask, writes=[Bmask])
            dma("sp", ADDM, addm_d[s, :, :, :], c_mask, writes=[Bmask])
            tt("dve", SCORE, IMP, VCM, ALU.mult, reads=[Bimp, Bmask], writes=[Bosb["score"]])
            tt("dve", SCORE, SCORE, ADDM, ALU.add, reads=[Bosb["score"], Bmask], writes=[Bosb["score"]])
            Btop = [Buf("top%d" % u) for u in range(4)]
            Bwk = [Buf("wk%d" % u) for u in range(4)]
            for u in range(4):
                S.op("dve", lambda e, u=u: e.max(out=TOP[:, u, 0:8], in_=SCORE[:, u, :]),
                     reads=[Bosb["score"], Bosb["work"], Bosb["top"]], writes=[Btop[u]])
            for u in range(4):
                S.op("dve", lambda e, u=u: e.match_replace(out=WORK[:, u, :], in_to_replace=TOP[:, u, 0:8],
                                                         in_values=SCORE[:, u, :], imm_value=-1e30),
                     reads=[Bosb["score"], Btop[u]], writes=[Bwk[u]])
            for u in range(4):
                S.op("dve", lambda e, u=u: e.max(out=TOP[:, u, 8:16], in_=WORK[:, u, :]),
                     reads=[Bwk[u]], writes=[Btop[u]])
            for u in range(4):
                ts("dve", WORK[:, u, :], SCORE[:, u, :], TOP[:, u, 15:16], ALU.is_ge,
                   reads=[Bosb["score"], Btop[u], Bwk[u]], writes=[Bwk[u]])
            tt("dve", WORK, WORK, VCM, ALU.mult, reads=Bwk + [Bosb["work"], Bmask], writes=Bwk + [Bosb["work"], Bosb["top"]])
            ts("dve", MNB_N, WORK, -1.0, ALU.add, -NEGBIG, ALU.mult, reads=[Bosb["work"]], writes=[Bosb["mnb"]])
            ts("dve", MNB_S[:, :, 0:64], WORK[:, :, 64:128], -1.0, ALU.add, -NEGBIG, ALU.mult,
               reads=[Bosb["work"]], writes=[Bosb["mnbs"]])
            ts("dve", MNB_S[:, :, 64:128], WORK[:, :, 0:64], -1.0, ALU.add, -NEGBIG, ALU.mult,
               reads=[Bosb["work"]], writes=[Bosb["mnbs"]])

        def stage2_pe(idx):
            p_ = idx % 2
            pst = ps_m[2][:, 0:512].bitcast(BF16)
            for u in range(4):
                tr(pst[:, u * 128:(u + 1) * 128], MNB_N[:, u, :], reads=[Bosb["mnb"], Bc], writes=[Bocp])
                tr(pst[:, 512 + u * 128:512 + (u + 1) * 128], MNB_S[:, u, :], reads=[Bosb["mnbs"], Bc], writes=[Bocp])
            vcopy("dve", MT1[p_], pst[64:128, 0:512], reads=[Bocp], writes=[Bmt[p_]])
            vcopy("dve", MT0[p_], pst[64:128, 512:1024], reads=[Bocp], writes=[Bmt[p_]])

        def stage3_head(idx, r):
            g, s = insts[idx]
            p_ = idx % 2
            T = 3 + 4 * s
            h = 4 * g + r
            q = h // 2
            half = h % 2
            pr = slice(64 * half, 64 * half + 64)
            qcols = slice(s * 512, (s + 1) * 512)
            k_ = hctr[0] % 2
            hctr[0] += 1
            QT_h = QTP[pr, q, qcols]

            def setup():
                dma("sp", GST[k_], G24[3 * h:3 * h + 3, qcols], c_gst[k_], writes=[Bgst[k_]])
                dma("sp", HS[k_], bass.AP(fd_d.tensor, h * LEN_F, [[1, 128], [1, 1024]]), c_tab[k_], writes=[Bhsw[k_]])
                dma("sp", HW[k_], bass.AP(fd_d.tensor, h * LEN_F + OFF_W, [[1, 128], [1, 1408]]), c_tab[k_], writes=[Bhsw[k_]])
                if half == 0:
                    vcopy("dve", X0[k_][0:64, :], QTP[0:64, q, qcols], writes=[Bx[k_]])
                    vcopy("dve", X1[k_][0:64, :], QTP[0:64, q, qcols], writes=[Bx[k_]])
                else:
                    dma("pool", X0[k_][0:64, :], QTP[64:128, q, qcols], c_x2[k_], writes=[Bx[k_]])
                    dma("pool", X1[k_][0:64, :], QTP[64:128, q, qcols], c_x2[k_], writes=[Bx[k_]])
                    dma("pool", SZO[k_], SZP[64:128, q, qcols], c_szo[k_], reads=[Bszp[h]], writes=[Bszo[k_]])
                vcopy("dve", X0[k_][64:128, :], MT0[p_], reads=[Bmt[p_]], writes=[Bx[k_]])
                vcopy("dve", X1[k_][64:128, :], MT1[p_], reads=[Bmt[p_]], writes=[Bx[k_]])
            items = [dict(call=setup)]
            nsel = 16 * (s + 1)

            def sel_tile(kt, first, last):
                Xs = X0[k_] if kt < 32 else X1[k_]
                mms = [(KAUG[:, g, kt * 128:(kt + 1) * 128], Xs)]
                delta = 128 * kt - 512 * T
                cols = (0, 512)
                if delta >= -128:
                    col0 = 384 - delta
                    mms.append((ANTI, HS[k_][:, col0:col0 + 512]))
                    if delta > 0:
                        cols = (delta, 512)

                def pvf(Ph, Bp_, lo, hi, kt=kt):
                    mm(ps_m[0][0:67, lo:hi], VS[:, kt, g, 0:67], Ph, first, last, reads=[Bp_], writes=[Bps[0]])
                return dict(mms=mms, reads=[Bx[k_], Bhsw[k_]], pv=pvf, h=h, cols=cols)

            def win_tile(delta, first, last):
                kt = (512 * T + delta) // 128
                slot = 8 * s + (kt - (4 * T - 4))
                col0 = 384 - delta
                mms = [(KWIN[pr, g, slot * 128:(slot + 1) * 128], QT_h), (ANTI, HW[k_][:, col0:col0 + 512])]
                lo = max(0, delta)
                hi = min(512, ((delta + 638) // 128 + 1) * 128)
                hi = min(512, max(hi, 128))

                def pvf(Ph, Bp_, lo_, hi_, slot=slot):
                    mm(ps_m[1][0:67, lo_:hi_], VW[:, slot, g, 0:67], Ph, first, last, reads=[Bp_], writes=[Bps[1]])
                return dict(mms=mms, reads=[Bhsw[k_]], pv=pvf, h=h, cols=(lo, hi))

            far = list(range(0, 4 * T - 1))
            order = [far[0], far[1], 4 * T + 1, far[2], 4 * T + 2, far[3], 4 * T + 3, far[4], 4 * T - 1, 4 * T] + far[5:]
            assert len(order) == nsel and order[-1] in far and len(set(order)) == nsel
            tiles = [sel_tile(kt, i_ == 0, i_ == nsel - 1) for i_, kt in enumerate(order)]
            worder = [0, -256, 256, -384, 384, -512, 128, -128]
            tiles += [win_tile(d_, i_ == 0, i_ == 7) for i_, d_ in enumerate(worder)]

            def fin1():
                vcopy("dve", OSB_S, ps_m[0][0:67, :], reads=[Bps[0]], writes=[Bosb["s"]])
                vcopy("dve", OSB_W, ps_m[1][0:67, :], reads=[Bps[1]], writes=[Bosb["w"]])
                tt("dve", ZSW, OSB_S[64:67, :], OSB_W[64:67, :], ALU.add, reads=[Bosb["s"], Bosb["w"]], writes=[Bosb["zsw"]])
                S.op("dve", lambda e: e.reciprocal(out=ZSW, in_=ZSW), reads=[Bosb["zsw"]], writes=[Bosb["zsw"]])
                tt("dve", COEF, ZSW, GST[k_], ALU.mult, reads=[Bosb["zsw"], Bgst[k_]], writes=[Bosb["coef"]])
                vcopy("dve", COEFH, COEF, reads=[Bosb["coef"]], writes=[Bosb["coefh"]])
                tt("dve", COEFL, COEF, COEFH, ALU.subtract, reads=[Bosb["coef"], Bosb["coefh"]], writes=[Bosb["coefl"]])

            def bc_mm(br):
                bcp = ps_m[3][0:64, :]
                mm(bcp, SELMB[64:67, br, :], COEFH, True, False, reads=[Bosb["coefh"], Bcc], writes=[Bimpp])
                mm(bcp, SELMB[64:67, br, :], COEFL, False, True, reads=[Bosb["coefl"], Bcc], writes=[Bimpp])

            def fin2():
                bc_mm(1)
                tt("dve", OSB_S[0:64, :], OSB_S[0:64, :], ps_m[3][0:64, :], ALU.mult,
                   reads=[Bosb["s"], Bimpp], writes=[Bosb["s"]])

            def fin3():
                bc_mm(2)
                tt("dve", OSB_W[0:64, :], OSB_W[0:64, :], ps_m[3][0:64, :], ALU.mult,
                   reads=[Bosb["w"], Bimpp], writes=[Bosb["w"]])
                tt("dve", OSB_S[0:64, :], OSB_S[0:64, :], OSB_W[0:64, :], ALU.add,
                   reads=[Bosb["s"], Bosb["w"]], writes=[Bosb["s"]])
                tt("dve", OSB_S[0:64, :], OSB_S[0:64, :], AC[p_][:, r, :], ALU.add,
                   reads=[Bosb["s"], Bac[p_][r]], writes=[Bosb["s"]])
                if half == 0:
                    tt("dve", SZP[0:64, q, qcols], OSB_S[0:64, :], SZP[0:64, q, qcols], ALU.mult,
                       reads=[Bosb["s"], Bszp[h]], writes=[Bszp[h]])
                else:
                    tt("dve", MXO[k_], OSB_S[0:64, :], SZO[k_], ALU.mult, reads=[Bosb["s"], Bszo[k_]], writes=[Bmxo[k_]])
                    dma("pool", SZP[64:128, q, qcols], MXO[k_], c_mxo[k_], reads=[Bmxo[k_]], writes=[Bszp[h]])
            items += to_groups(tiles, [(1, fin1), (9, fin2), (11, fin3)])
            return items

        seq = []
        for r in range(4):
            seq += stage1_head(0, r, delays=(1, 1, 2), gst=(GST[r % 2][0:1, :], Bgst[r % 2], c_gst[r % 2]))
        run_seq(seq)
        pipe.flush()
        stage2_dve(0)
        stage2_pe(0)
        for idx in range(8):
            seq3 = []
            for r in range(4):
                seq3 += stage3_head(idx, r)
            if idx + 1 < 8:
                n3 = sum(1 for it in seq3 if "call" not in it)
                ins = []
                step = max(11, int(0.12 * n3))
                for r in range(4):
                    ins.append((1 + r * step, stage1_head(idx + 1, r)))
                p_dve = max(int(0.60 * n3), 1 + 3 * step + 2 + 4)
                ins.append((p_dve, [dict(call=lambda idx=idx: stage2_dve(idx + 1))]))
                ins.append((p_dve + 8, [dict(call=lambda idx=idx: stage2_pe(idx + 1))]))
                seq3 = merge(seq3, ins)
            run_seq(seq3)
        pipe.flush()
        barrier()
        if DEBUG == "C":
            dump("szp", SZP, BF16)
            finish()
            return nc, dbg

        WOA = c3(0, [4, 1024], BF16)
        WOC = c3(8192, [4, 1024], BF16)
        WOS = [c3(16384 + 4096 * k, [1024], F32) for k in range(4)]
        XR = [c3(32768 + 4096 * k, [1024], F32) for k in range(3)]
        YT = [c3(45056 + 4096 * k, [1024], F32) for k in range(4)]
        FNW = c3(61440, [1024], F32)
        JUNKF = c3(65536, [1024], F32)
        c_wo = [newchan("c_wo%d" % k) for k in range(4)]
        c_xr = [newchan("c_xr%d" % k) for k in range(3)]
        c_out = [newchan("c_out%d" % k) for k in range(4)]
        c_fnw = newchan("c_fnw")
        Bwos = [Buf("wos%d" % k) for k in range(4)]
        Bwo = Buf("wo")
        Bxr = [Buf("xr%d" % k) for k in range(3)]
        Byt = [Buf("yt%d" % k) for k in range(4)]
        Bssd = [Buf("ssd%d" % k) for k in range(8)]
        Bfnw = Buf("fnw")
        dma("sp", FNW, bass.AP(fnw_d.tensor, 0, [[0, 128], [1, 1024]]), c_fnw, writes=[Bfnw])
        for rc in range(8):
            k_ = rc % 4
            dma("sp", WOS[k_], wout_d[128 * rc:128 * rc + 128, :], c_wo[k_], writes=[Bwos[k_]])
            dst = WOA[:, rc, :] if rc < 4 else WOC[:, rc - 4, :]
            vcopy("dve", dst, WOS[k_], reads=[Bwos[k_]], writes=[Bwo] if rc in (0, 7) else [])

        def d_st1(n):
            s, sub = n // 4, n % 4
            kp, kx, ky = n % 2, n % 3, n % 4
            tc0 = s * 512 + sub * 128
            L0 = 512 * (3 + 4 * s) + 128 * sub
            dma("pool", XR[kx], x_d[L0:L0 + 128, :], c_xr[kx], writes=[Bxr[kx]])
            for hf in range(2):
                pso = ps_sc[kp][:, hf * 512:(hf + 1) * 512]
                for q in range(4):
                    mm(pso, SZP[:, q, tc0:tc0 + 128], WOA[:, q, hf * 512:(hf + 1) * 512], q == 0, False,
                       reads=[Bwo], writes=[Bsc[kp]])
                for ch in range(4):
                    mm(pso, MC[:, ch, tc0:tc0 + 128], WOC[:, ch, hf * 512:(hf + 1) * 512], False, ch == 3,
                       reads=[Bwo], writes=[Bsc[kp]])
            tt("dve", YT[ky], ps_sc[kp][:, :], XR[kx], ALU.add, reads=[Bsc[kp], Bxr[kx]], writes=[Byt[ky]])

        def d_sq(n):
            ss = SMALL[:, (n % 8) * 2:(n % 8) * 2 + 1]
            act(JUNKF, YT[n % 4], AF.Square, reads=[Byt[n % 4]], writes=[Bssd[n % 8]], accum_out=ss)

        def d_st2a(n):
            ss = SMALL[:, (n % 8) * 2:(n % 8) * 2 + 1]
            rs = SMALL[:, (n % 8) * 2 + 1:(n % 8) * 2 + 2]
            ts("dve", rs, ss, 1.0 / DM, ALU.mult, EPS, ALU.add, reads=[Bssd[n % 8]], writes=[Bssd[n % 8]])
            act(rs, rs, AF.Sqrt, reads=[Bssd[n % 8]], writes=[Bssd[n % 8]])

        def d_st2b(n):
            s, sub = n // 4, n % 4
            ky = n % 4
            tc0 = s * 512 + sub * 128
            rs = SMALL[:, (n % 8) * 2 + 1:(n % 8) * 2 + 2]
            S.op("dve", lambda e, rs=rs: e.reciprocal(out=rs, in_=rs), reads=[Bssd[n % 8]], writes=[Bssd[n % 8]])
            stt("dve", YT[ky], YT[ky], rs, FNW, ALU.mult, ALU.mult, reads=[Byt[ky], Bssd[n % 8], Bfnw], writes=[Byt[ky]])
            dma("sp", out_d[tc0:tc0 + 128, :], YT[ky], c_out[ky], reads=[Byt[ky]])

        d_st1(0)
        d_sq(0)
        for n in range(16):
            if n + 1 < 16:
                d_st1(n + 1)
            d_st2a(n)
            if n + 1 < 16:
                d_sq(n + 1)
            d_st2b(n)
        finish()
    return nc, dbg


_PROG = {}


def _get_program():
    if "p" not in _PROG:
        _PROG["p"] = build_program()
    return _PROG["p"]


def make_in_maps(x, norm_w, w_in, w_ck1, w_ck2, pe_k, w_cv1, w_cv2, pe_v, conv_w, w_out, rel_bias, final_norm_w):
    f32 = np.float32
    x = np.asarray(x, f32)
    w_in0 = np.ascontiguousarray(np.asarray(w_in, f32)[0])
    shared = dict(_shared_consts())
    shared.pop("ov_base")
    gate_cols = [1280 + br * 8 + h for h in range(8) for br in range(3)]
    shared.update(
        w_in=w_in0,
        w_gate=np.ascontiguousarray(w_in0[:, gate_cols]),
        nw=np.ascontiguousarray(np.asarray(norm_w, f32)[0].reshape(8, 128).T),
        w1k=np.ascontiguousarray(np.asarray(w_ck1, f32)[0].transpose(1, 0, 2)),
        w1v=np.ascontiguousarray(np.asarray(w_cv1, f32)[0].transpose(1, 0, 2)),
        w2k=np.ascontiguousarray(np.asarray(w_ck2, f32)[0]),
        w2v=np.ascontiguousarray(np.asarray(w_cv2, f32)[0]),
        pekT=np.ascontiguousarray(np.asarray(pe_k, f32)[0].T),
        pevT=np.ascontiguousarray(np.asarray(pe_v, f32)[0].T),
        cw=np.ascontiguousarray(np.asarray(conv_w, f32)[0].T.reshape(4, 128, 3).transpose(1, 0, 2).reshape(128, 12)),
        w_out=np.ascontiguousarray(np.asarray(w_out, f32)[0]),
        rel_bias=np.ascontiguousarray(np.asarray(rel_bias, f32)),
        fnw=np.ascontiguousarray(np.asarray(final_norm_w, f32)[None, :]),
    )
    in_maps = []
    for core in range(8):
        b, i = core // 4, core % 4
        pad = 1536 - 512 * i
        xc = np.zeros((SEQ, DM), f32)
        xc[pad:] = x[b, :SEQ - pad]
        m = dict(shared)
        m.update(_core_consts(i))
        m["x"] = xc
        in_maps.append(m)
    return in_maps


def kernel(x, norm_w, w_in, w_ck1, w_ck2, pe_k, w_cv1, w_cv2, pe_v, conv_w, w_out, rel_bias, final_norm_w):
    nc, _ = _get_program()
    in_maps = make_in_maps(x, norm_w, w_in, w_ck1, w_ck2, pe_k, w_cv1, w_cv2, pe_v, conv_w, w_out, rel_bias,
                           final_norm_w)
    res = run_bass_kernel_spmd(nc, in_maps, core_ids=list(range(8)))
    out = np.zeros((2, SEQ, DM), np.float32)
    for core in range(8):
        b, i = core // 4, core % 4
        o = res.results[core]["out"]
        for s in range(4):
            a = 4 * s + i
            out[b, 512 * a:512 * a + 512] = o[512 * s:512 * s + 512]
    return out
```

```python
import os
from contextlib import ExitStack

import numpy as np
import ml_dtypes

import concourse.bass as bass
import concourse.mybir as mybir
from concourse.bass_utils import run_bass_kernel_spmd

F32 = mybir.dt.float32
BF16 = mybir.dt.bfloat16
AF = mybir.ActivationFunctionType
ALU = mybir.AluOpType
NPBF = ml_dtypes.bfloat16

NEGBIG = -30000.0
EPS = 1e-6
SEQ = 8192
DM = 1024
NCOLS = 3864
LEN_S, LEN_W, LEN_C = 1151, 1535, 2544
LEN_F = LEN_S + LEN_W + LEN_C
OFF_W = LEN_S
OFF_C = LEN_S + LEN_W

DEBUG = os.environ.get("KDEBUG", "")
KCUT = int(os.environ.get("KCUT", "0"))


class Buf:
    __slots__ = ("name", "w", "r")

    def __init__(self, name=""):
        self.name = name
        self.w = None
        self.r = []


class Chan:
    __slots__ = ("sem", "n")

    def __init__(self, sem):
        self.sem = sem
        self.n = 0


class Op:
    __slots__ = ("eng", "fn", "deps", "chan", "chan_n", "sig", "signo", "is_dma", "chan_waits")

    def __init__(self, eng, fn, is_dma):
        self.eng = eng
        self.fn = fn
        self.deps = []
        self.chan = None
        self.chan_n = 0
        self.sig = False
        self.signo = 0
        self.is_dma = is_dma
        self.chan_waits = None


class Sched:
    ENGS = ("pe", "act", "dve", "pool", "sp")

    def __init__(self, nc):
        self.nc = nc
        self.ops = {e: [] for e in self.ENGS}
        self.all_ops = []
        self.cur_barrier = None
        self.chans = []

    def new_chan(self, sem):
        c = Chan(sem)
        self.chans.append(c)
        return c

    def _dep(self, op, other):
        if other is None or other is op:
            return
        if (not other.is_dma) and (not op.is_dma) and other.eng == op.eng and op.eng == "pe":
            return
        if other not in op.deps:
            op.deps.append(other)

    def op(self, eng, fn, reads=(), writes=(), chan=None):
        is_dma = chan is not None
        o = Op(eng, fn, is_dma)
        if self.cur_barrier is not None:
            o.deps.append(self.cur_barrier)
        for b in reads:
            self._dep(o, b.w)
        for b in writes:
            self._dep(o, b.w)
            for r in b.r:
                self._dep(o, r)
        for b in reads:
            b.r.append(o)
        for b in writes:
            b.w = o
            b.r = []
        if is_dma:
            chan.n += 1
            o.chan = chan
            o.chan_n = chan.n
        self.ops[eng].append(o)
        self.all_ops.append(o)
        return o

    def barrier(self, fn, chan):
        o = Op("sp", fn, True)
        for e in self.ENGS:
            for prev in reversed(self.ops[e]):
                if not prev.is_dma:
                    o.deps.append(prev)
                    break
        o.chan_waits = [(c, c.n) for c in self.chans if c.n > 0]
        chan.n += 1
        o.chan = chan
        o.chan_n = chan.n
        self.ops["sp"].append(o)
        self.all_ops.append(o)
        self.cur_barrier = o
        return o

    def emit(self, sems):
        nc = self.nc
        for o in self.all_ops:
            for d in o.deps:
                if not d.is_dma:
                    d.sig = True
        for e in self.ENGS:
            n = 0
            for o in self.ops[e]:
                if o.sig and not o.is_dma:
                    n += 1
                    o.signo = n
        all_chans = self.chans

        def run_engine(ename, eng):
            seen = {}
            for o in self.ops[ename]:
                need = {}
                for d in o.deps:
                    if d.is_dma:
                        key = ("c", id(d.chan))
                        val = 16 * d.chan_n
                        sem = d.chan.sem
                    else:
                        key = ("e", d.eng)
                        val = d.signo
                        sem = sems[d.eng]
                    if val > need.get(key, (0, None))[0]:
                        need[key] = (val, sem)
                if o.chan_waits:
                    for c, n in o.chan_waits:
                        key = ("c", id(c))
                        if 16 * n > need.get(key, (0, None))[0]:
                            need[key] = (16 * n, c.sem)
                for key, (val, sem) in need.items():
                    if seen.get(key, 0) >= val:
                        continue
                    seen[key] = val
                    eng.wait_ge(sem, val)
                ins = o.fn(eng)
                if o.is_dma:
                    ins.then_inc(o.chan.sem, 16)
                elif o.sig:
                    ins.then_inc(sems[ename], 1)
            if ename == "sp":
                for c in all_chans:
                    if c.n > 0 and seen.get(("c", id(c)), 0) < 16 * c.n:
                        eng.wait_ge(c.sem, 16 * c.n)
                for e2 in self.ENGS:
                    last = 0
                    for o2 in self.ops[e2]:
                        if o2.sig and not o2.is_dma:
                            last = o2.signo
                    if last > 0:
                        eng.wait_ge(sems[e2], last)

        with nc.Block() as block:
            @block.tensor
            def _(eng):
                run_engine("pe", eng)

            @block.scalar
            def _(eng):
                run_engine("act", eng)

            @block.vector
            def _(eng):
                run_engine("dve", eng)

            @block.gpsimd
            def _(eng):
                run_engine("pool", eng)

            @block.sync
            def _(eng):
                run_engine("sp", eng)


def _t5_bucket(d):
    n = np.maximum(d, 0)
    nf = np.maximum(n, 1).astype(np.float32)
    large = 16 + (np.log(nf / np.float32(16.0)) / np.float32(np.log(8.0)) * np.float32(16.0)).astype(np.int32)
    large = np.minimum(large, 31)
    return np.where(n < 16, n, large)


def _onehot_rows():
    oh = np.zeros((33, LEN_F), np.float32)

    def fill(off, length, i0, wmax):
        d = np.arange(length) - i0
        masked = d < 0
        if wmax is not None:
            masked |= d >= wmax
        bk = _t5_bucket(d)
        for i in range(length):
            if masked[i]:
                oh[32, off + i] = 1.0
            else:
                oh[bk[i], off + i] = 1.0

    fill(0, LEN_S, 511, None)
    fill(OFF_W, LEN_W, 511, 512)
    fill(OFF_C, LEN_C, 527, None)
    return oh


_CONST_CACHE = {}


def _shared_consts():
    if "c" in _CONST_CACHE:
        return _CONST_CACHE["c"]
    ident = np.eye(128, dtype=np.float32).astype(NPBF)
    antiI = np.eye(128, dtype=np.float32)[::-1].copy().astype(NPBF)
    L = np.arange(SEQ)
    eall = np.zeros((64, SEQ), np.float32)
    eall[(L // 64) % 64, L] = 1.0
    c = np.arange(512)
    j = np.arange(128)
    ov = ((16 * c[:, None] < 64 * j[None, :] + 64) & (16 * c[:, None] + 32 > 64 * j[None, :])).astype(np.float32)
    ov[511, :] = 0.0
    ov = ov.reshape(4, 128, 128).transpose(1, 0, 2).copy()
    sel = np.zeros((67, 3, 64), np.float32)
    for br in range(3):
        sel[64 + br, br, :] = 1.0
    d = dict(ident=ident, antiI=antiI, eall=eall.astype(NPBF), ov_base=ov, oh=_onehot_rows(), selm=sel)
    _CONST_CACHE["c"] = d
    return d


def _core_consts(i):
    pad = 1536 - 512 * i
    jmin = pad // 64
    kinv = (np.arange(1536) < pad).astype(np.float32)[None].astype(NPBF)
    cinv = (16 * np.arange(128) < pad).astype(np.float32)[None].astype(NPBF)
    vcm = np.zeros((4, 128, 4, 128), np.float32)
    addm = np.zeros((4, 128, 4, 128), np.float32)
    jj = np.arange(128)[None, :]
    for s in range(4):
        T = 3 + 4 * s
        for u in range(4):
            tL = 512 * T + 128 * u + np.arange(128)
            blk = (tL // 64)[:, None]
            valid = jj >= jmin
            vc = valid & (jj <= blk)
            forced = ((jj == jmin) | (jj == blk) | (jj == blk - 1)) & vc
            vcm[s, :, u, :] = vc
            addm[s, :, u, :] = (vc.astype(np.float32) - 1.0) + 2000.0 * forced
    cval = ((16 * np.arange(512) >= pad) & (np.arange(512) <= 510)).astype(np.float32)
    cval_pc = cval.reshape(4, 128).T
    ov = _shared_consts()["ov_base"].copy()
    ov[:, :, 127] = 1.0
    ov = ov * cval_pc[:, :, None]
    kvw = np.zeros((128, 32), np.float32)
    for s in range(4):
        for wi in range(8):
            kt = 8 + 16 * s + wi
            kvw[:, 8 * s + wi] = (128 * kt + np.arange(128)) >= pad
    return dict(vcm=vcm, addm=addm, ov=ov.astype(NPBF), kvw=kvw, cvc=np.ascontiguousarray(cval_pc))


P1 = 0
O_KAUG = 0
O_VS = 32768
O_KWIN = 50176
O_VW = 66560
O_KCT = 75264
O_VC = 77312
P2 = 78464
P3 = P2 + 57344
P3_SIZE = 72704
PC = P3 + P3_SIZE
ARENA_BYTES = PC + 3584


def build_program(stop_after="D"):
    nc = bass.Bass("TRN2", target_bir_lowering=False)

    def din(name, shape, dt=F32):
        return nc.dram_tensor(name, list(shape), dt, kind="ExternalInput").ap()

    x_d = din("x", [SEQ, DM])
    win_d = din("w_in", [DM, NCOLS])
    wgate_d = din("w_gate", [DM, 24])
    nw_d = din("nw", [128, 8])
    w1k_d = din("w1k", [64, 32, 128])
    w1v_d = din("w1v", [64, 32, 128])
    w2k_d = din("w2k", [128, 64])
    w2v_d = din("w2v", [128, 64])
    pek_d = din("pekT", [64, 32])
    pev_d = din("pevT", [64, 32])
    cw_d = din("cw", [128, 12])
    wout_d = din("w_out", [DM, DM])
    relb_d = din("rel_bias", [32, 8])
    fnw_d = din("fnw", [1, DM])
    ident_d = din("ident", [128, 128], BF16)
    anti_d = din("antiI", [128, 128], BF16)
    eall_d = din("eall", [64, SEQ], BF16)
    ov_d = din("ov", [128, 4, 128], BF16)
    oh_d = din("oh", [33, LEN_F])
    selm_d = din("selm", [67, 3, 64])
    kvw_d = din("kvw", [128, 32])
    cvc_d = din("cvc", [128, 4])
    vcm_d = din("vcm", [4, 128, 4, 128])
    addm_d = din("addm", [4, 128, 4, 128])
    out_d = nc.dram_tensor("out", [2048, DM], F32, kind="ExternalOutput").ap()
    fd_d = nc.dram_tensor("fd_scr", [8, LEN_F], BF16, kind="Internal").ap()
    bar_d = nc.dram_tensor("bar_scr", [1, 64], F32, kind="Internal").ap()
    dbg = {}

    def dbg_out(name, shape, dt):
        t = nc.dram_tensor("dbg_" + name, list(shape), dt, kind="ExternalOutput").ap()
        dbg[name] = t
        return t

    with ExitStack() as es:
        arena = es.enter_context(nc.sbuf_tensor("arena", [128, ARENA_BYTES // 2], BF16))
        ps_sc = [es.enter_context(nc.psum_tensor("ps_sc%d" % k, [128, 1024], F32)) for k in range(2)]
        ps_m = [es.enter_context(nc.psum_tensor("ps_m%d" % k, [128, 512], F32)) for k in range(4)]
        sems = {e: es.enter_context(nc.semaphore("s_" + e)) for e in Sched.ENGS}
        S = Sched(nc)

        def newchan(name):
            return S.new_chan(es.enter_context(nc.semaphore(name)))

        bar_chan = newchan("c_bar")

        def carve(off, shape, dt, p0=0, p1=128):
            n = 1
            for d_ in shape:
                n *= d_
            nb = n * (4 if dt == F32 else 2)
            assert off % 4 == 0 and off + nb <= ARENA_BYTES, (off, nb)
            v = arena[p0:p1, off // 2: (off + nb) // 2]
            if dt == F32:
                v = v.bitcast(F32)
            if len(shape) == 2:
                v = v.rearrange("p (a b) -> p a b", a=shape[0])
            elif len(shape) == 3:
                v = v.rearrange("p (a b c) -> p a b c", a=shape[0], b=shape[1])
            return v

        def barrier():
            S.barrier(lambda e: e.dma_start(out=bar_d[0:1, 0:16], in_=fnw_d[0:1, 0:16]), bar_chan)

        def mm(out, lhsT, rhs, start, stop, reads=(), writes=()):
            return S.op("pe", lambda e: e.matmul(out, lhsT=lhsT, rhs=rhs, start=start, stop=stop), reads, writes)

        def tr(out, in_, reads=(), writes=()):
            return S.op("pe", lambda e: e.transpose(out=out, in_=in_, identity=IDENT), reads, writes)

        def act(out, in_, func, reads=(), writes=(), bias=None, scale=None, accum_out=None):
            kw = {}
            if bias is not None:
                kw["bias"] = bias
            if scale is not None:
                kw["scale"] = scale
            if accum_out is not None:
                kw["accum_out"] = accum_out
            return S.op("act", lambda e: e.activation(out=out, in_=in_, func=func, **kw), reads, writes)

        def vcopy(eng, out, in_, reads=(), writes=()):
            return S.op(eng, lambda e: e.tensor_copy(out=out, in_=in_), reads, writes)

        def tt(eng, out, in0, in1, op, reads=(), writes=()):
            return S.op(eng, lambda e: e.tensor_tensor(out=out, in0=in0, in1=in1, op=op), reads, writes)

        def ts(eng, out, in0, s1, op0, s2=None, op1=None, reads=(), writes=()):
            if op1 is None:
                return S.op(eng, lambda e: e.tensor_scalar(out=out, in0=in0, scalar1=s1, scalar2=None, op0=op0), reads, writes)
            return S.op(eng, lambda e: e.tensor_scalar(out=out, in0=in0, scalar1=s1, scalar2=s2, op0=op0, op1=op1), reads, writes)

        def stt(eng, out, in0, scalar, in1, op0, op1, reads=(), writes=()):
            return S.op(eng, lambda e: e.scalar_tensor_tensor(out=out, in0=in0, scalar=scalar, in1=in1, op0=op0, op1=op1), reads, writes)

        def memset(eng, ap, val, reads=(), writes=()):
            return S.op(eng, lambda e: e.memset(ap, val), reads, writes)

        def dma(eng, out, in_, chan, reads=(), writes=()):
            return S.op(eng, lambda e: e.dma_start(out=out, in_=in_), reads, writes, chan=chan)

        KAUG = carve(O_KAUG, [2, 8192], BF16)
        VS = carve(O_VS, [64, 2, 68], BF16)
        KWIN = carve(O_KWIN, [2, 4096], BF16)
        VW = carve(O_VW, [32, 2, 68], BF16)
        KCT = carve(O_KCT, [2, 512], BF16)
        VC = carve(O_VC, [4, 2, 68], BF16)
        QTP = carve(P2 + 0, [4, 2048], BF16)
        SZP = carve(P2 + 16384, [4, 2048], BF16)
        MC = carve(P2 + 32768, [4, 2048], BF16)
        G24 = carve(P2 + 49152, [2048], F32, 0, 24)
        IDENT = carve(PC + 0, [128], BF16)
        ANTI = carve(PC + 256, [128], BF16)
        NW = carve(PC + 512, [8], F32)
        B31 = carve(PC + 544, [8], F32)
        OV = carve(PC + 576, [4, 128], BF16)
        SELM = carve(PC + 1600, [3, 64], F32, 0, 67)
        CW = carve(PC + 2368, [12], F32)
        HIDB = carve(PC + 2416, [2], F32)
        RELB = carve(PC + 2432, [8], F32, 0, 33)
        RB31 = carve(PC + 2464, [8], F32, 0, 32)
        ONES = carve(PC + 2496, [2], BF16)
        SMALL = carve(PC + 2560, [64], F32)
        c_const = newchan("c_const")
        c_dbg = newchan("c_dbg")
        Bps = [Buf("psm%d" % k) for k in range(4)]
        Bsc = [Buf("pssc%d" % k) for k in range(2)]
        Bsc4 = [Buf("pssch%d" % k) for k in range(4)]
        Bc = Buf("consts")
        Bsmall = Buf("small")

        def finish():
            S.emit(sems)

        def dump(name, ap, dt):
            t = dbg_out(name, list(ap.shape), dt)
            idx = tuple(slice(None) for _ in ap.shape)
            dma("sp", t[idx], ap, c_dbg)

        KVW = carve(PC + 2816, [32], F32)
        CVC = carve(PC + 2944, [4], F32)
        b31_src = bass.AP(relb_d.tensor, 31 * 8, [[0, 128], [1, 8]])
        rb31_src = bass.AP(relb_d.tensor, 31 * 8, [[0, 32], [1, 8]])
        const_loads = [(IDENT, ident_d[:, :]), (ANTI, anti_d[:, :]), (NW, nw_d[:, :]), (OV, ov_d[:, :, :]),
                       (SELM, selm_d[:, :, :]), (CW, cw_d[:, :]), (RELB[0:32, :], relb_d[:, :]),
                       (B31, b31_src), (RB31, rb31_src), (KVW, kvw_d[:, :]), (CVC, cvc_d[:, :]),
                       (KAUG[64:128, 0, :], eall_d[:, :]), (KAUG[64:128, 1, :], eall_d[:, :])]
        for i_, (dst, src_) in enumerate(const_loads):
            dma("sp", dst, src_, c_const, writes=[Bc] if i_ == len(const_loads) - 1 else [])
        Bvi = Buf("vind")
        memset("pool", ONES, 1.0, writes=[Bvi])
        for (V_, one) in ((VS, 65), (VW, 66), (VC, 64)):
            memset("pool", V_[:, :, :, 64:68], 0.0, writes=[Bvi])
            memset("pool", V_[:, :, :, one:one + 1], 1.0, writes=[Bvi])
        memset("pool", VS[:, :, :, 64:65], 0.5, writes=[Bvi])
        memset("pool", VW[:, :, :, 64:65], 0.5, writes=[Bvi])
        for g in range(2):
            tt("dve", VW[:, :, g, 64], VW[:, :, g, 64], KVW, ALU.mult, reads=[Bc, Bvi], writes=[Bvi])
            tt("dve", VW[:, :, g, 66], VW[:, :, g, 66], KVW, ALU.mult, reads=[Bc, Bvi], writes=[Bvi])
            tt("dve", VC[:, :, g, 64], VC[:, :, g, 64], CVC, ALU.mult, reads=[Bc, Bvi], writes=[Bvi])
        KCMPT = carve(P2 + 0, [8192], BF16)
        VCMPT = carve(P2 + 16384, [8192], BF16)
        XNTOWN = carve(P3 + 0, [8, 4, 514], BF16)
        XNTW = [carve(P3 + 32896 + 8192 * k, [8, 512], BF16) for k in range(2)]
        WKV = carve(P3 + 49280, [8, 1024], BF16)
        WSTG = [carve(P3 + 65664 + 2048 * k, [8, 64], F32) for k in range(2)]
        win_v = win_d.rearrange("(c p) n -> p c n", p=128)
        c_w = [newchan("c_w%d" % k) for k in range(2)]
        Bwstg = [Buf("wstg%d" % k) for k in range(2)]
        Bwkv = Buf("wkv")
        wcnt = [0]

        def load_w_block(dst_fn, src_v, col0, ncol, stg, Bdst):
            k_ = wcnt[0] % 2
            wcnt[0] += 1
            dma("sp", stg[k_][:, :, 0:ncol], src_v[:, :, col0:col0 + ncol], c_w[k_], writes=[Bwstg[k_]])
            for c in range(8):
                ts("dve", dst_fn(c), stg[k_][:, c, 0:ncol], NW[:, c:c + 1], ALU.mult,
                   reads=[Bwstg[k_], Bc], writes=[Bdst] if c in (0, 7) else [])

        XT = [carve(P2 + 32768 + 4096 * k, [1024], F32) for k in range(4)]
        Bxt = [Buf("xt%d" % k) for k in range(4)]
        c_x = [newchan("c_x%d" % k) for k in range(4)]
        wmap = [(0, 512, 128), (128, 640, 128), (6 * 128, 896, 128), (7 * 128, 1152, 128)]
        wmap.append((2 * 128, 768, 128))
        wmap.append((3 * 128, 768, 128))
        for g in range(2):
            for dup in range(2):
                wmap.append(((4 + g) * 128 + 64 * dup, 1024 + 64 * g, 64))
        for c in range(8):
            k_ = c % 4
            dma("sp", XT[k_][:, 0:768], win_d[128 * c:128 * c + 128, 512:1280], c_x[k_], writes=[Bxt[k_]])
            for wi_, (dcol, scol, n) in enumerate(wmap):
                ts("dve", WKV[:, c, dcol:dcol + n], XT[k_][:, scol - 512:scol - 512 + n], NW[:, c:c + 1], ALU.mult,
                   reads=[Bxt[k_], Bc], writes=[Bwkv] if (c == 7 and wi_ == len(wmap) - 1) else [])

        OHS = carve(P3 + 0, [LEN_F], F32, 0, 33)
        FROW = carve(P3 + 20992, [LEN_F], BF16, 0, 8)
        Boh = Buf("oh")
        Brelb = Buf("relb")
        c_oh = newchan("c_oh")
        dma("sp", OHS, oh_d[:, :], c_oh, writes=[Boh])
        memset("dve", RELB[32:33, :], NEGBIG, reads=[Bc], writes=[Brelb])
        tt("dve", RELB[0:32, :], RELB[0:32, :], RB31, ALU.subtract, reads=[Bc, Brelb], writes=[Brelb])
        Bfrow = Buf("frow")
        c0 = 0
        k = 0
        while c0 < LEN_F:
            n = min(512, LEN_F - c0)
            bank = k % 2
            mm(ps_m[bank][0:8, 0:n], RELB[0:33, :], OHS[:, c0:c0 + n], True, True, reads=[Brelb, Boh], writes=[Bps[bank]])
            vcopy("dve", FROW[:, c0:c0 + n], ps_m[bank][0:8, 0:n], reads=[Bps[bank]], writes=[Bfrow])
            c0 += n
            k += 1
        Bfd = Buf("fd")
        dma("sp", fd_d[:, :], FROW, c_const, reads=[Bfrow], writes=[Bfd])
        if DEBUG == "0":
            dump("frow", FROW, BF16)
            dump("kaug", KAUG[:, :, 0:2048], BF16)
            finish()
            return nc, dbg

        if DEBUG == "w":
            dump("wkv", WKV, BF16)
            finish()
            return nc, dbg
        XN = [carve(P2 + 49152 + 2048 * k, [1024], BF16) for k in range(4)]
        JUNK = carve(P3 + 69760, [1024], BF16)
        Bxn = [Buf("xn%d" % k) for k in range(4)]
        Bss = [Buf("ss%d" % k) for k in range(8)]
        Bxntw = [Buf("xntw%d" % k) for k in range(2)]
        Bxown = [Buf("xown%d" % k) for k in range(4)]

        SLST = [carve(P3 + 67712 + 1024 * k, [512], BF16) for k in range(2)]
        Bsl = [Buf("sl%d" % k) for k in range(2)]
        c_sl = [newchan("c_sl%d" % k) for k in range(2)]

        def tile_dst(n):
            T = n // 4
            if T % 4 == 3:
                return XNTOWN[:, :, T // 4, 2:514], Bxown[T // 4]
            return XNTW[T % 2], Bxntw[T % 2]

        def stA(n):
            kx = n % 4
            ss = SMALL[:, (n % 8) * 2:(n % 8) * 2 + 1]
            rs = SMALL[:, (n % 8) * 2 + 1:(n % 8) * 2 + 2]
            dma("sp", XT[kx], x_d[128 * n:128 * n + 128, :], c_x[kx], writes=[Bxt[kx]])
            act(JUNK, XT[kx], AF.Square, reads=[Bxt[kx]], writes=[Bss[n % 8]], accum_out=ss)
            ts("dve", rs, ss, 1.0 / DM, ALU.mult, EPS, ALU.add, reads=[Bss[n % 8]], writes=[Bss[n % 8]])

        def stB(n):
            rs = SMALL[:, (n % 8) * 2 + 1:(n % 8) * 2 + 2]
            act(rs, rs, AF.Sqrt, reads=[Bss[n % 8]], writes=[Bss[n % 8]])
            S.op("dve", lambda e, rs=rs: e.reciprocal(out=rs, in_=rs), reads=[Bss[n % 8]], writes=[Bss[n % 8]])

        def stC(n):
            kx = n % 4
            sub = n % 4
            rs = SMALL[:, (n % 8) * 2 + 1:(n % 8) * 2 + 2]
            dstT, dstB = tile_dst(n)
            act(XN[kx], XT[kx], AF.Copy, reads=[Bxt[kx], Bss[n % 8]], writes=[Bxn[kx]], scale=rs)
            pb = 2 + (n % 2)
            pst = ps_m[pb][:, 0:512].bitcast(BF16)
            for c in range(8):
                tr(pst[:, c * 128:(c + 1) * 128], XN[kx][:, c * 128:(c + 1) * 128], reads=[Bxn[kx], Bc], writes=[Bps[pb]])
            vcopy("dve", dstT[:, :, sub * 128:(sub + 1) * 128], pst.rearrange("p (c n) -> p c n", c=8),
                  reads=[Bps[pb]] + ([Bfd] if (n // 4) % 4 == 3 else []), writes=[dstB])

        def projA(T):
            XT_T, BT = tile_dst(4 * T)
            if T % 4 == 2:
                vcopy("pool", XNTOWN[:, :, T // 4, 0:2], XT_T[:, :, 510:512], reads=[BT, Bfd], writes=[Bxown[T // 4]])
            cols = slice(512 * T, 512 * T + 512)
            kcv = KCMPT.rearrange("p (r c) -> p r c", r=16)[:, :, 32 * T:32 * T + 32]
            vcv = VCMPT.rearrange("p (r c) -> p r c", r=16)[:, :, 32 * T:32 * T + 32]
            fm = [(0, ("cm", kcv), 128), (1, ("cm", vcv), 128), (2, None, 128)]
            if T % 4 >= 2:
                w0_ = 512 * (2 * (T // 4) + (T % 4 - 2))
                fm += [(4, KWIN[:, 0, w0_:w0_ + 512], 128), (5, KWIN[:, 1, w0_:w0_ + 512], 128)]
            for ii, (blk, dst, m) in enumerate(fm):
                pb = ii % 4
                pso = ps_sc[pb // 2][0:m, (pb % 2) * 512:(pb % 2) * 512 + 512]
                for c in range(8):
                    mm(pso, WKV[:, c, blk * 128:blk * 128 + m], XT_T[:, c, :], c == 0, c == 7,
                       reads=[BT, Bwkv], writes=[Bsc4[pb]])
                if isinstance(dst, tuple):
                    vcopy("dve", dst[1], pso.rearrange("p (c r) -> p r c", r=16), reads=[Bsc4[pb]])
                elif dst is not None:
                    vcopy("dve", dst, pso, reads=[Bsc4[pb]])
                else:
                    ks = T % 2
                    vcopy("dve", KAUG[0:64, 0, cols], pso[0:64, :], reads=[Bsc4[pb]])
                    vcopy("dve", SLST[ks][64:128, :], pso[64:128, :], reads=[Bsc4[pb]], writes=[Bsl[ks]])
                    dma("pool", KAUG[0:64, 1, cols], SLST[ks][64:128, :], c_sl[ks], reads=[Bsl[ks]])
            for sub in range(4):
                kt = 4 * T + sub
                pb = sub % 2
                pso = ps_m[pb][:, 0:256]
                for c in range(8):
                    mm(pso, XT_T[:, c, sub * 128:(sub + 1) * 128], WKV[:, c, 768:1024], c == 0, c == 7,
                       reads=[BT, Bwkv], writes=[Bps[pb]])
                vcopy("dve", VS[:, kt, :, 0:64], pso[:, 0:128].rearrange("p (g d) -> p g d", g=2), reads=[Bps[pb]])
                if T % 4 >= 2:
                    slot = 8 * (T // 4) + 4 * (T % 4 - 2) + sub
                    vcopy("dve", VW[:, slot, :, 0:64], pso[:, 128:256].rearrange("p (g d) -> p g d", g=2),
                          reads=[Bps[pb]])

        for n in range(4):
            stA(n)
            stB(n)
        for T in range(16):
            for n in range(4 * T, 4 * T + 4):
                stC(n)
            if T + 1 < 16:
                for n in range(4 * T + 4, 4 * T + 8):
                    stA(n)
                    stB(n)
            projA(T)
        W1SX = carve(P2 + 32768, [32, 128], F32)
        W2S = carve(P3 + 66176, [2, 64], F32)
        PETS = carve(P3 + 66816, [2, 32], F32)
        c_a2 = newchan("c_a2")
        c_a2b = newchan("c_a2b")
        Bw1s = Buf("w1s")
        Bw1sx = Buf("w1sx")
        for hf in range(2):
            dma("sp", W1SX[64 * hf:64 * hf + 64, :, :], w1k_d[:, :, :], c_a2b, writes=Bxt + [Bw1sx])
        for kv, (w2d, ped) in enumerate(((w2k_d, pek_d), (w2v_d, pev_d))):
            for hf in range(2):
                dma("sp", PETS[64 * hf:64 * hf + 64, kv, :], ped[:, :], c_a2b, writes=[Bw1sx])
            dma("sp", W2S[:, kv, :], w2d[:, :], c_a2b, writes=[Bw1sx])
        barrier()
        if DEBUG == "A1":
            dump("kaug", KAUG[:, :, 0:2048], BF16)
            dump("vs", VS, BF16)
            dump("kwin", KWIN, BF16)
            dump("vw", VW, BF16)
            finish()
            return nc, dbg

        W1 = [carve(P3 + 32896 + 8192 * k, [32, 128], BF16) for k in range(2)]
        W1S = carve(P3 + 49280, [32, 128], F32)
        W2 = carve(P3 + 65664, [2, 128], BF16)
        PET = carve(P3 + 66688, [2, 32], BF16)
        HSIL = [carve(P3 + 67072 + 1024 * k, [512], BF16) for k in range(2)]
        Bw1 = Buf("w1")
        Bhs = [Buf("hs0"), Buf("hs1")]
        for hf in range(2):
            dma("sp", W1S[64 * hf:64 * hf + 64, :, :], w1v_d[:, :, :], c_a2, writes=[Bw1s])
        for kv in range(2):
            stg, Bstg = (W1SX, Bw1sx) if kv == 0 else (W1S, Bw1s)
            vcopy("dve", W1[kv], stg, reads=[Bstg], writes=[Bw1])
            vcopy("pool", PET[:, kv, :], PETS[:, kv, :], reads=[Bw1sx], writes=[Bw1])
            vcopy("pool", W2[:, kv, 0:64], W2S[:, kv, :], reads=[Bw1sx], writes=[Bw1])
            vcopy("pool", W2[:, kv, 64:128], W2S[:, kv, :], reads=[Bw1sx], writes=[Bw1])
            for j in range(32):
                mm(ps_m[2][:, 0:1], W1[kv][0:64, j, :], PET[0:64, kv, j:j + 1], j == 0, j == 31, reads=[Bw1], writes=[Bps[2]])
            vcopy("dve", HIDB[:, kv:kv + 1], ps_m[2][:, 0:1], reads=[Bps[2]], writes=[Bw1])
        Bvc = Buf("vc")
        memset("pool", HSIL[0][:, 511:512], 0.0, writes=[Bhs[0]])
        memset("pool", HSIL[1][:, 511:512], 0.0, writes=[Bhs[1]])
        for g in range(2):
            memset("pool", KCT[:, g, 511:512], 0.0)
            for kv in range(2):
                src = (KCMPT if kv == 0 else VCMPT)
                pr = slice(64 * g, 64 * g + 64)
                hid = ps_sc[kv][:, 0:511]
                srcv = src[pr, :].rearrange("p (r c) -> p r c", r=16)
                for j in range(32):
                    mm(hid, W1[kv][pr, j, :], srcv[:, j % 16, j // 16:j // 16 + 511], j == 0, j == 31,
                       reads=[Bw1], writes=[Bsc[kv]])
                act(HSIL[kv][:, 0:511], hid, AF.Silu, reads=[Bsc[kv], Bw1], writes=[Bhs[kv]], bias=HIDB[:, kv:kv + 1])
                if kv == 0:
                    mm(ps_m[0][:, 0:511], W2[:, 0, :], HSIL[0][:, 0:511], True, True, reads=[Bhs[0], Bw1], writes=[Bps[0]])
                    vcopy("dve", KCT[:, g, 0:511], ps_m[0][:, 0:511], reads=[Bps[0]])
                else:
                    for ct in range(4):
                        mm(ps_m[1][:, ct * 64:(ct + 1) * 64], HSIL[1][:, ct * 128:(ct + 1) * 128], W2[:, 1, 0:64], True, True,
                           reads=[Bhs[1], Bw1], writes=[Bps[1]])
                    vcopy("dve", VC[:, :, g, 0:64], ps_m[1][:, 0:256].rearrange("p (c d) -> p c d", c=4),
                          reads=[Bps[1]], writes=[Bvc])
                    for ct in range(4):
                        ts("dve", VC[:, ct, g, 0:64], VC[:, ct, g, 0:64], CVC[:, ct:ct + 1], ALU.mult,
                           reads=[Bvc, Bc], writes=[Bvc])
        barrier()
        if DEBUG == "A":
            dump("kaug", KAUG[:, :, 0:2048], BF16)
            dump("vs", VS, BF16)
            dump("kwin", KWIN, BF16)
            dump("vw", VW, BF16)
            dump("kct", KCT, BF16)
            dump("vc", VC, BF16)
            finish()
            return nc, dbg

        WSTG2 = [carve(P3 + 32896 + 4096 * k, [8, 128], F32) for k in range(2)]
        WG = [carve(P3 + 41088 + 2048 * k, [8, 128], BF16) for k in range(6)]
        HSB = carve(P3 + 53376, [514], F32)
        UU = carve(P3 + 55440, [514], F32)
        YY = carve(P3 + 57504, [512], F32)
        SZC = carve(P3 + 59552, [512], F32)
        BS = carve(P3 + 61600, [512], F32)
        Bwg = [Buf("wg%d" % k) for k in range(6)]
        Bout = Buf("projout")
        Btmp = {n: Buf(n) for n in ("hsb", "uu", "yy", "szc", "bs")}
        wgate_v = wgate_d.rearrange("(c p) n -> p c n", p=128)

        def own_cols(s):
            return XNTOWN[:, :, s, 2:514]

        def proj4(wg, nm, dst_fn):
            for s in range(4):
                pb = s % 4
                pso = ps_sc[pb // 2][0:nm, (pb % 2) * 512:(pb % 2) * 512 + 512]
                for c in range(8):
                    mm(pso, WG[wg][:, c, 0:nm], XNTOWN[:, c, s, 2:514], c == 0, c == 7, reads=[Bwg[wg]], writes=[Bsc4[pb]])
                dst_fn(s, pso, Bsc4[pb])

        wgi = 0
        for qb in range(4):
            w_ = wgi % 6
            wgi += 1
            load_w_block(lambda c, w_=w_: WG[w_][:, c, :], win_v, 128 * qb, 128, WSTG2, Bwg[w_])
            proj4(w_, 128, lambda s, pso, B_, qb=qb: act(QTP[:, qb, s * 512:(s + 1) * 512], pso, AF.Copy,
                                                       reads=[B_], scale=0.125))
        for qb in range(4):
            w_ = wgi % 6
            wgi += 1
            load_w_block(lambda c, w_=w_: WG[w_][:, c, :], win_v, 1304 + 128 * qb, 128, WSTG2, Bwg[w_])
            proj4(w_, 128, lambda s, pso, B_, qb=qb: act(SZP[:, qb, s * 512:(s + 1) * 512], pso, AF.Silu,
                                                       reads=[B_]))
        w_ = wgi % 6
        wgi += 1
        load_w_block(lambda c, w_=w_: WG[w_][:, c, 0:24], wgate_v, 0, 24, WSTG2, Bwg[w_])
        proj4(w_, 24, lambda s, pso, B_: act(G24[:, s * 512:(s + 1) * 512], pso, AF.Sigmoid, reads=[B_]))
        for ch in range(4):
            ws = []
            for (nm_, c0_) in (("h", 1816), ("b", 2328), ("c", 2840), ("z", 3352)):
                w_ = wgi % 6
                wgi += 1
                load_w_block(lambda c, w_=w_: WG[w_][:, c, :], win_v, c0_ + 128 * ch, 128, WSTG2, Bwg[w_])
                ws.append(w_)
            wh, wb, wc_, wz = ws
            for s in range(4):
                qcols = slice(s * 512, (s + 1) * 512)
                ps_h = ps_sc[0][:, 0:512]
                ps_c = ps_sc[0][:, 512:1024]
                ps_z = ps_sc[1][:, 0:512]
                ps_b = ps_sc[1][:, 512:1024]
                ps_h2 = ps_m[0][:, 0:2]
                ps_c2 = ps_m[1][:, 0:2]
                for c in range(8):
                    mm(ps_h, WG[wh][:, c, :], XNTOWN[:, c, s, 2:514], c == 0, c == 7, reads=[Bwg[wh]], writes=[Bsc4[0]])
                for c in range(8):
                    mm(ps_h2, WG[wh][:, c, :], XNTOWN[:, c, s, 0:2], c == 0, c == 7, reads=[Bwg[wh]], writes=[Bps[0]])
                for c in range(8):
                    mm(ps_c, WG[wc_][:, c, :], XNTOWN[:, c, s, 2:514], c == 0, c == 7, reads=[Bwg[wc_]], writes=[Bsc4[1]])
                for c in range(8):
                    mm(ps_c2, WG[wc_][:, c, :], XNTOWN[:, c, s, 0:2], c == 0, c == 7, reads=[Bwg[wc_]], writes=[Bps[1]])
                for c in range(8):
                    mm(ps_z, WG[wz][:, c, :], XNTOWN[:, c, s, 2:514], c == 0, c == 7, reads=[Bwg[wz]], writes=[Bsc4[2]])
                for c in range(8):
                    mm(ps_b, WG[wb][:, c, :], XNTOWN[:, c, s, 2:514], c == 0, c == 7, reads=[Bwg[wb]], writes=[Bsc4[3]])
                act(HSB[:, 2:514], ps_h, AF.Copy, reads=[Bsc4[0]], writes=[Btmp["hsb"]])
                vcopy("dve", HSB[:, 0:2], ps_h2, reads=[Bps[0]], writes=[Btmp["hsb"]])
                tt("dve", UU[:, 2:514], ps_c, HSB[:, 2:514], ALU.mult, reads=[Bsc4[1], Btmp["hsb"]], writes=[Btmp["uu"]])
                tt("dve", UU[:, 0:2], ps_c2, HSB[:, 0:2], ALU.mult, reads=[Bps[1], Btmp["hsb"]], writes=[Btmp["uu"]])
                ts("dve", YY, UU[:, 2:514], CW[:, 3 * ch + 2:3 * ch + 3], ALU.mult, reads=[Btmp["uu"], Bc], writes=[Btmp["yy"]])
                stt("dve", YY, UU[:, 1:513], CW[:, 3 * ch + 1:3 * ch + 2], YY, ALU.mult, ALU.add,
                    reads=[Btmp["uu"], Btmp["yy"], Bc], writes=[Btmp["yy"]])
                stt("dve", YY, UU[:, 0:512], CW[:, 3 * ch:3 * ch + 1], YY, ALU.mult, ALU.add,
                    reads=[Btmp["uu"], Btmp["yy"], Bc], writes=[Btmp["yy"]])
                act(SZC, ps_z, AF.Silu, reads=[Bsc4[2]], writes=[Btmp["szc"]])
                tt("dve", BS, ps_b, YY, ALU.mult, reads=[Bsc4[3], Btmp["yy"]], writes=[Btmp["bs"]])
                tt("pool", MC[:, ch, qcols], BS, SZC, ALU.mult, reads=[Btmp["bs"], Btmp["szc"]])
        barrier()
        if DEBUG == "B":
            dump("qtp", QTP, BF16)
            dump("szp", SZP, BF16)
            dump("mc", MC, BF16)
            dump("g24", G24, F32)
            finish()
            return nc, dbg

        def c3(off, shape, dt, p0=0, p1=128):
            return carve(P3 + off, shape, dt, p0, p1)
        AC = [c3(4096 * k, [4, 512], BF16, 0, 64) for k in range(2)]
        MT0 = [c3(1024 * k, [512], BF16, 64, 128) for k in range(2)]
        MT1 = [c3(2048 + 1024 * k, [512], BF16, 64, 128) for k in range(2)]
        ZSW1 = c3(4096, [512], F32, 64, 65)
        CF1H = c3(6144, [512], BF16, 64, 65)
        CF1L = c3(7168, [512], BF16, 64, 65)
        X0 = [c3(8192 + 1024 * k, [512], BF16) for k in range(2)]
        X1 = [c3(10240 + 1024 * k, [512], BF16) for k in range(2)]
        HS = [c3(12288 + 2048 * k, [1024], BF16) for k in range(2)]
        HW = [c3(16384 + 2816 * k, [1408], BF16) for k in range(2)]
        HC = c3(22016, [8, 512], BF16)
        IMP = c3(30208, [4, 128], F32)
        VCM = c3(32256, [4, 128], F32)
        ADDM = c3(34304, [4, 128], F32)
        PB = [c3(65728 + 2048 * k, [1024], BF16) for k in range(3)]
        OSB_S = c3(40448, [512], F32, 0, 67)
        OSB_W = c3(42496, [512], F32, 0, 67)
        OCSB = c3(44544, [512], F32, 0, 67)
        COEF = c3(46592, [512], F32, 64, 67)
        ZSW = c3(48640, [512], F32, 64, 67)
        SZO = [c3(50688 + 2048 * k, [512], BF16, 0, 64) for k in range(2)]
        MXO = [c3(50688 + 2048 * k + 1024, [512], BF16, 0, 64) for k in range(2)]
        GST = [c3(50688 + 2048 * k, [512], F32, 64, 67) for k in range(2)]
        GST1 = c3(54784, [512], F32, 64, 65)
        SCORE = c3(56832, [4, 128], F32)
        WORK = c3(58880, [4, 128], F32)
        TOP = c3(60928, [4, 16], F32)
        MNB_N = c3(61184, [4, 128], BF16)
        MNB_S = c3(62208, [4, 128], BF16)
        RZT = c3(63232, [4], F32)
        COEFH = c3(63296, [512], BF16, 64, 67)
        COEFL = c3(64320, [512], BF16, 64, 67)
        SELMB = c3(65344, [3, 64], BF16, 0, 67)

        SC = [ps_sc[0], ps_sc[1]]
        OCP = ps_m[2][0:67, :]
        IMPP = ps_m[3][:, :]
        Bsc2 = [Buf("sc0"), Buf("sc1")]
        Bocp = Buf("ocp")
        Bimpp = Buf("impp")

        Bcc = Buf("cconst")
        c_c = newchan("c_cconst")
        vcopy("pool", SELMB, SELM, reads=[Bc], writes=[Bcc])
        Bhc = Buf("hc")
        c_hc = newchan("c_hc")
        for h in range(8):
            dma("sp", HC[:, h, :], bass.AP(fd_d.tensor, h * LEN_F + OFF_C, [[16, 128], [1, 512]]), c_hc, writes=[Bhc])

        Bp = [Buf("p%d" % k) for k in range(3)]
        Bx = [Buf("x%d" % k) for k in range(2)]
        Bhsw = [Buf("hsw%d" % k) for k in range(2)]
        Bimp = Buf("imp")
        Bmask = Buf("mask")
        Bmt = [Buf("mt0"), Buf("mt1")]
        Bac = [[Buf("ac%d_%d" % (p_, k)) for k in range(4)] for p_ in range(2)]
        Bgst = [Buf("gst%d" % k) for k in range(2)]
        Bgst1 = Buf("gst1")
        Bszo = [Buf("szo%d" % k) for k in range(2)]
        Bmxo = [Buf("mxo%d" % k) for k in range(2)]
        Bosb = {n: Buf(n) for n in ("s", "w", "c", "coef", "coefh", "coefl", "zsw", "zsw1", "cf1h", "cf1l",
                                    "score", "work", "top", "mnb", "mnbs", "rzt")}
        Bszp = [Buf("szp%d" % k) for k in range(8)]
        c_tab = [newchan("c_tab%d" % k) for k in range(2)]
        c_mask = newchan("c_mask")
        c_gst = [newchan("c_gst%d" % k) for k in range(2)]
        c_gst1 = newchan("c_gstone")
        c_x2 = [newchan("c_xq%d" % k) for k in range(2)]
        c_szo = [newchan("c_szo%d" % k) for k in range(2)]
        c_mxo = [newchan("c_mxo%d" % k) for k in range(2)]

        class Pipe:
            def __init__(self):
                self.q = []
                self.tick = 0

            def _run_due(self):
                due = [f for (t_, f) in self.q if t_ <= self.tick]
                self.q = [(t_, f) for (t_, f) in self.q if t_ > self.tick]
                for f in due:
                    f()

            def push(self, qk_fn, later=()):
                self.tick += 1
                qk_fn()
                self._run_due()
                for (d_, f) in later:
                    self.q.append((self.tick + d_, f))

            def flush(self):
                while self.q:
                    self.tick += 1
                    self._run_due()

        pipe = Pipe()
        gctr = [0]

        pctr = [0]

        def push_group(it):
            grp = it["group"]
            n = gctr[0]
            gctr[0] += 1
            sc = SC[n % 2]
            Bs_ = Bsc2[n % 2]
            pn = pctr[0]
            pctr[0] += 1
            P_ = PB[pn % 3]
            Bp_ = Bp[pn % 3]
            h = grp[0]["h"]
            width = 512 * len(grp)

            rng = [tl.get("cols", (0, 512)) for tl in grp]
            if len(grp) == 2:
                assert rng[0][1] == 512 and rng[1][0] == 0, rng
            e0 = rng[0][0]
            e1 = 512 * (len(grp) - 1) + rng[-1][1]

            def qk():
                for idx, tl in enumerate(grp):
                    lo, hi = rng[idx]
                    o = sc[:, idx * 512 + lo:idx * 512 + hi]
                    nm = len(tl["mms"])
                    for mi, (l_, r_) in enumerate(tl["mms"]):
                        mm(o, l_, r_[:, lo:hi], mi == 0, mi == nm - 1, reads=tl["reads"], writes=[Bs_])
                act(P_[:, e0:e1], sc[:, e0:e1], AF.Exp, reads=[Bs_, Bc], writes=[Bp_], bias=B31[:, h:h + 1])

            def pv():
                for idx, tl in enumerate(grp):
                    lo, hi = rng[idx]
                    tl["pv"](P_[:, idx * 512 + lo:idx * 512 + hi], Bp_, lo, hi)
            pipe.push(qk, [(1, pv)] + list(it.get("after", [])))

        def special(fn):
            n = gctr[0]
            gctr[0] += 1
            fn(SC[n % 2], Bsc2[n % 2])

        def to_groups(tiles, after):
            out = [dict(group=tiles[i:i + 2]) for i in range(0, len(tiles), 2)]
            out[-1]["after"] = after
            return out

        def run_seq(seq):
            for it in seq:
                if "call" in it:
                    it["call"]()
                else:
                    push_group(it)

        def merge(seq3, inserts):
            out = []
            ti = 0
            inserts = sorted(inserts, key=lambda x: x[0])
            ii = 0
            for it in seq3:
                while ii < len(inserts) and "call" not in it and inserts[ii][0] <= ti:
                    out.extend(inserts[ii][1])
                    ii += 1
                out.append(it)
                if "call" not in it:
                    ti += 1
            while ii < len(inserts):
                out.extend(inserts[ii][1])
                ii += 1
            return out

        insts = [(g, s) for g in range(2) for s in range(4)]
        hctr = [0]

        def stage1_head(idx, r, delays=(1, 3, 8), gst=None):
            GST1_, Bgst1_, c_gst1_ = gst if gst is not None else (GST1, Bgst1, c_gst1)
            g, s = insts[idx]
            p_ = idx % 2
            h = 4 * g + r
            q = h // 2
            pr = slice(64 * (h % 2), 64 * (h % 2) + 64)
            qcols = slice(s * 512, (s + 1) * 512)
            QT_h = QTP[pr, q, qcols]
            items = [dict(call=lambda: dma("sp", GST1_, G24[3 * h:3 * h + 1, qcols], c_gst1_, writes=[Bgst1_]))]
            tiles = []
            pend = []
            for ct in range(s + 1):
                mms = [(KCT[pr, g, ct * 128:(ct + 1) * 128], QT_h)]
                if ct == s:
                    mms.append((ANTI, HC[:, h, :]))

                def pvf(Ph, Bp_, lo, hi, ct=ct):
                    mm(OCP, VC[:, ct, g, 0:67], Ph, ct == 0, ct == s, reads=[Bp_], writes=[Bocp])
                    pend.append((ct, Ph, Bp_))
                    if ct == s:
                        for u in range(4):
                            for (c2, Ph2, Bp2) in pend:
                                mm(IMPP[:, u * 128:(u + 1) * 128], Ph2[:, u * 128:(u + 1) * 128], OV[:, c2, :],
                                   c2 == 0, c2 == s, reads=[Bp2], writes=[Bimpp])
                tiles.append(dict(mms=mms, reads=[Bhc], pv=pvf, h=h))

            def fin1():
                IMPRAW = WORK.rearrange("p u j -> p (u j)")
                vcopy("dve", OCSB, OCP, reads=[Bocp], writes=[Bosb["c"]])
                vcopy("dve", IMPRAW, IMPP, reads=[Bimpp], writes=[Bosb["work"]])
                ts("dve", RZT, WORK[:, :, 127], 1e-30, ALU.max, reads=[Bosb["work"]], writes=[Bosb["rzt"]])
                ts("dve", ZSW1, OCSB[64:65, :], 1e-30, ALU.max, reads=[Bosb["c"]], writes=[Bosb["zsw1"]])
                S.op("dve", lambda e: e.reciprocal(out=RZT, in_=RZT), reads=[Bosb["rzt"]], writes=[Bosb["rzt"]])
                for u in range(4):
                    if r == 0:
                        ts("dve", IMP[:, u, :], WORK[:, u, :], RZT[:, u:u + 1], ALU.mult,
                           reads=[Bosb["work"], Bosb["rzt"]], writes=[Bimp])
                    else:
                        stt("dve", IMP[:, u, :], WORK[:, u, :], RZT[:, u:u + 1], IMP[:, u, :],
                            ALU.mult, ALU.add, reads=[Bosb["work"], Bosb["rzt"], Bimp], writes=[Bimp])

            def fin1b():
                act(ZSW1, ZSW1, AF.Ln, reads=[Bosb["zsw1"]], writes=[Bosb["zsw1"]])
                act(ZSW1, ZSW1, AF.Exp, reads=[Bosb["zsw1"]], writes=[Bosb["zsw1"]], scale=-1.0)
                tt("dve", ZSW1, ZSW1, GST1_, ALU.mult, reads=[Bosb["zsw1"], Bgst1_], writes=[Bosb["zsw1"]])
                vcopy("dve", CF1H, ZSW1, reads=[Bosb["zsw1"]], writes=[Bosb["cf1h"]])
                tt("dve", CF1L, ZSW1, CF1H, ALU.subtract, reads=[Bosb["zsw1"], Bosb["cf1h"]], writes=[Bosb["cf1l"]])

            def fin2():
                bcp = ps_m[3][0:64, :]
                mm(bcp, SELMB[64:65, 0, :], CF1H, True, False, reads=[Bosb["cf1h"], Bcc], writes=[Bimpp])
                mm(bcp, SELMB[64:65, 0, :], CF1L, False, True, reads=[Bosb["cf1l"], Bcc], writes=[Bimpp])
                tt("dve", AC[p_][:, r, :], OCSB[0:64, :], bcp, ALU.mult,
                   reads=[Bosb["c"], Bimpp], writes=[Bac[p_][r]])
            items += to_groups(tiles, [(delays[0], fin1), (delays[1], fin1b), (delays[2], fin2)])
            return items

        def stage2_dve(idx):
            g, s = insts[idx]
            dma("sp", VCM, vcm_d[s, :, :, :], c_mask, writes=[Bmask])
            dma("sp", ADDM, addm_d[s, :, :, :], c_mask, writes=[Bmask])
            tt("dve", SCORE, IMP, VCM, ALU.mult, reads=[Bimp, Bmask], writes=[Bosb["score"]])
            tt("dve", SCORE, SCORE, ADDM, ALU.add, reads=[Bosb["score"], Bmask], writes=[Bosb["score"]])
            Btop = [Buf("top%d" % u) for u in range(4)]
            Bwk = [Buf("wk%d" % u) for u in range(4)]
            for u in range(4):
                S.op("dve", lambda e, u=u: e.max(out=TOP[:, u, 0:8], in_=SCORE[:, u, :]),
                     reads=[Bosb["score"], Bosb["work"], Bosb["top"]], writes=[Btop[u]])
            for u in range(4):
                S.op("dve", lambda e, u=u: e.match_replace(out=WORK[:, u, :], in_to_replace=TOP[:, u, 0:8],
                                                         in_values=SCORE[:, u, :], imm_value=-1e30),
                     reads=[Bosb["score"], Btop[u]], writes=[Bwk[u]])
            for u in range(4):
                S.op("dve", lambda e, u=u: e.max(out=TOP[:, u, 8:16], in_=WORK[:, u, :]),
                     reads=[Bwk[u]], writes=[Btop[u]])
            for u in range(4):
                ts("dve", WORK[:, u, :], SCORE[:, u, :], TOP[:, u, 15:16], ALU.is_ge,
                   reads=[Bosb["score"], Btop[u], Bwk[u]], writes=[Bwk[u]])
            tt("dve", WORK, WORK, VCM, ALU.mult, reads=Bwk + [Bosb["work"], Bmask], writes=Bwk + [Bosb["work"], Bosb["top"]])
            ts("dve", MNB_N, WORK, -1.0, ALU.add, -NEGBIG, ALU.mult, reads=[Bosb["work"]], writes=[Bosb["mnb"]])
            ts("dve", MNB_S[:, :, 0:64], WORK[:, :, 64:128], -1.0, ALU.add, -NEGBIG, ALU.mult,
               reads=[Bosb["work"]], writes=[Bosb["mnbs"]])
            ts("dve", MNB_S[:, :, 64:128], WORK[:, :, 0:64], -1.0, ALU.add, -NEGBIG, ALU.mult,
               reads=[Bosb["work"]], writes=[Bosb["mnbs"]])

        def stage2_pe(idx):
            p_ = idx % 2
            pst = ps_m[2][:, 0:512].bitcast(BF16)
            for u in range(4):
                tr(pst[:, u * 128:(u + 1) * 128], MNB_N[:, u, :], reads=[Bosb["mnb"], Bc], writes=[Bocp])
                tr(pst[:, 512 + u * 128:512 + (u + 1) * 128], MNB_S[:, u, :], reads=[Bosb["mnbs"], Bc], writes=[Bocp])
            vcopy("dve", MT1[p_], pst[64:128, 0:512], reads=[Bocp], writes=[Bmt[p_]])
            vcopy("dve", MT0[p_], pst[64:128, 512:1024], reads=[Bocp], writes=[Bmt[p_]])

        def stage3_head(idx, r):
            g, s = insts[idx]
            p_ = idx % 2
            T = 3 + 4 * s
            h = 4 * g + r
            q = h // 2
            half = h % 2
            pr = slice(64 * half, 64 * half + 64)
            qcols = slice(s * 512, (s + 1) * 512)
            k_ = hctr[0] % 2
            hctr[0] += 1
            QT_h = QTP[pr, q, qcols]

            def setup():
                dma("sp", GST[k_], G24[3 * h:3 * h + 3, qcols], c_gst[k_], writes=[Bgst[k_]])
                dma("sp", HS[k_], bass.AP(fd_d.tensor, h * LEN_F, [[1, 128], [1, 1024]]), c_tab[k_], writes=[Bhsw[k_]])
                dma("sp", HW[k_], bass.AP(fd_d.tensor, h * LEN_F + OFF_W, [[1, 128], [1, 1408]]), c_tab[k_], writes=[Bhsw[k_]])
                if half == 0:
                    vcopy("dve", X0[k_][0:64, :], QTP[0:64, q, qcols], writes=[Bx[k_]])
                    vcopy("dve", X1[k_][0:64, :], QTP[0:64, q, qcols], writes=[Bx[k_]])
                else:
                    dma("pool", X0[k_][0:64, :], QTP[64:128, q, qcols], c_x2[k_], writes=[Bx[k_]])
                    dma("pool", X1[k_][0:64, :], QTP[64:128, q, qcols], c_x2[k_], writes=[Bx[k_]])
                    dma("pool", SZO[k_], SZP[64:128, q, qcols], c_szo[k_], reads=[Bszp[h]], writes=[Bszo[k_]])
                vcopy("dve", X0[k_][64:128, :], MT0[p_], reads=[Bmt[p_]], writes=[Bx[k_]])
                vcopy("dve", X1[k_][64:128, :], MT1[p_], reads=[Bmt[p_]], writes=[Bx[k_]])
            items = [dict(call=setup)]
            nsel = 16 * (s + 1)

            def sel_tile(kt, first, last):
                Xs = X0[k_] if kt < 32 else X1[k_]
                mms = [(KAUG[:, g, kt * 128:(kt + 1) * 128], Xs)]
                delta = 128 * kt - 512 * T
                cols = (0, 512)
                if delta >= -128:
                    col0 = 384 - delta
                    mms.append((ANTI, HS[k_][:, col0:col0 + 512]))
                    if delta > 0:
                        cols = (delta, 512)

                def pvf(Ph, Bp_, lo, hi, kt=kt):
                    mm(ps_m[0][0:67, lo:hi], VS[:, kt, g, 0:67], Ph, first, last, reads=[Bp_], writes=[Bps[0]])
                return dict(mms=mms, reads=[Bx[k_], Bhsw[k_]], pv=pvf, h=h, cols=cols)

            def win_tile(delta, first, last):
                kt = (512 * T + delta) // 128
                slot = 8 * s + (kt - (4 * T - 4))
                col0 = 384 - delta
                mms = [(KWIN[pr, g, slot * 128:(slot + 1) * 128], QT_h), (ANTI, HW[k_][:, col0:col0 + 512])]
                lo = max(0, delta)
                hi = min(512, ((delta + 638) // 128 + 1) * 128)
                hi = min(512, max(hi, 128))

                def pvf(Ph, Bp_, lo_, hi_, slot=slot):
                    mm(ps_m[1][0:67, lo_:hi_], VW[:, slot, g, 0:67], Ph, first, last, reads=[Bp_], writes=[Bps[1]])
                return dict(mms=mms, reads=[Bhsw[k_]], pv=pvf, h=h, cols=(lo, hi))

            far = list(range(0, 4 * T - 1))
            order = [far[0], far[1], 4 * T + 1, far[2], 4 * T + 2, far[3], 4 * T + 3, far[4], 4 * T - 1, 4 * T] + far[5:]
            assert len(order) == nsel and order[-1] in far and len(set(order)) == nsel
            tiles = [sel_tile(kt, i_ == 0, i_ == nsel - 1) for i_, kt in enumerate(order)]
            worder = [0, -256, 256, -384, 384, -512, 128, -128]
            tiles += [win_tile(d_, i_ == 0, i_ == 7) for i_, d_ in enumerate(worder)]

            def fin1():
                vcopy("dve", OSB_S, ps_m[0][0:67, :], reads=[Bps[0]], writes=[Bosb["s"]])
                vcopy("dve", OSB_W, ps_m[1][0:67, :], reads=[Bps[1]], writes=[Bosb["w"]])
                tt("dve", ZSW, OSB_S[64:67, :], OSB_W[64:67, :], ALU.add, reads=[Bosb["s"], Bosb["w"]], writes=[Bosb["zsw"]])
                S.op("dve", lambda e: e.reciprocal(out=ZSW, in_=ZSW), reads=[Bosb["zsw"]], writes=[Bosb["zsw"]])
                tt("dve", COEF, ZSW, GST[k_], ALU.mult, reads=[Bosb["zsw"], Bgst[k_]], writes=[Bosb["coef"]])
                vcopy("dve", COEFH, COEF, reads=[Bosb["coef"]], writes=[Bosb["coefh"]])
                tt("dve", COEFL, COEF, COEFH, ALU.subtract, reads=[Bosb["coef"], Bosb["coefh"]], writes=[Bosb["coefl"]])

            def bc_mm(br):
                bcp = ps_m[3][0:64, :]
                mm(bcp, SELMB[64:67, br, :], COEFH, True, False, reads=[Bosb["coefh"], Bcc], writes=[Bimpp])
                mm(bcp, SELMB[64:67, br, :], COEFL, False, True, reads=[Bosb["coefl"], Bcc], writes=[Bimpp])

            def fin2():
                bc_mm(1)
                tt("dve", OSB_S[0:64, :], OSB_S[0:64, :], ps_m[3][0:64, :], ALU.mult,
                   reads=[Bosb["s"], Bimpp], writes=[Bosb["s"]])

            def fin3():
                bc_mm(2)
                tt("dve", OSB_W[0:64, :], OSB_W[0:64, :], ps_m[3][0:64, :], ALU.mult,
                   reads=[Bosb["w"], Bimpp], writes=[Bosb["w"]])
                tt("dve", OSB_S[0:64, :], OSB_S[0:64, :], OSB_W[0:64, :], ALU.add,
                   reads=[Bosb["s"], Bosb["w"]], writes=[Bosb["s"]])
                tt("dve", OSB_S[0:64, :], OSB_S[0:64, :], AC[p_][:, r, :], ALU.add,
                   reads=[Bosb["s"], Bac[p_][r]], writes=[Bosb["s"]])
                if half == 0:
                    tt("dve", SZP[0:64, q, qcols], OSB_S[0:64, :], SZP[0:64, q, qcols], ALU.mult,
                       reads=[Bosb["s"], Bszp[h]], writes=[Bszp[h]])
                else:
                    tt("dve", MXO[k_], OSB_S[0:64, :], SZO[k_], ALU.mult, reads=[Bosb["s"], Bszo[k_]], writes=[Bmxo[k_]])
                    dma("pool", SZP[64:128, q, qcols], MXO[k_], c_mxo[k_], reads=[Bmxo[k_]], writes=[Bszp[h]])
            items += to_groups(tiles, [(1, fin1), (9, fin2), (11, fin3)])
            return items

        seq = []
        for r in range(4):
            seq += stage1_head(0, r, delays=(1, 1, 2), gst=(GST[r % 2][0:1, :], Bgst[r % 2], c_gst[r % 2]))
        run_seq(seq)
        pipe.flush()
        stage2_dve(0)
        stage2_pe(0)
        for idx in range(8):
            seq3 = []
            for r in range(4):
                seq3 += stage3_head(idx, r)
            if idx + 1 < 8:
                n3 = sum(1 for it in seq3 if "call" not in it)
                ins = []
                step = max(11, int(0.12 * n3))
                for r in range(4):
                    ins.append((1 + r * step, stage1_head(idx + 1, r)))
                p_dve = max(int(0.60 * n3), 1 + 3 * step + 2 + 4)
                ins.append((p_dve, [dict(call=lambda idx=idx: stage2_dve(idx + 1))]))
                ins.append((p_dve + 8, [dict(call=lambda idx=idx: stage2_pe(idx + 1))]))
                seq3 = merge(seq3, ins)
            run_seq(seq3)
        pipe.flush()
        barrier()
        if DEBUG == "C":
            dump("szp", SZP, BF16)
            finish()
            return nc, dbg

        WOA = c3(0, [4, 1024], BF16)
        WOC = c3(8192, [4, 1024], BF16)
        WOS = [c3(16384 + 4096 * k, [1024], F32) for k in range(4)]
        XR = [c3(32768 + 4096 * k, [1024], F32) for k in range(3)]
        YT = [c3(45056 + 4096 * k, [1024], F32) for k in range(4)]
        FNW = c3(61440, [1024], F32)
        JUNKF = c3(65536, [1024], F32)
        c_wo = [newchan("c_wo%d" % k) for k in range(4)]
        c_xr = [newchan("c_xr%d" % k) for k in range(3)]
        c_out = [newchan("c_out%d" % k) for k in range(4)]
        c_fnw = newchan("c_fnw")
        Bwos = [Buf("wos%d" % k) for k in range(4)]
        Bwo = Buf("wo")
        Bxr = [Buf("xr%d" % k) for k in range(3)]
        Byt = [Buf("yt%d" % k) for k in range(4)]
        Bssd = [Buf("ssd%d" % k) for k in range(8)]
        Bfnw = Buf("fnw")
        dma("sp", FNW, bass.AP(fnw_d.tensor, 0, [[0, 128], [1, 1024]]), c_fnw, writes=[Bfnw])
        for rc in range(8):
            k_ = rc % 4
            dma("sp", WOS[k_], wout_d[128 * rc:128 * rc + 128, :], c_wo[k_], writes=[Bwos[k_]])
            dst = WOA[:, rc, :] if rc < 4 else WOC[:, rc - 4, :]
            vcopy("dve", dst, WOS[k_], reads=[Bwos[k_]], writes=[Bwo] if rc in (0, 7) else [])

        def d_st1(n):
            s, sub = n // 4, n % 4
            kp, kx, ky = n % 2, n % 3, n % 4
            tc0 = s * 512 + sub * 128
            L0 = 512 * (3 + 4 * s) + 128 * sub
            dma("pool", XR[kx], x_d[L0:L0 + 128, :], c_xr[kx], writes=[Bxr[kx]])
            for hf in range(2):
                pso = ps_sc[kp][:, hf * 512:(hf + 1) * 512]
                for q in range(4):
                    mm(pso, SZP[:, q, tc0:tc0 + 128], WOA[:, q, hf * 512:(hf + 1) * 512], q == 0, False,
                       reads=[Bwo], writes=[Bsc[kp]])
                for ch in range(4):
                    mm(pso, MC[:, ch, tc0:tc0 + 128], WOC[:, ch, hf * 512:(hf + 1) * 512], False, ch == 3,
                       reads=[Bwo], writes=[Bsc[kp]])
            tt("dve", YT[ky], ps_sc[kp][:, :], XR[kx], ALU.add, reads=[Bsc[kp], Bxr[kx]], writes=[Byt[ky]])

        def d_sq(n):
            ss = SMALL[:, (n % 8) * 2:(n % 8) * 2 + 1]
            act(JUNKF, YT[n % 4], AF.Square, reads=[Byt[n % 4]], writes=[Bssd[n % 8]], accum_out=ss)

        def d_st2a(n):
            ss = SMALL[:, (n % 8) * 2:(n % 8) * 2 + 1]
            rs = SMALL[:, (n % 8) * 2 + 1:(n % 8) * 2 + 2]
            ts("dve", rs, ss, 1.0 / DM, ALU.mult, EPS, ALU.add, reads=[Bssd[n % 8]], writes=[Bssd[n % 8]])
            act(rs, rs, AF.Sqrt, reads=[Bssd[n % 8]], writes=[Bssd[n % 8]])

        def d_st2b(n):
            s, sub = n // 4, n % 4
            ky = n % 4
            tc0 = s * 512 + sub * 128
            rs = SMALL[:, (n % 8) * 2 + 1:(n % 8) * 2 + 2]
            S.op("dve", lambda e, rs=rs: e.reciprocal(out=rs, in_=rs), reads=[Bssd[n % 8]], writes=[Bssd[n % 8]])
            stt("dve", YT[ky], YT[ky], rs, FNW, ALU.mult, ALU.mult, reads=[Byt[ky], Bssd[n % 8], Bfnw], writes=[Byt[ky]])
            dma("sp", out_d[tc0:tc0 + 128, :], YT[ky], c_out[ky], reads=[Byt[ky]])

        d_st1(0)
        d_sq(0)
        for n in range(16):
            if n + 1 < 16:
                d_st1(n + 1)
            d_st2a(n)
            if n + 1 < 16:
                d_sq(n + 1)
            d_st2b(n)
        finish()
    return nc, dbg


_PROG = {}


def _get_program():
    if "p" not in _PROG:
        _PROG["p"] = build_program()
    return _PROG["p"]


def make_in_maps(x, norm_w, w_in, w_ck1, w_ck2, pe_k, w_cv1, w_cv2, pe_v, conv_w, w_out, rel_bias, final_norm_w):
    f32 = np.float32
    x = np.asarray(x, f32)
    w_in0 = np.ascontiguousarray(np.asarray(w_in, f32)[0])
    shared = dict(_shared_consts())
    shared.pop("ov_base")
    gate_cols = [1280 + br * 8 + h for h in range(8) for br in range(3)]
    shared.update(
        w_in=w_in0,
        w_gate=np.ascontiguousarray(w_in0[:, gate_cols]),
        nw=np.ascontiguousarray(np.asarray(norm_w, f32)[0].reshape(8, 128).T),
        w1k=np.ascontiguousarray(np.asarray(w_ck1, f32)[0].transpose(1, 0, 2)),
        w1v=np.ascontiguousarray(np.asarray(w_cv1, f32)[0].transpose(1, 0, 2)),
        w2k=np.ascontiguousarray(np.asarray(w_ck2, f32)[0]),
        w2v=np.ascontiguousarray(np.asarray(w_cv2, f32)[0]),
        pekT=np.ascontiguousarray(np.asarray(pe_k, f32)[0].T),
        pevT=np.ascontiguousarray(np.asarray(pe_v, f32)[0].T),
        cw=np.ascontiguousarray(np.asarray(conv_w, f32)[0].T.reshape(4, 128, 3).transpose(1, 0, 2).reshape(128, 12)),
        w_out=np.ascontiguousarray(np.asarray(w_out, f32)[0]),
        rel_bias=np.ascontiguousarray(np.asarray(rel_bias, f32)),
        fnw=np.ascontiguousarray(np.asarray(final_norm_w, f32)[None, :]),
    )
    in_maps = []
    for core in range(8):
        b, i = core // 4, core % 4
        pad = 1536 - 512 * i
        xc = np.zeros((SEQ, DM), f32)
        xc[pad:] = x[b, :SEQ - pad]
        m = dict(shared)
        m.update(_core_consts(i))
        m["x"] = xc
        in_maps.append(m)
    return in_maps


def kernel(x, norm_w, w_in, w_ck1, w_ck2, pe_k, w_cv1, w_cv2, pe_v, conv_w, w_out, rel_bias, final_norm_w):
    nc, _ = _get_program()
    in_maps = make_in_maps(x, norm_w, w_in, w_ck1, w_ck2, pe_k, w_cv1, w_cv2, pe_v, conv_w, w_out, rel_bias,
                           final_norm_w)
    res = run_bass_kernel_spmd(nc, in_maps, core_ids=list(range(8)))
    out = np.zeros((2, SEQ, DM), np.float32)
    for core in range(8):
        b, i = core // 4, core % 4
        o = res.results[core]["out"]
        for s in range(4):
            a = 4 * s + i
            out[b, 512 * a:512 * a + 512] = o[512 * s:512 * s + 512]
    return out
```

```python
import os
from contextlib import ExitStack

import numpy as np
import ml_dtypes

import concourse.bass as bass
import concourse.mybir as mybir
from concourse.bass_utils import run_bass_kernel_spmd

F32 = mybir.dt.float32
BF16 = mybir.dt.bfloat16
AF = mybir.ActivationFunctionType
ALU = mybir.AluOpType
NPBF = ml_dtypes.bfloat16

NEGBIG = -30000.0
EPS = 1e-6
SEQ = 8192
DM = 1024
NCOLS = 3864
LEN_S, LEN_W, LEN_C = 1151, 1535, 2544
LEN_F = LEN_S + LEN_W + LEN_C
OFF_W = LEN_S
OFF_C = LEN_S + LEN_W

DEBUG = os.environ.get("KDEBUG", "")
KCUT = int(os.environ.get("KCUT", "0"))


class Buf:
    __slots__ = ("name", "w", "r")

    def __init__(self, name=""):
        self.name = name
        self.w = None
        self.r = []


class Chan:
    __slots__ = ("sem", "n")

    def __init__(self, sem):
        self.sem = sem
        self.n = 0


class Op:
    __slots__ = ("eng", "fn", "deps", "chan", "chan_n", "sig", "signo", "is_dma", "chan_waits")

    def __init__(self, eng, fn, is_dma):
        self.eng = eng
        self.fn = fn
        self.deps = []
        self.chan = None
        self.chan_n = 0
        self.sig = False
        self.signo = 0
        self.is_dma = is_dma
        self.chan_waits = None


class Sched:
    ENGS = ("pe", "act", "dve", "pool", "sp")

    def __init__(self, nc):
        self.nc = nc
        self.ops = {e: [] for e in self.ENGS}
        self.all_ops = []
        self.cur_barrier = None
        self.chans = []

    def new_chan(self, sem):
        c = Chan(sem)
        self.chans.append(c)
        return c

    def _dep(self, op, other):
        if other is None or other is op:
            return
        if (not other.is_dma) and (not op.is_dma) and other.eng == op.eng and op.eng == "pe":
            return
        if other not in op.deps:
            op.deps.append(other)

    def op(self, eng, fn, reads=(), writes=(), chan=None):
        is_dma = chan is not None
        o = Op(eng, fn, is_dma)
        if self.cur_barrier is not None:
            o.deps.append(self.cur_barrier)
        for b in reads:
            self._dep(o, b.w)
        for b in writes:
            self._dep(o, b.w)
            for r in b.r:
                self._dep(o, r)
        for b in reads:
            b.r.append(o)
        for b in writes:
            b.w = o
            b.r = []
        if is_dma:
            chan.n += 1
            o.chan = chan
            o.chan_n = chan.n
        self.ops[eng].append(o)
        self.all_ops.append(o)
        return o

    def barrier(self, fn, chan):
        o = Op("sp", fn, True)
        for e in self.ENGS:
            for prev in reversed(self.ops[e]):
                if not prev.is_dma:
                    o.deps.append(prev)
                    break
        o.chan_waits = [(c, c.n) for c in self.chans if c.n > 0]
        chan.n += 1
        o.chan = chan
        o.chan_n = chan.n
        self.ops["sp"].append(o)
        self.all_ops.append(o)
        self.cur_barrier = o
        return o

    def emit(self, sems):
        nc = self.nc
        for o in self.all_ops:
            for d in o.deps:
                if not d.is_dma:
                    d.sig = True
        for e in self.ENGS:
            n = 0
            for o in self.ops[e]:
                if o.sig and not o.is_dma:
                    n += 1
                    o.signo = n
        all_chans = self.chans

        def run_engine(ename, eng):
            seen = {}
            for o in self.ops[ename]:
                need = {}
                for d in o.deps:
                    if d.is_dma:
                        key = ("c", id(d.chan))
                        val = 16 * d.chan_n
                        sem = d.chan.sem
                    else:
                        key = ("e", d.eng)
                        val = d.signo
                        sem = sems[d.eng]
                    if val > need.get(key, (0, None))[0]:
                        need[key] = (val, sem)
                if o.chan_waits:
                    for c, n in o.chan_waits:
                        key = ("c", id(c))
                        if 16 * n > need.get(key, (0, None))[0]:
                            need[key] = (16 * n, c.sem)
                for key, (val, sem) in need.items():
                    if seen.get(key, 0) >= val:
                        continue
                    seen[key] = val
                    eng.wait_ge(sem, val)
                ins = o.fn(eng)
                if o.is_dma:
                    ins.then_inc(o.chan.sem, 16)
                elif o.sig:
                    ins.then_inc(sems[ename], 1)
            if ename == "sp":
                for c in all_chans:
                    if c.n > 0 and seen.get(("c", id(c)), 0) < 16 * c.n:
                        eng.wait_ge(c.sem, 16 * c.n)
                for e2 in self.ENGS:
                    last = 0
                    for o2 in self.ops[e2]:
                        if o2.sig and not o2.is_dma:
                            last = o2.signo
                    if last > 0:
                        eng.wait_ge(sems[e2], last)

        with nc.Block() as block:
            @block.tensor
            def _(eng):
                run_engine("pe", eng)

            @block.scalar
            def _(eng):
                run_engine("act", eng)

            @block.vector
            def _(eng):
                run_engine("dve", eng)

            @block.gpsimd
            def _(eng):
                run_engine("pool", eng)

            @block.sync
            def _(eng):
                run_engine("sp", eng)


def _t5_bucket(d):
    n = np.maximum(d, 0)
    nf = np.maximum(n, 1).astype(np.float32)
    large = 16 + (np.log(nf / np.float32(16.0)) / np.float32(np.log(8.0)) * np.float32(16.0)).astype(np.int32)
    large = np.minimum(large, 31)
    return np.where(n < 16, n, large)


def _onehot_rows():
    oh = np.zeros((33, LEN_F), np.float32)

    def fill(off, length, i0, wmax):
        d = np.arange(length) - i0
        masked = d < 0
        if wmax is not None:
            masked |= d >= wmax
        bk = _t5_bucket(d)
        for i in range(length):
            if masked[i]:
                oh[32, off + i] = 1.0
            else:
                oh[bk[i], off + i] = 1.0

    fill(0, LEN_S, 511, None)
    fill(OFF_W, LEN_W, 511, 512)
    fill(OFF_C, LEN_C, 527, None)
    return oh


_CONST_CACHE = {}


def _shared_consts():
    if "c" in _CONST_CACHE:
        return _CONST_CACHE["c"]
    ident = np.eye(128, dtype=np.float32).astype(NPBF)
    antiI = np.eye(128, dtype=np.float32)[::-1].copy().astype(NPBF)
    L = np.arange(SEQ)
    eall = np.zeros((64, SEQ), np.float32)
    eall[(L // 64) % 64, L] = 1.0
    c = np.arange(512)
    j = np.arange(128)
    ov = ((16 * c[:, None] < 64 * j[None, :] + 64) & (16 * c[:, None] + 32 > 64 * j[None, :])).astype(np.float32)
    ov[511, :] = 0.0
    ov = ov.reshape(4, 128, 128).transpose(1, 0, 2).copy()
    sel = np.zeros((67, 3, 64), np.float32)
    for br in range(3):
        sel[64 + br, br, :] = 1.0
    d = dict(ident=ident, antiI=antiI, eall=eall.astype(NPBF), ov_base=ov, oh=_onehot_rows(), selm=sel)
    _CONST_CACHE["c"] = d
    return d


def _core_consts(i):
    pad = 1536 - 512 * i
    jmin = pad // 64
    kinv = (np.arange(1536) < pad).astype(np.float32)[None].astype(NPBF)
    cinv = (16 * np.arange(128) < pad).astype(np.float32)[None].astype(NPBF)
    vcm = np.zeros((4, 128, 4, 128), np.float32)
    addm = np.zeros((4, 128, 4, 128), np.float32)
    jj = np.arange(128)[None, :]
    for s in range(4):
        T = 3 + 4 * s
        for u in range(4):
            tL = 512 * T + 128 * u + np.arange(128)
            blk = (tL // 64)[:, None]
            valid = jj >= jmin
            vc = valid & (jj <= blk)
            forced = ((jj == jmin) | (jj == blk) | (jj == blk - 1)) & vc
            vcm[s, :, u, :] = vc
            addm[s, :, u, :] = (vc.astype(np.float32) - 1.0) + 2000.0 * forced
    cval = ((16 * np.arange(512) >= pad) & (np.arange(512) <= 510)).astype(np.float32)
    cval_pc = cval.reshape(4, 128).T
    ov = _shared_consts()["ov_base"].copy()
    ov[:, :, 127] = 1.0
    ov = ov * cval_pc[:, :, None]
    kvw = np.zeros((128, 32), np.float32)
    for s in range(4):
        for wi in range(8):
            kt = 8 + 16 * s + wi
            kvw[:, 8 * s + wi] = (128 * kt + np.arange(128)) >= pad
    return dict(vcm=vcm, addm=addm, ov=ov.astype(NPBF), kvw=kvw, cvc=np.ascontiguousarray(cval_pc))


P1 = 0
O_KAUG = 0
O_VS = 32768
O_KWIN = 50176
O_VW = 66560
O_KCT = 75264
O_VC = 77312
P2 = 78464
P3 = P2 + 57344
P3_SIZE = 72704
PC = P3 + P3_SIZE
ARENA_BYTES = PC + 3584


def build_program(stop_after="D"):
    nc = bass.Bass("TRN2", target_bir_lowering=False)

    def din(name, shape, dt=F32):
        return nc.dram_tensor(name, list(shape), dt, kind="ExternalInput").ap()

    x_d = din("x", [SEQ, DM])
    win_d = din("w_in", [DM, NCOLS])
    wgate_d = din("w_gate", [DM, 24])
    nw_d = din("nw", [128, 8])
    w1k_d = din("w1k", [64, 32, 128])
    w1v_d = din("w1v", [64, 32, 128])
    w2k_d = din("w2k", [128, 64])
    w2v_d = din("w2v", [128, 64])
    pek_d = din("pekT", [64, 32])
    pev_d = din("pevT", [64, 32])
    cw_d = din("cw", [128, 12])
    wout_d = din("w_out", [DM, DM])
    relb_d = din("rel_bias", [32, 8])
    fnw_d = din("fnw", [1, DM])
    ident_d = din("ident", [128, 128], BF16)
    anti_d = din("antiI", [128, 128], BF16)
    eall_d = din("eall", [64, SEQ], BF16)
    ov_d = din("ov", [128, 4, 128], BF16)
    oh_d = din("oh", [33, LEN_F])
    selm_d = din("selm", [67, 3, 64])
    kvw_d = din("kvw", [128, 32])
    cvc_d = din("cvc", [128, 4])
    vcm_d = din("vcm", [4, 128, 4, 128])
    addm_d = din("addm", [4, 128, 4, 128])
    out_d = nc.dram_tensor("out", [2048, DM], F32, kind="ExternalOutput").ap()
    fd_d = nc.dram_tensor("fd_scr", [8, LEN_F], BF16, kind="Internal").ap()
    bar_d = nc.dram_tensor("bar_scr", [1, 64], F32, kind="Internal").ap()
    dbg = {}

    def dbg_out(name, shape, dt):
        t = nc.dram_tensor("dbg_" + name, list(shape), dt, kind="ExternalOutput").ap()
        dbg[name] = t
        return t

    with ExitStack() as es:
        arena = es.enter_context(nc.sbuf_tensor("arena", [128, ARENA_BYTES // 2], BF16))
        ps_sc = [es.enter_context(nc.psum_tensor("ps_sc%d" % k, [128, 1024], F32)) for k in range(2)]
        ps_m = [es.enter_context(nc.psum_tensor("ps_m%d" % k, [128, 512], F32)) for k in range(4)]
        sems = {e: es.enter_context(nc.semaphore("s_" + e)) for e in Sched.ENGS}
        S = Sched(nc)

        def newchan(name):
            return S.new_chan(es.enter_context(nc.semaphore(name)))

        bar_chan = newchan("c_bar")

        def carve(off, shape, dt, p0=0, p1=128):
            n = 1
            for d_ in shape:
                n *= d_
            nb = n * (4 if dt == F32 else 2)
            assert off % 4 == 0 and off + nb <= ARENA_BYTES, (off, nb)
            v = arena[p0:p1, off // 2: (off + nb) // 2]
            if dt == F32:
                v = v.bitcast(F32)
            if len(shape) == 2:
                v = v.rearrange("p (a b) -> p a b", a=shape[0])
            elif len(shape) == 3:
                v = v.rearrange("p (a b c) -> p a b c", a=shape[0], b=shape[1])
            return v

        def barrier():
            S.barrier(lambda e: e.dma_start(out=bar_d[0:1, 0:16], in_=fnw_d[0:1, 0:16]), bar_chan)

        def mm(out, lhsT, rhs, start, stop, reads=(), writes=()):
            return S.op("pe", lambda e: e.matmul(out, lhsT=lhsT, rhs=rhs, start=start, stop=stop), reads, writes)

        def tr(out, in_, reads=(), writes=()):
            return S.op("pe", lambda e: e.transpose(out=out, in_=in_, identity=IDENT), reads, writes)

        def act(out, in_, func, reads=(), writes=(), bias=None, scale=None, accum_out=None):
            kw = {}
            if bias is not None:
                kw["bias"] = bias
            if scale is not None:
                kw["scale"] = scale
            if accum_out is not None:
                kw["accum_out"] = accum_out
            return S.op("act", lambda e: e.activation(out=out, in_=in_, func=func, **kw), reads, writes)

        def vcopy(eng, out, in_, reads=(), writes=()):
            return S.op(eng, lambda e: e.tensor_copy(out=out, in_=in_), reads, writes)

        def tt(eng, out, in0, in1, op, reads=(), writes=()):
            return S.op(eng, lambda e: e.tensor_tensor(out=out, in0=in0, in1=in1, op=op), reads, writes)

        def ts(eng, out, in0, s1, op0, s2=None, op1=None, reads=(), writes=()):
            if op1 is None:
                return S.op(eng, lambda e: e.tensor_scalar(out=out, in0=in0, scalar1=s1, scalar2=None, op0=op0), reads, writes)
            return S.op(eng, lambda e: e.tensor_scalar(out=out, in0=in0, scalar1=s1, scalar2=s2, op0=op0, op1=op1), reads, writes)

        def stt(eng, out, in0, scalar, in1, op0, op1, reads=(), writes=()):
            return S.op(eng, lambda e: e.scalar_tensor_tensor(out=out, in0=in0, scalar=scalar, in1=in1, op0=op0, op1=op1), reads, writes)

        def memset(eng, ap, val, reads=(), writes=()):
            return S.op(eng, lambda e: e.memset(ap, val), reads, writes)

        def dma(eng, out, in_, chan, reads=(), writes=()):
            return S.op(eng, lambda e: e.dma_start(out=out, in_=in_), reads, writes, chan=chan)

        KAUG = carve(O_KAUG, [2, 8192], BF16)
        VS = carve(O_VS, [64, 2, 68], BF16)
        KWIN = carve(O_KWIN, [2, 4096], BF16)
        VW = carve(O_VW, [32, 2, 68], BF16)
        KCT = carve(O_KCT, [2, 512], BF16)
        VC = carve(O_VC, [4, 2, 68], BF16)
        QTP = carve(P2 + 0, [4, 2048], BF16)
        SZP = carve(P2 + 16384, [4, 2048], BF16)
        MC = carve(P2 + 32768, [4, 2048], BF16)
        G24 = carve(P2 + 49152, [2048], F32, 0, 24)
        IDENT = carve(PC + 0, [128], BF16)
        ANTI = carve(PC + 256, [128], BF16)
        NW = carve(PC + 512, [8], F32)
        B31 = carve(PC + 544, [8], F32)
        OV = carve(PC + 576, [4, 128], BF16)
        SELM = carve(PC + 1600, [3, 64], F32, 0, 67)
        CW = carve(PC + 2368, [12], F32)
        HIDB = carve(PC + 2416, [2], F32)
        RELB = carve(PC + 2432, [8], F32, 0, 33)
        RB31 = carve(PC + 2464, [8], F32, 0, 32)
        ONES = carve(PC + 2496, [2], BF16)
        SMALL = carve(PC + 2560, [64], F32)
        c_const = newchan("c_const")
        c_dbg = newchan("c_dbg")
        Bps = [Buf("psm%d" % k) for k in range(4)]
        Bsc = [Buf("pssc%d" % k) for k in range(2)]
        Bsc4 = [Buf("pssch%d" % k) for k in range(4)]
        Bc = Buf("consts")
        Bsmall = Buf("small")

        def finish():
            S.emit(sems)

        def dump(name, ap, dt):
            t = dbg_out(name, list(ap.shape), dt)
            idx = tuple(slice(None) for _ in ap.shape)
            dma("sp", t[idx], ap, c_dbg)

        KVW = carve(PC + 2816, [32], F32)
        CVC = carve(PC + 2944, [4], F32)
        b31_src = bass.AP(relb_d.tensor, 31 * 8, [[0, 128], [1, 8]])
        rb31_src = bass.AP(relb_d.tensor, 31 * 8, [[0, 32], [1, 8]])
        const_loads = [(IDENT, ident_d[:, :]), (ANTI, anti_d[:, :]), (NW, nw_d[:, :]), (OV, ov_d[:, :, :]),
                       (SELM, selm_d[:, :, :]), (CW, cw_d[:, :]), (RELB[0:32, :], relb_d[:, :]),
                       (B31, b31_src), (RB31, rb31_src), (KVW, kvw_d[:, :]), (CVC, cvc_d[:, :]),
                       (KAUG[64:128, 0, :], eall_d[:, :]), (KAUG[64:128, 1, :], eall_d[:, :])]
        for i_, (dst, src_) in enumerate(const_loads):
            dma("sp", dst, src_, c_const, writes=[Bc] if i_ == len(const_loads) - 1 else [])
        Bvi = Buf("vind")
        memset("pool", ONES, 1.0, writes=[Bvi])
        for (V_, one) in ((VS, 65), (VW, 66), (VC, 64)):
            memset("pool", V_[:, :, :, 64:68], 0.0, writes=[Bvi])
            memset("pool", V_[:, :, :, one:one + 1], 1.0, writes=[Bvi])
        memset("pool", VS[:, :, :, 64:65], 0.5, writes=[Bvi])
        memset("pool", VW[:, :, :, 64:65], 0.5, writes=[Bvi])
        for g in range(2):
            tt("dve", VW[:, :, g, 64], VW[:, :, g, 64], KVW, ALU.mult, reads=[Bc, Bvi], writes=[Bvi])
            tt("dve", VW[:, :, g, 66], VW[:, :, g, 66], KVW, ALU.mult, reads=[Bc, Bvi], writes=[Bvi])
            tt("dve", VC[:, :, g, 64], VC[:, :, g, 64], CVC, ALU.mult, reads=[Bc, Bvi], writes=[Bvi])
        KCMPT = carve(P2 + 0, [8192], BF16)
        VCMPT = carve(P2 + 16384, [8192], BF16)
        XNTOWN = carve(P3 + 0, [8, 4, 514], BF16)
        XNTW = [carve(P3 + 32896 + 8192 * k, [8, 512], BF16) for k in range(2)]
        WKV = carve(P3 + 49280, [8, 1024], BF16)
        WSTG = [carve(P3 + 65664 + 2048 * k, [8, 64], F32) for k in range(2)]
        win_v = win_d.rearrange("(c p) n -> p c n", p=128)
        c_w = [newchan("c_w%d" % k) for k in range(2)]
        Bwstg = [Buf("wstg%d" % k) for k in range(2)]
        Bwkv = Buf("wkv")
        wcnt = [0]

        def load_w_block(dst_fn, src_v, col0, ncol, stg, Bdst):
            k_ = wcnt[0] % 2
            wcnt[0] += 1
            dma("sp", stg[k_][:, :, 0:ncol], src_v[:, :, col0:col0 + ncol], c_w[k_], writes=[Bwstg[k_]])
            for c in range(8):
                ts("dve", dst_fn(c), stg[k_][:, c, 0:ncol], NW[:, c:c + 1], ALU.mult,
                   reads=[Bwstg[k_], Bc], writes=[Bdst] if c in (0, 7) else [])

        XT = [carve(P2 + 32768 + 4096 * k, [1024], F32) for k in range(4)]
        Bxt = [Buf("xt%d" % k) for k in range(4)]
        c_x = [newchan("c_x%d" % k) for k in range(4)]
        wmap = [(0, 512, 128), (128, 640, 128), (6 * 128, 896, 128), (7 * 128, 1152, 128)]
        wmap.append((2 * 128, 768, 128))
        wmap.append((3 * 128, 768, 128))
        for g in range(2):
            for dup in range(2):
                wmap.append(((4 + g) * 128 + 64 * dup, 1024 + 64 * g, 64))
        for c in range(8):
            k_ = c % 4
            dma("sp", XT[k_][:, 0:768], win_d[128 * c:128 * c + 128, 512:1280], c_x[k_], writes=[Bxt[k_]])
            for wi_, (dcol, scol, n) in enumerate(wmap):
                ts("dve", WKV[:, c, dcol:dcol + n], XT[k_][:, scol - 512:scol - 512 + n], NW[:, c:c + 1], ALU.mult,
                   reads=[Bxt[k_], Bc], writes=[Bwkv] if (c == 7 and wi_ == len(wmap) - 1) else [])

        OHS = carve(P3 + 0, [LEN_F], F32, 0, 33)
        FROW = carve(P3 + 20992, [LEN_F], BF16, 0, 8)
        Boh = Buf("oh")
        Brelb = Buf("relb")
        c_oh = newchan("c_oh")
        dma("sp", OHS, oh_d[:, :], c_oh, writes=[Boh])
        memset("dve", RELB[32:33, :], NEGBIG, reads=[Bc], writes=[Brelb])
        tt("dve", RELB[0:32, :], RELB[0:32, :], RB31, ALU.subtract, reads=[Bc, Brelb], writes=[Brelb])
        Bfrow = Buf("frow")
        c0 = 0
        k = 0
        while c0 < LEN_F:
            n = min(512, LEN_F - c0)
            bank = k % 2
            mm(ps_m[bank][0:8, 0:n], RELB[0:33, :], OHS[:, c0:c0 + n], True, True, reads=[Brelb, Boh], writes=[Bps[bank]])
            vcopy("dve", FROW[:, c0:c0 + n], ps_m[bank][0:8, 0:n], reads=[Bps[bank]], writes=[Bfrow])
            c0 += n
            k += 1
        Bfd = Buf("fd")
        dma("sp", fd_d[:, :], FROW, c_const, reads=[Bfrow], writes=[Bfd])
        if DEBUG == "0":
            dump("frow", FROW, BF16)
            dump("kaug", KAUG[:, :, 0:2048], BF16)
            finish()
            return nc, dbg

        if DEBUG == "w":
            dump("wkv", WKV, BF16)
            finish()
            return nc, dbg
        XN = [carve(P2 + 49152 + 2048 * k, [1024], BF16) for k in range(4)]
        JUNK = carve(P3 + 69760, [1024], BF16)
        Bxn = [Buf("xn%d" % k) for k in range(4)]
        Bss = [Buf("ss%d" % k) for k in range(8)]
        Bxntw = [Buf("xntw%d" % k) for k in range(2)]
        Bxown = [Buf("xown%d" % k) for k in range(4)]

        SLST = [carve(P3 + 67712 + 1024 * k, [512], BF16) for k in range(2)]
        Bsl = [Buf("sl%d" % k) for k in range(2)]
        c_sl = [newchan("c_sl%d" % k) for k in range(2)]

        def tile_dst(n):
            T = n // 4
            if T % 4 == 3:
                return XNTOWN[:, :, T // 4, 2:514], Bxown[T // 4]
            return XNTW[T % 2], Bxntw[T % 2]

        def stA(n):
            kx = n % 4
            ss = SMALL[:, (n % 8) * 2:(n % 8) * 2 + 1]
            rs = SMALL[:, (n % 8) * 2 + 1:(n % 8) * 2 + 2]
            dma("sp", XT[kx], x_d[128 * n:128 * n + 128, :], c_x[kx], writes=[Bxt[kx]])
            act(JUNK, XT[kx], AF.Square, reads=[Bxt[kx]], writes=[Bss[n % 8]], accum_out=ss)
            ts("dve", rs, ss, 1.0 / DM, ALU.mult, EPS, ALU.add, reads=[Bss[n % 8]], writes=[Bss[n % 8]])

        def stB(n):
            rs = SMALL[:, (n % 8) * 2 + 1:(n % 8) * 2 + 2]
            act(rs, rs, AF.Sqrt, reads=[Bss[n % 8]], writes=[Bss[n % 8]])
            S.op("dve", lambda e, rs=rs: e.reciprocal(out=rs, in_=rs), reads=[Bss[n % 8]], writes=[Bss[n % 8]])

        def stC(n):
            kx = n % 4
            sub = n % 4
            rs = SMALL[:, (n % 8) * 2 + 1:(n % 8) * 2 + 2]
            dstT, dstB = tile_dst(n)
            act(XN[kx], XT[kx], AF.Copy, reads=[Bxt[kx], Bss[n % 8]], writes=[Bxn[kx]], scale=rs)
            pb = 2 + (n % 2)
            pst = ps_m[pb][:, 0:512].bitcast(BF16)
            for c in range(8):
                tr(pst[:, c * 128:(c + 1) * 128], XN[kx][:, c * 128:(c + 1) * 128], reads=[Bxn[kx], Bc], writes=[Bps[pb]])
            vcopy("dve", dstT[:, :, sub * 128:(sub + 1) * 128], pst.rearrange("p (c n) -> p c n", c=8),
                  reads=[Bps[pb]] + ([Bfd] if (n // 4) % 4 == 3 else []), writes=[dstB])

        def projA(T):
            XT_T, BT = tile_dst(4 * T)
            if T % 4 == 2:
                vcopy("pool", XNTOWN[:, :, T // 4, 0:2], XT_T[:, :, 510:512], reads=[BT, Bfd], writes=[Bxown[T // 4]])
            cols = slice(512 * T, 512 * T + 512)
            kcv = KCMPT.rearrange("p (r c) -> p r c", r=16)[:, :, 32 * T:32 * T + 32]
            vcv = VCMPT.rearrange("p (r c) -> p r c", r=16)[:, :, 32 * T:32 * T + 32]
            fm = [(0, ("cm", kcv), 128), (1, ("cm", vcv), 128), (2, None, 128)]
            if T % 4 >= 2:
                w0_ = 512 * (2 * (T // 4) + (T % 4 - 2))
                fm += [(4, KWIN[:, 0, w0_:w0_ + 512], 128), (5, KWIN[:, 1, w0_:w0_ + 512], 128)]
            for ii, (blk, dst, m) in enumerate(fm):
                pb = ii % 4
                pso = ps_sc[pb // 2][0:m, (pb % 2) * 512:(pb % 2) * 512 + 512]
                for c in range(8):
                    mm(pso, WKV[:, c, blk * 128:blk * 128 + m], XT_T[:, c, :], c == 0, c == 7,
                       reads=[BT, Bwkv], writes=[Bsc4[pb]])
                if isinstance(dst, tuple):
                    vcopy("dve", dst[1], pso.rearrange("p (c r) -> p r c", r=16), reads=[Bsc4[pb]])
                elif dst is not None:
                    vcopy("dve", dst, pso, reads=[Bsc4[pb]])
                else:
                    ks = T % 2
                    vcopy("dve", KAUG[0:64, 0, cols], pso[0:64, :], reads=[Bsc4[pb]])
                    vcopy("dve", SLST[ks][64:128, :], pso[64:128, :], reads=[Bsc4[pb]], writes=[Bsl[ks]])
                    dma("pool", KAUG[0:64, 1, cols], SLST[ks][64:128, :], c_sl[ks], reads=[Bsl[ks]])
            for sub in range(4):
                kt = 4 * T + sub
                pb = sub % 2
                pso = ps_m[pb][:, 0:256]
                for c in range(8):
                    mm(pso, XT_T[:, c, sub * 128:(sub + 1) * 128], WKV[:, c, 768:1024], c == 0, c == 7,
                       reads=[BT, Bwkv], writes=[Bps[pb]])
                vcopy("dve", VS[:, kt, :, 0:64], pso[:, 0:128].rearrange("p (g d) -> p g d", g=2), reads=[Bps[pb]])
                if T % 4 >= 2:
                    slot = 8 * (T // 4) + 4 * (T % 4 - 2) + sub
                    vcopy("dve", VW[:, slot, :, 0:64], pso[:, 128:256].rearrange("p (g d) -> p g d", g=2),
                          reads=[Bps[pb]])

        for n in range(4):
            stA(n)
            stB(n)
        for T in range(16):
            for n in range(4 * T, 4 * T + 4):
                stC(n)
            if T + 1 < 16:
                for n in range(4 * T + 4, 4 * T + 8):
                    stA(n)
                    stB(n)
            projA(T)
        W1SX = carve(P2 + 32768, [32, 128], F32)
        W2S = carve(P3 + 66176, [2, 64], F32)
        PETS = carve(P3 + 66816, [2, 32], F32)
        c_a2 = newchan("c_a2")
        c_a2b = newchan("c_a2b")
        Bw1s = Buf("w1s")
        Bw1sx = Buf("w1sx")
        for hf in range(2):
            dma("sp", W1SX[64 * hf:64 * hf + 64, :, :], w1k_d[:, :, :], c_a2b, writes=Bxt + [Bw1sx])
        for kv, (w2d, ped) in enumerate(((w2k_d, pek_d), (w2v_d, pev_d))):
            for hf in range(2):
                dma("sp", PETS[64 * hf:64 * hf + 64, kv, :], ped[:, :], c_a2b, writes=[Bw1sx])
            dma("sp", W2S[:, kv, :], w2d[:, :], c_a2b, writes=[Bw1sx])
        barrier()
        if DEBUG == "A1":
            dump("kaug", KAUG[:, :, 0:2048], BF16)
            dump("vs", VS, BF16)
            dump("kwin", KWIN, BF16)
            dump("vw", VW, BF16)
            finish()
            return nc, dbg

        W1 = [carve(P3 + 32896 + 8192 * k, [32, 128], BF16) for k in range(2)]
        W1S = carve(P3 + 49280, [32, 128], F32)
        W2 = carve(P3 + 65664, [2, 128], BF16)
        PET = carve(P3 + 66688, [2, 32], BF16)
        HSIL = [carve(P3 + 67072 + 1024 * k, [512], BF16) for k in range(2)]
        Bw1 = Buf("w1")
        Bhs = [Buf("hs0"), Buf("hs1")]
        for hf in range(2):
            dma("sp", W1S[64 * hf:64 * hf + 64, :, :], w1v_d[:, :, :], c_a2, writes=[Bw1s])
        for kv in range(2):
            stg, Bstg = (W1SX, Bw1sx) if kv == 0 else (W1S, Bw1s)
            vcopy("dve", W1[kv], stg, reads=[Bstg], writes=[Bw1])
            vcopy("pool", PET[:, kv, :], PETS[:, kv, :], reads=[Bw1sx], writes=[Bw1])
            vcopy("pool", W2[:, kv, 0:64], W2S[:, kv, :], reads=[Bw1sx], writes=[Bw1])
            vcopy("pool", W2[:, kv, 64:128], W2S[:, kv, :], reads=[Bw1sx], writes=[Bw1])
            for j in range(32):
                mm(ps_m[2][:, 0:1], W1[kv][0:64, j, :], PET[0:64, kv, j:j + 1], j == 0, j == 31, reads=[Bw1], writes=[Bps[2]])
            vcopy("dve", HIDB[:, kv:kv + 1], ps_m[2][:, 0:1], reads=[Bps[2]], writes=[Bw1])
        Bvc = Buf("vc")
        memset("pool", HSIL[0][:, 511:512], 0.0, writes=[Bhs[0]])
        memset("pool", HSIL[1][:, 511:512], 0.0, writes=[Bhs[1]])
        for g in range(2):
            memset("pool", KCT[:, g, 511:512], 0.0)
            for kv in range(2):
                src = (KCMPT if kv == 0 else VCMPT)
                pr = slice(64 * g, 64 * g + 64)
                hid = ps_sc[kv][:, 0:511]
                srcv = src[pr, :].rearrange("p (r c) -> p r c", r=16)
                for j in range(32):
                    mm(hid, W1[kv][pr, j, :], srcv[:, j % 16, j // 16:j // 16 + 511], j == 0, j == 31,
                       reads=[Bw1], writes=[Bsc[kv]])
                act(HSIL[kv][:, 0:511], hid, AF.Silu, reads=[Bsc[kv], Bw1], writes=[Bhs[kv]], bias=HIDB[:, kv:kv + 1])
            mm(ps_m[0][:, 0:511], W2[:, 0, :], HSIL[0][:, 0:511], True, True, reads=[Bhs[0], Bw1], writes=[Bps[0]])
            vcopy("dve", KCT[:, g, 0:511], ps_m[0][:, 0:511], reads=[Bps[0]])
            for ct in range(4):
                mm(ps_m[1][:, ct * 64:(ct + 1) * 64], HSIL[1][:, ct * 128:(ct + 1) * 128], W2[:, 1, 0:64], True, True,
                   reads=[Bhs[1], Bw1], writes=[Bps[1]])
            vcopy("dve", VC[:, :, g, 0:64], ps_m[1][:, 0:256].rearrange("p (c d) -> p c d", c=4),
                  reads=[Bps[1]], writes=[Bvc])
            for ct in range(4):
                ts("dve", VC[:, ct, g, 0:64], VC[:, ct, g, 0:64], CVC[:, ct:ct + 1], ALU.mult,
                   reads=[Bvc, Bc], writes=[Bvc])
        barrier()
        if DEBUG == "A":
            dump("kaug", KAUG[:, :, 0:2048], BF16)
            dump("vs", VS, BF16)
            dump("kwin", KWIN, BF16)
            dump("vw", VW, BF16)
            dump("kct", KCT, BF16)
            dump("vc", VC, BF16)
            finish()
            return nc, dbg

        WSTG2 = [carve(P3 + 32896 + 4096 * k, [8, 128], F32) for k in range(2)]
        WG = [carve(P3 + 41088 + 2048 * k, [8, 128], BF16) for k in range(6)]
        HSB = carve(P3 + 53376, [514], F32)
        UU = carve(P3 + 55440, [514], F32)
        YY = carve(P3 + 57504, [512], F32)
        SZC = carve(P3 + 59552, [512], F32)
        BS = carve(P3 + 61600, [512], F32)
        Bwg = [Buf("wg%d" % k) for k in range(6)]
        Bout = Buf("projout")
        Btmp = {n: Buf(n) for n in ("hsb", "uu", "yy", "szc", "bs")}
        wgate_v = wgate_d.rearrange("(c p) n -> p c n", p=128)

        def own_cols(s):
            return XNTOWN[:, :, s, 2:514]

        def proj4(wg, nm, dst_fn):
            for s in range(4):
                pb = s % 4
                pso = ps_sc[pb // 2][0:nm, (pb % 2) * 512:(pb % 2) * 512 + 512]
                for c in range(8):
                    mm(pso, WG[wg][:, c, 0:nm], XNTOWN[:, c, s, 2:514], c == 0, c == 7, reads=[Bwg[wg]], writes=[Bsc4[pb]])
                dst_fn(s, pso, Bsc4[pb])

        wgi = 0
        for qb in range(4):
            w_ = wgi % 6
            wgi += 1
            load_w_block(lambda c, w_=w_: WG[w_][:, c, :], win_v, 128 * qb, 128, WSTG2, Bwg[w_])
            proj4(w_, 128, lambda s, pso, B_, qb=qb: act(QTP[:, qb, s * 512:(s + 1) * 512], pso, AF.Copy,
                                                       reads=[B_], scale=0.125))
        for qb in range(4):
            w_ = wgi % 6
            wgi += 1
            load_w_block(lambda c, w_=w_: WG[w_][:, c, :], win_v, 1304 + 128 * qb, 128, WSTG2, Bwg[w_])
            proj4(w_, 128, lambda s, pso, B_, qb=qb: act(SZP[:, qb, s * 512:(s + 1) * 512], pso, AF.Silu,
                                                       reads=[B_]))
        w_ = wgi % 6
        wgi += 1
        load_w_block(lambda c, w_=w_: WG[w_][:, c, 0:24], wgate_v, 0, 24, WSTG2, Bwg[w_])
        proj4(w_, 24, lambda s, pso, B_: act(G24[:, s * 512:(s + 1) * 512], pso, AF.Sigmoid, reads=[B_]))
        for ch in range(4):
            ws = []
            for (nm_, c0_) in (("h", 1816), ("b", 2328), ("c", 2840), ("z", 3352)):
                w_ = wgi % 6
                wgi += 1
                load_w_block(lambda c, w_=w_: WG[w_][:, c, :], win_v, c0_ + 128 * ch, 128, WSTG2, Bwg[w_])
                ws.append(w_)
            wh, wb, wc_, wz = ws
            for s in range(4):
                qcols = slice(s * 512, (s + 1) * 512)
                ps_h = ps_sc[0][:, 0:512]
                ps_c = ps_sc[0][:, 512:1024]
                ps_z = ps_sc[1][:, 0:512]
                ps_b = ps_sc[1][:, 512:1024]
                ps_h2 = ps_m[0][:, 0:2]
                ps_c2 = ps_m[1][:, 0:2]
                for c in range(8):
                    mm(ps_h, WG[wh][:, c, :], XNTOWN[:, c, s, 2:514], c == 0, c == 7, reads=[Bwg[wh]], writes=[Bsc4[0]])
                for c in range(8):
                    mm(ps_h2, WG[wh][:, c, :], XNTOWN[:, c, s, 0:2], c == 0, c == 7, reads=[Bwg[wh]], writes=[Bps[0]])
                for c in range(8):
                    mm(ps_c, WG[wc_][:, c, :], XNTOWN[:, c, s, 2:514], c == 0, c == 7, reads=[Bwg[wc_]], writes=[Bsc4[1]])
                for c in range(8):
                    mm(ps_c2, WG[wc_][:, c, :], XNTOWN[:, c, s, 0:2], c == 0, c == 7, reads=[Bwg[wc_]], writes=[Bps[1]])
                for c in range(8):
                    mm(ps_z, WG[wz][:, c, :], XNTOWN[:, c, s, 2:514], c == 0, c == 7, reads=[Bwg[wz]], writes=[Bsc4[2]])
                for c in range(8):
                    mm(ps_b, WG[wb][:, c, :], XNTOWN[:, c, s, 2:514], c == 0, c == 7, reads=[Bwg[wb]], writes=[Bsc4[3]])
                act(HSB[:, 2:514], ps_h, AF.Copy, reads=[Bsc4[0]], writes=[Btmp["hsb"]])
                vcopy("dve", HSB[:, 0:2], ps_h2, reads=[Bps[0]], writes=[Btmp["hsb"]])
                tt("dve", UU[:, 2:514], ps_c, HSB[:, 2:514], ALU.mult, reads=[Bsc4[1], Btmp["hsb"]], writes=[Btmp["uu"]])
                tt("dve", UU[:, 0:2], ps_c2, HSB[:, 0:2], ALU.mult, reads=[Bps[1], Btmp["hsb"]], writes=[Btmp["uu"]])
                ts("dve", YY, UU[:, 2:514], CW[:, 3 * ch + 2:3 * ch + 3], ALU.mult, reads=[Btmp["uu"], Bc], writes=[Btmp["yy"]])
                stt("dve", YY, UU[:, 1:513], CW[:, 3 * ch + 1:3 * ch + 2], YY, ALU.mult, ALU.add,
                    reads=[Btmp["uu"], Btmp["yy"], Bc], writes=[Btmp["yy"]])
                stt("dve", YY, UU[:, 0:512], CW[:, 3 * ch:3 * ch + 1], YY, ALU.mult, ALU.add,
                    reads=[Btmp["uu"], Btmp["yy"], Bc], writes=[Btmp["yy"]])
                act(SZC, ps_z, AF.Silu, reads=[Bsc4[2]], writes=[Btmp["szc"]])
                tt("dve", BS, ps_b, YY, ALU.mult, reads=[Bsc4[3], Btmp["yy"]], writes=[Btmp["bs"]])
                tt("pool", MC[:, ch, qcols], BS, SZC, ALU.mult, reads=[Btmp["bs"], Btmp["szc"]])
        barrier()
        if DEBUG == "B":
            dump("qtp", QTP, BF16)
            dump("szp", SZP, BF16)
            dump("mc", MC, BF16)
            dump("g24", G24, F32)
            finish()
            return nc, dbg

        def c3(off, shape, dt, p0=0, p1=128):
            return carve(P3 + off, shape, dt, p0, p1)
        AC = [c3(4096 * k, [4, 512], BF16, 0, 64) for k in range(2)]
        MT0 = [c3(1024 * k, [512], BF16, 64, 128) for k in range(2)]
        MT1 = [c3(2048 + 1024 * k, [512], BF16, 64, 128) for k in range(2)]
        ZSW1 = c3(4096, [512], F32, 64, 65)
        CF1H = c3(6144, [512], BF16, 64, 65)
        CF1L = c3(7168, [512], BF16, 64, 65)
        X0 = [c3(8192 + 1024 * k, [512], BF16) for k in range(2)]
        X1 = [c3(10240 + 1024 * k, [512], BF16) for k in range(2)]
        HS = [c3(12288 + 2048 * k, [1024], BF16) for k in range(2)]
        HW = [c3(16384 + 2816 * k, [1408], BF16) for k in range(2)]
        HC = c3(22016, [8, 512], BF16)
        IMP = c3(30208, [4, 128], F32)
        VCM = c3(32256, [4, 128], F32)
        ADDM = c3(34304, [4, 128], F32)
        PB = [c3(65728 + 2048 * k, [1024], BF16) for k in range(3)]
        OSB_S = c3(40448, [512], F32, 0, 67)
        OSB_W = c3(42496, [512], F32, 0, 67)
        OCSB = c3(44544, [512], F32, 0, 67)
        COEF = c3(46592, [512], F32, 64, 67)
        ZSW = c3(48640, [512], F32, 64, 67)
        SZO = [c3(50688 + 2048 * k, [512], BF16, 0, 64) for k in range(2)]
        MXO = [c3(50688 + 2048 * k + 1024, [512], BF16, 0, 64) for k in range(2)]
        GST = [c3(50688 + 2048 * k, [512], F32, 64, 67) for k in range(2)]
        GST1 = c3(54784, [512], F32, 64, 65)
        SCORE = c3(56832, [4, 128], F32)
        WORK = c3(58880, [4, 128], F32)
        TOP = c3(60928, [4, 16], F32)
        MNB_N = c3(61184, [4, 128], BF16)
        MNB_S = c3(62208, [4, 128], BF16)
        RZT = c3(63232, [4], F32)
        COEFH = c3(63296, [512], BF16, 64, 67)
        COEFL = c3(64320, [512], BF16, 64, 67)
        SELMB = c3(65344, [3, 64], BF16, 0, 67)

        SC = [ps_sc[0], ps_sc[1]]
        OCP = ps_m[2][0:67, :]
        IMPP = ps_m[3][:, :]
        Bsc2 = [Buf("sc0"), Buf("sc1")]
        Bocp = Buf("ocp")
        Bimpp = Buf("impp")

        Bcc = Buf("cconst")
        c_c = newchan("c_cconst")
        vcopy("pool", SELMB, SELM, reads=[Bc], writes=[Bcc])
        Bhc = Buf("hc")
        c_hc = newchan("c_hc")
        for h in range(8):
            dma("sp", HC[:, h, :], bass.AP(fd_d.tensor, h * LEN_F + OFF_C, [[16, 128], [1, 512]]), c_hc, writes=[Bhc])

        Bp = [Buf("p%d" % k) for k in range(3)]
        Bx = [Buf("x%d" % k) for k in range(2)]
        Bhsw = [Buf("hsw%d" % k) for k in range(2)]
        Bimp = Buf("imp")
        Bmask = Buf("mask")
        Bmt = [Buf("mt0"), Buf("mt1")]
        Bac = [[Buf("ac%d_%d" % (p_, k)) for k in range(4)] for p_ in range(2)]
        Bgst = [Buf("gst%d" % k) for k in range(2)]
        Bgst1 = Buf("gst1")
        Bszo = [Buf("szo%d" % k) for k in range(2)]
        Bmxo = [Buf("mxo%d" % k) for k in range(2)]
        Bosb = {n: Buf(n) for n in ("s", "w", "c", "coef", "coefh", "coefl", "zsw", "zsw1", "cf1h", "cf1l",
                                    "score", "work", "top", "mnb", "mnbs", "rzt")}
        Bszp = [Buf("szp%d" % k) for k in range(8)]
        c_tab = [newchan("c_tab%d" % k) for k in range(2)]
        c_mask = newchan("c_mask")
        c_gst = [newchan("c_gst%d" % k) for k in range(2)]
        c_gst1 = newchan("c_gstone")
        c_x2 = [newchan("c_xq%d" % k) for k in range(2)]
        c_szo = [newchan("c_szo%d" % k) for k in range(2)]
        c_mxo = [newchan("c_mxo%d" % k) for k in range(2)]

        class Pipe:
            def __init__(self):
                self.q = []
                self.tick = 0

            def _run_due(self):
                due = [f for (t_, f) in self.q if t_ <= self.tick]
                self.q = [(t_, f) for (t_, f) in self.q if t_ > self.tick]
                for f in due:
                    f()

            def push(self, qk_fn, later=()):
                self.tick += 1
                qk_fn()
                self._run_due()
                for (d_, f) in later:
                    self.q.append((self.tick + d_, f))

            def flush(self):
                while self.q:
                    self.tick += 1
                    self._run_due()

        pipe = Pipe()
        gctr = [0]

        pctr = [0]

        def push_group(it):
            grp = it["group"]
            n = gctr[0]
            gctr[0] += 1
            sc = SC[n % 2]
            Bs_ = Bsc2[n % 2]
            pn = pctr[0]
            pctr[0] += 1
            P_ = PB[pn % 3]
            Bp_ = Bp[pn % 3]
            h = grp[0]["h"]
            width = 512 * len(grp)

            rng = [tl.get("cols", (0, 512)) for tl in grp]
            if len(grp) == 2:
                assert rng[0][1] == 512 and rng[1][0] == 0, rng
            e0 = rng[0][0]
            e1 = 512 * (len(grp) - 1) + rng[-1][1]

            def qk():
                for idx, tl in enumerate(grp):
                    lo, hi = rng[idx]
                    o = sc[:, idx * 512 + lo:idx * 512 + hi]
                    nm = len(tl["mms"])
                    for mi, (l_, r_) in enumerate(tl["mms"]):
                        mm(o, l_, r_[:, lo:hi], mi == 0, mi == nm - 1, reads=tl["reads"], writes=[Bs_])
                act(P_[:, e0:e1], sc[:, e0:e1], AF.Exp, reads=[Bs_, Bc], writes=[Bp_], bias=B31[:, h:h + 1])

            def pv():
                for idx, tl in enumerate(grp):
                    lo, hi = rng[idx]
                    tl["pv"](P_[:, idx * 512 + lo:idx * 512 + hi], Bp_, lo, hi)
            pipe.push(qk, [(1, pv)] + list(it.get("after", [])))

        def special(fn):
            n = gctr[0]
            gctr[0] += 1
            fn(SC[n % 2], Bsc2[n % 2])

        def to_groups(tiles, after):
            out = [dict(group=tiles[i:i + 2]) for i in range(0, len(tiles), 2)]
            out[-1]["after"] = after
            return out

        def run_seq(seq):
            for it in seq:
                if "call" in it:
                    it["call"]()
                else:
                    push_group(it)

        def merge(seq3, inserts):
            out = []
            ti = 0
            inserts = sorted(inserts, key=lambda x: x[0])
            ii = 0
            for it in seq3:
                while ii < len(inserts) and "call" not in it and inserts[ii][0] <= ti:
                    out.extend(inserts[ii][1])
                    ii += 1
                out.append(it)
                if "call" not in it:
                    ti += 1
            while ii < len(inserts):
                out.extend(inserts[ii][1])
                ii += 1
            return out

        insts = [(g, s) for g in range(2) for s in range(4)]
        hctr = [0]

        def stage1_head(idx, r, delays=(1, 3, 8), gst=None):
            GST1_, Bgst1_, c_gst1_ = gst if gst is not None else (GST1, Bgst1, c_gst1)
            g, s = insts[idx]
            p_ = idx % 2
            h = 4 * g + r
            q = h // 2
            pr = slice(64 * (h % 2), 64 * (h % 2) + 64)
            qcols = slice(s * 512, (s + 1) * 512)
            QT_h = QTP[pr, q, qcols]
            items = [dict(call=lambda: dma("sp", GST1_, G24[3 * h:3 * h + 1, qcols], c_gst1_, writes=[Bgst1_]))]
            tiles = []
            pend = []
            for ct in range(s + 1):
                mms = [(KCT[pr, g, ct * 128:(ct + 1) * 128], QT_h)]
                if ct == s:
                    mms.append((ANTI, HC[:, h, :]))

                def pvf(Ph, Bp_, lo, hi, ct=ct):
                    mm(OCP, VC[:, ct, g, 0:67], Ph, ct == 0, ct == s, reads=[Bp_], writes=[Bocp])
                    pend.append((ct, Ph, Bp_))
                    if ct == s:
                        for u in range(4):
                            for (c2, Ph2, Bp2) in pend:
                                mm(IMPP[:, u * 128:(u + 1) * 128], Ph2[:, u * 128:(u + 1) * 128], OV[:, c2, :],
                                   c2 == 0, c2 == s, reads=[Bp2], writes=[Bimpp])
                tiles.append(dict(mms=mms, reads=[Bhc], pv=pvf, h=h))

            def fin1():
                IMPRAW = WORK.rearrange("p u j -> p (u j)")
                vcopy("dve", OCSB, OCP, reads=[Bocp], writes=[Bosb["c"]])
                vcopy("dve", IMPRAW, IMPP, reads=[Bimpp], writes=[Bosb["work"]])
                ts("dve", RZT, WORK[:, :, 127], 1e-30, ALU.max, reads=[Bosb["work"]], writes=[Bosb["rzt"]])
                ts("dve", ZSW1, OCSB[64:65, :], 1e-30, ALU.max, reads=[Bosb["c"]], writes=[Bosb["zsw1"]])
                S.op("dve", lambda e: e.reciprocal(out=RZT, in_=RZT), reads=[Bosb["rzt"]], writes=[Bosb["rzt"]])
                for u in range(4):
                    if r == 0:
                        ts("dve", IMP[:, u, :], WORK[:, u, :], RZT[:, u:u + 1], ALU.mult,
                           reads=[Bosb["work"], Bosb["rzt"]], writes=[Bimp])
                    else:
                        stt("dve", IMP[:, u, :], WORK[:, u, :], RZT[:, u:u + 1], IMP[:, u, :],
                            ALU.mult, ALU.add, reads=[Bosb["work"], Bosb["rzt"], Bimp], writes=[Bimp])

            def fin1b():
                act(ZSW1, ZSW1, AF.Ln, reads=[Bosb["zsw1"]], writes=[Bosb["zsw1"]])
                act(ZSW1, ZSW1, AF.Exp, reads=[Bosb["zsw1"]], writes=[Bosb["zsw1"]], scale=-1.0)
                tt("dve", ZSW1, ZSW1, GST1_, ALU.mult, reads=[Bosb["zsw1"], Bgst1_], writes=[Bosb["zsw1"]])
                vcopy("dve", CF1H, ZSW1, reads=[Bosb["zsw1"]], writes=[Bosb["cf1h"]])
                tt("dve", CF1L, ZSW1, CF1H, ALU.subtract, reads=[Bosb["zsw1"], Bosb["cf1h"]], writes=[Bosb["cf1l"]])

            def fin2():
                bcp = ps_m[3][0:64, :]
                mm(bcp, SELMB[64:65, 0, :], CF1H, True, False, reads=[Bosb["cf1h"], Bcc], writes=[Bimpp])
                mm(bcp, SELMB[64:65, 0, :], CF1L, False, True, reads=[Bosb["cf1l"], Bcc], writes=[Bimpp])
                tt("dve", AC[p_][:, r, :], OCSB[0:64, :], bcp, ALU.mult,
                   reads=[Bosb["c"], Bimpp], writes=[Bac[p_][r]])
            items += to_groups(tiles, [(delays[0], fin1), (delays[1], fin1b), (delays[2], fin2)])
            return items

        def stage2_dve(idx):
            g, s = insts[idx]
            dma("sp", VCM, vcm_d[s, :, :, :], c_mask, writes=[Bmask])
            dma("sp", ADDM, addm_d[s, :, :, :], c_mask, writes=[Bmask])
            tt("dve", SCORE, IMP, VCM, ALU.mult, reads=[Bimp, Bmask], writes=[Bosb["score"]])
            tt("dve", SCORE, SCORE, ADDM, ALU.add, reads=[Bosb["score"], Bmask], writes=[Bosb["score"]])
            Btop = [Buf("top%d" % u) for u in range(4)]
            Bwk = [Buf("wk%d" % u) for u in range(4)]
            for u in range(4):
                S.op("dve", lambda e, u=u: e.max(out=TOP[:, u, 0:8], in_=SCORE[:, u, :]),
                     reads=[Bosb["score"], Bosb["work"], Bosb["top"]], writes=[Btop[u]])
            for u in range(4):
                S.op("dve", lambda e, u=u: e.match_replace(out=WORK[:, u, :], in_to_replace=TOP[:, u, 0:8],
                                                         in_values=SCORE[:, u, :], imm_value=-1e30),
                     reads=[Bosb["score"], Btop[u]], writes=[Bwk[u]])
            for u in range(4):
                S.op("dve", lambda e, u=u: e.max(out=TOP[:, u, 8:16], in_=WORK[:, u, :]),
                     reads=[Bwk[u]], writes=[Btop[u]])
            for u in range(4):
                ts("dve", WORK[:, u, :], SCORE[:, u, :], TOP[:, u, 15:16], ALU.is_ge,
                   reads=[Bosb["score"], Btop[u], Bwk[u]], writes=[Bwk[u]])
            tt("dve", WORK, WORK, VCM, ALU.mult, reads=Bwk + [Bosb["work"], Bmask], writes=Bwk + [Bosb["work"], Bosb["top"]])
            ts("dve", MNB_N, WORK, -1.0, ALU.add, -NEGBIG, ALU.mult, reads=[Bosb["work"]], writes=[Bosb["mnb"]])
            ts("dve", MNB_S[:, :, 0:64], WORK[:, :, 64:128], -1.0, ALU.add, -NEGBIG, ALU.mult,
               reads=[Bosb["work"]], writes=[Bosb["mnbs"]])
            ts("dve", MNB_S[:, :, 64:128], WORK[:, :, 0:64], -1.0, ALU.add, -NEGBIG, ALU.mult,
               reads=[Bosb["work"]], writes=[Bosb["mnbs"]])

        def stage2_pe(idx):
            p_ = idx % 2
            pst = ps_m[2][:, 0:512].bitcast(BF16)
            for u in range(4):
                tr(pst[:, u * 128:(u + 1) * 128], MNB_N[:, u, :], reads=[Bosb["mnb"], Bc], writes=[Bocp])
                tr(pst[:, 512 + u * 128:512 + (u + 1) * 128], MNB_S[:, u, :], reads=[Bosb["mnbs"], Bc], writes=[Bocp])
            vcopy("dve", MT1[p_], pst[64:128, 0:512], reads=[Bocp], writes=[Bmt[p_]])
            vcopy("dve", MT0[p_], pst[64:128, 512:1024], reads=[Bocp], writes=[Bmt[p_]])

        def stage3_head(idx, r):
            g, s = insts[idx]
            p_ = idx % 2
            T = 3 + 4 * s
            h = 4 * g + r
            q = h // 2
            half = h % 2
            pr = slice(64 * half, 64 * half + 64)
            qcols = slice(s * 512, (s + 1) * 512)
            k_ = hctr[0] % 2
            hctr[0] += 1
            QT_h = QTP[pr, q, qcols]

            def setup():
                dma("sp", GST[k_], G24[3 * h:3 * h + 3, qcols], c_gst[k_], writes=[Bgst[k_]])
                dma("sp", HS[k_], bass.AP(fd_d.tensor, h * LEN_F, [[1, 128], [1, 1024]]), c_tab[k_], writes=[Bhsw[k_]])
                dma("sp", HW[k_], bass.AP(fd_d.tensor, h * LEN_F + OFF_W, [[1, 128], [1, 1408]]), c_tab[k_], writes=[Bhsw[k_]])
                if half == 0:
                    vcopy("dve", X0[k_][0:64, :], QTP[0:64, q, qcols], writes=[Bx[k_]])
                    vcopy("dve", X1[k_][0:64, :], QTP[0:64, q, qcols], writes=[Bx[k_]])
                else:
                    dma("pool", X0[k_][0:64, :], QTP[64:128, q, qcols], c_x2[k_], writes=[Bx[k_]])
                    dma("pool", X1[k_][0:64, :], QTP[64:128, q, qcols], c_x2[k_], writes=[Bx[k_]])
                    dma("pool", SZO[k_], SZP[64:128, q, qcols], c_szo[k_], reads=[Bszp[h]], writes=[Bszo[k_]])
                vcopy("dve", X0[k_][64:128, :], MT0[p_], reads=[Bmt[p_]], writes=[Bx[k_]])
                vcopy("dve", X1[k_][64:128, :], MT1[p_], reads=[Bmt[p_]], writes=[Bx[k_]])
            items = [dict(call=setup)]
            nsel = 16 * (s + 1)

            def sel_tile(kt, first, last):
                Xs = X0[k_] if kt < 32 else X1[k_]
                mms = [(KAUG[:, g, kt * 128:(kt + 1) * 128], Xs)]
                delta = 128 * kt - 512 * T
                cols = (0, 512)
                if delta >= -128:
                    col0 = 384 - delta
                    mms.append((ANTI, HS[k_][:, col0:col0 + 512]))
                    if delta > 0:
                        cols = (delta, 512)

                def pvf(Ph, Bp_, lo, hi, kt=kt):
                    mm(ps_m[0][0:67, lo:hi], VS[:, kt, g, 0:67], Ph, first, last, reads=[Bp_], writes=[Bps[0]])
                return dict(mms=mms, reads=[Bx[k_], Bhsw[k_]], pv=pvf, h=h, cols=cols)

            def win_tile(delta, first, last):
                kt = (512 * T + delta) // 128
                slot = 8 * s + (kt - (4 * T - 4))
                col0 = 384 - delta
                mms = [(KWIN[pr, g, slot * 128:(slot + 1) * 128], QT_h), (ANTI, HW[k_][:, col0:col0 + 512])]
                lo = max(0, delta)
                hi = min(512, ((delta + 638) // 128 + 1) * 128)
                hi = min(512, max(hi, 128))

                def pvf(Ph, Bp_, lo_, hi_, slot=slot):
                    mm(ps_m[1][0:67, lo_:hi_], VW[:, slot, g, 0:67], Ph, first, last, reads=[Bp_], writes=[Bps[1]])
                return dict(mms=mms, reads=[Bhsw[k_]], pv=pvf, h=h, cols=(lo, hi))

            far = list(range(0, 4 * T - 1))
            order = [far[0], far[1], 4 * T + 1, far[2], 4 * T + 2, far[3], 4 * T + 3, far[4], 4 * T - 1, 4 * T] + far[5:]
            assert len(order) == nsel and order[-1] in far and len(set(order)) == nsel
            tiles = [sel_tile(kt, i_ == 0, i_ == nsel - 1) for i_, kt in enumerate(order)]
            worder = [0, -256, 256, -384, 384, -512, 128, -128]
            tiles += [win_tile(d_, i_ == 0, i_ == 7) for i_, d_ in enumerate(worder)]

            def fin1():
                vcopy("dve", OSB_S, ps_m[0][0:67, :], reads=[Bps[0]], writes=[Bosb["s"]])
                vcopy("dve", OSB_W, ps_m[1][0:67, :], reads=[Bps[1]], writes=[Bosb["w"]])
                tt("dve", ZSW, OSB_S[64:67, :], OSB_W[64:67, :], ALU.add, reads=[Bosb["s"], Bosb["w"]], writes=[Bosb["zsw"]])
                S.op("dve", lambda e: e.reciprocal(out=ZSW, in_=ZSW), reads=[Bosb["zsw"]], writes=[Bosb["zsw"]])
                tt("dve", COEF, ZSW, GST[k_], ALU.mult, reads=[Bosb["zsw"], Bgst[k_]], writes=[Bosb["coef"]])
                vcopy("dve", COEFH, COEF, reads=[Bosb["coef"]], writes=[Bosb["coefh"]])
                tt("dve", COEFL, COEF, COEFH, ALU.subtract, reads=[Bosb["coef"], Bosb["coefh"]], writes=[Bosb["coefl"]])

            def bc_mm(br):
                bcp = ps_m[3][0:64, :]
                mm(bcp, SELMB[64:67, br, :], COEFH, True, False, reads=[Bosb["coefh"], Bcc], writes=[Bimpp])
                mm(bcp, SELMB[64:67, br, :], COEFL, False, True, reads=[Bosb["coefl"], Bcc], writes=[Bimpp])

            def fin2():
                bc_mm(1)
                tt("dve", OSB_S[0:64, :], OSB_S[0:64, :], ps_m[3][0:64, :], ALU.mult,
                   reads=[Bosb["s"], Bimpp], writes=[Bosb["s"]])

            def fin3():
                bc_mm(2)
                tt("dve", OSB_W[0:64, :], OSB_W[0:64, :], ps_m[3][0:64, :], ALU.mult,
                   reads=[Bosb["w"], Bimpp], writes=[Bosb["w"]])
                tt("dve", OSB_S[0:64, :], OSB_S[0:64, :], OSB_W[0:64, :], ALU.add,
                   reads=[Bosb["s"], Bosb["w"]], writes=[Bosb["s"]])
                tt("dve", OSB_S[0:64, :], OSB_S[0:64, :], AC[p_][:, r, :], ALU.add,
                   reads=[Bosb["s"], Bac[p_][r]], writes=[Bosb["s"]])
                if half == 0:
                    tt("dve", SZP[0:64, q, qcols], OSB_S[0:64, :], SZP[0:64, q, qcols], ALU.mult,
                       reads=[Bosb["s"], Bszp[h]], writes=[Bszp[h]])
                else:
                    tt("dve", MXO[k_], OSB_S[0:64, :], SZO[k_], ALU.mult, reads=[Bosb["s"], Bszo[k_]], writes=[Bmxo[k_]])
                    dma("pool", SZP[64:128, q, qcols], MXO[k_], c_mxo[k_], reads=[Bmxo[k_]], writes=[Bszp[h]])
            items += to_groups(tiles, [(1, fin1), (9, fin2), (11, fin3)])
            return items

        seq = []
        for r in range(4):
            seq += stage1_head(0, r, delays=(1, 1, 2), gst=(GST[r % 2][0:1, :], Bgst[r % 2], c_gst[r % 2]))
        run_seq(seq)
        pipe.flush()
        stage2_dve(0)
        stage2_pe(0)
        for idx in range(8):
            seq3 = []
            for r in range(4):
                seq3 += stage3_head(idx, r)
            if idx + 1 < 8:
                n3 = sum(1 for it in seq3 if "call" not in it)
                ins = []
                step = max(11, int(0.12 * n3))
                for r in range(4):
                    ins.append((1 + r * step, stage1_head(idx + 1, r)))
                p_dve = max(int(0.60 * n3), 1 + 3 * step + 2 + 4)
                ins.append((p_dve, [dict(call=lambda idx=idx: stage2_dve(idx + 1))]))
                ins.append((p_dve + 8, [dict(call=lambda idx=idx: stage2_pe(idx + 1))]))
                seq3 = merge(seq3, ins)
            run_seq(seq3)
        pipe.flush()
        barrier()
        if DEBUG == "C":
            dump("szp", SZP, BF16)
            finish()
            return nc, dbg

        WOA = c3(0, [4, 1024], BF16)
        WOC = c3(8192, [4, 1024], BF16)
        WOS = [c3(16384 + 4096 * k, [1024], F32) for k in range(4)]
        XR = [c3(32768 + 4096 * k, [1024], F32) for k in range(3)]
        YT = [c3(45056 + 4096 * k, [1024], F32) for k in range(4)]
        FNW = c3(61440, [1024], F32)
        JUNKF = c3(65536, [1024], F32)
        c_wo = [newchan("c_wo%d" % k) for k in range(4)]
        c_xr = [newchan("c_xr%d" % k) for k in range(3)]
        c_out = [newchan("c_out%d" % k) for k in range(4)]
        c_fnw = newchan("c_fnw")
        Bwos = [Buf("wos%d" % k) for k in range(4)]
        Bwo = Buf("wo")
        Bxr = [Buf("xr%d" % k) for k in range(3)]
        Byt = [Buf("yt%d" % k) for k in range(4)]
        Bssd = [Buf("ssd%d" % k) for k in range(8)]
        Bfnw = Buf("fnw")
        dma("sp", FNW, bass.AP(fnw_d.tensor, 0, [[0, 128], [1, 1024]]), c_fnw, writes=[Bfnw])
        for rc in range(8):
            k_ = rc % 4
            dma("sp", WOS[k_], wout_d[128 * rc:128 * rc + 128, :], c_wo[k_], writes=[Bwos[k_]])
            dst = WOA[:, rc, :] if rc < 4 else WOC[:, rc - 4, :]
            vcopy("dve", dst, WOS[k_], reads=[Bwos[k_]], writes=[Bwo] if rc in (0, 7) else [])

        def d_st1(n):
            s, sub = n // 4, n % 4
            kp, kx, ky = n % 2, n % 3, n % 4
            tc0 = s * 512 + sub * 128
            L0 = 512 * (3 + 4 * s) + 128 * sub
            dma("pool", XR[kx], x_d[L0:L0 + 128, :], c_xr[kx], writes=[Bxr[kx]])
            for hf in range(2):
                pso = ps_sc[kp][:, hf * 512:(hf + 1) * 512]
                for q in range(4):
                    mm(pso, SZP[:, q, tc0:tc0 + 128], WOA[:, q, hf * 512:(hf + 1) * 512], q == 0, False,
                       reads=[Bwo], writes=[Bsc[kp]])
                for ch in range(4):
                    mm(pso, MC[:, ch, tc0:tc0 + 128], WOC[:, ch, hf * 512:(hf + 1) * 512], False, ch == 3,
                       reads=[Bwo], writes=[Bsc[kp]])
            tt("dve", YT[ky], ps_sc[kp][:, :], XR[kx], ALU.add, reads=[Bsc[kp], Bxr[kx]], writes=[Byt[ky]])

        def d_sq(n):
            ss = SMALL[:, (n % 8) * 2:(n % 8) * 2 + 1]
            act(JUNKF, YT[n % 4], AF.Square, reads=[Byt[n % 4]], writes=[Bssd[n % 8]], accum_out=ss)

        def d_st2a(n):
            ss = SMALL[:, (n % 8) * 2:(n % 8) * 2 + 1]
            rs = SMALL[:, (n % 8) * 2 + 1:(n % 8) * 2 + 2]
            ts("dve", rs, ss, 1.0 / DM, ALU.mult, EPS, ALU.add, reads=[Bssd[n % 8]], writes=[Bssd[n % 8]])
            act(rs, rs, AF.Sqrt, reads=[Bssd[n % 8]], writes=[Bssd[n % 8]])

        def d_st2b(n):
            s, sub = n // 4, n % 4
            ky = n % 4
            tc0 = s * 512 + sub * 128
            rs = SMALL[:, (n % 8) * 2 + 1:(n % 8) * 2 + 2]
            S.op("dve", lambda e, rs=rs: e.reciprocal(out=rs, in_=rs), reads=[Bssd[n % 8]], writes=[Bssd[n % 8]])
            stt("dve", YT[ky], YT[ky], rs, FNW, ALU.mult, ALU.mult, reads=[Byt[ky], Bssd[n % 8], Bfnw], writes=[Byt[ky]])
            dma("sp", out_d[tc0:tc0 + 128, :], YT[ky], c_out[ky], reads=[Byt[ky]])

        d_st1(0)
        d_sq(0)
        for n in range(16):
            if n + 1 < 16:
                d_st1(n + 1)
            d_st2a(n)
            if n + 1 < 16:
                d_sq(n + 1)
            d_st2b(n)
        finish()
    return nc, dbg


_PROG = {}


def _get_program():
    if "p" not in _PROG:
        _PROG["p"] = build_program()
    return _PROG["p"]


def make_in_maps(x, norm_w, w_in, w_ck1, w_ck2, pe_k, w_cv1, w_cv2, pe_v, conv_w, w_out, rel_bias, final_norm_w):
    f32 = np.float32
    x = np.asarray(x, f32)
    w_in0 = np.ascontiguousarray(np.asarray(w_in, f32)[0])
    shared = dict(_shared_consts())
    shared.pop("ov_base")
    gate_cols = [1280 + br * 8 + h for h in range(8) for br in range(3)]
    shared.update(
        w_in=w_in0,
        w_gate=np.ascontiguousarray(w_in0[:, gate_cols]),
        nw=np.ascontiguousarray(np.asarray(norm_w, f32)[0].reshape(8, 128).T),
        w1k=np.ascontiguousarray(np.asarray(w_ck1, f32)[0].transpose(1, 0, 2)),
        w1v=np.ascontiguousarray(np.asarray(w_cv1, f32)[0].transpose(1, 0, 2)),
        w2k=np.ascontiguousarray(np.asarray(w_ck2, f32)[0]),
        w2v=np.ascontiguousarray(np.asarray(w_cv2, f32)[0]),
        pekT=np.ascontiguousarray(np.asarray(pe_k, f32)[0].T),
        pevT=np.ascontiguousarray(np.asarray(pe_v, f32)[0].T),
        cw=np.ascontiguousarray(np.asarray(conv_w, f32)[0].T.reshape(4, 128, 3).transpose(1, 0, 2).reshape(128, 12)),
        w_out=np.ascontiguousarray(np.asarray(w_out, f32)[0]),
        rel_bias=np.ascontiguousarray(np.asarray(rel_bias, f32)),
        fnw=np.ascontiguousarray(np.asarray(final_norm_w, f32)[None, :]),
    )
    in_maps = []
    for core in range(8):
        b, i = core // 4, core % 4
        pad = 1536 - 512 * i
        xc = np.zeros((SEQ, DM), f32)
        xc[pad:] = x[b, :SEQ - pad]
        m = dict(shared)
        m.update(_core_consts(i))
        m["x"] = xc
        in_maps.append(m)
    return in_maps


def kernel(x, norm_w, w_in, w_ck1, w_ck2, pe_k, w_cv1, w_cv2, pe_v, conv_w, w_out, rel_bias, final_norm_w):
    nc, _ = _get_program()
    in_maps = make_in_maps(x, norm_w, w_in, w_ck1, w_ck2, pe_k, w_cv1, w_cv2, pe_v, conv_w, w_out, rel_bias,
                           final_norm_w)
    res = run_bass_kernel_spmd(nc, in_maps, core_ids=list(range(8)))
    out = np.zeros((2, SEQ, DM), np.float32)
    for core in range(8):
        b, i = core // 4, core % 4
        o = res.results[core]["out"]
        for s in range(4):
            a = 4 * s + i
            out[b, 512 * a:512 * a + 512] = o[512 * s:512 * s + 512]
    return out
```

```python
import os
from contextlib import ExitStack

import numpy as np
import ml_dtypes

import concourse.bass as bass
import concourse.mybir as mybir
from concourse.bass_utils import run_bass_kernel_spmd

F32 = mybir.dt.float32
BF16 = mybir.dt.bfloat16
AF = mybir.ActivationFunctionType
ALU = mybir.AluOpType
NPBF = ml_dtypes.bfloat16

NEGBIG = -30000.0
EPS = 1e-6
SEQ = 8192
DM = 1024
NCOLS = 3864
LEN_S, LEN_W, LEN_C = 1151, 1535, 2544
LEN_F = LEN_S + LEN_W + LEN_C
OFF_W = LEN_S
OFF_C = LEN_S + LEN_W

DEBUG = os.environ.get("KDEBUG", "")
KCUT = int(os.environ.get("KCUT", "0"))


class Buf:
    __slots__ = ("name", "w", "r")

    def __init__(self, name=""):
        self.name = name
        self.w = None
        self.r = []


class Chan:
    __slots__ = ("sem", "n")

    def __init__(self, sem):
        self.sem = sem
        self.n = 0


class Op:
    __slots__ = ("eng", "fn", "deps", "chan", "chan_n", "sig", "signo", "is_dma", "chan_waits")

    def __init__(self, eng, fn, is_dma):
        self.eng = eng
        self.fn = fn
        self.deps = []
        self.chan = None
        self.chan_n = 0
        self.sig = False
        self.signo = 0
        self.is_dma = is_dma
        self.chan_waits = None


class Sched:
    ENGS = ("pe", "act", "dve", "pool", "sp")

    def __init__(self, nc):
        self.nc = nc
        self.ops = {e: [] for e in self.ENGS}
        self.all_ops = []
        self.cur_barrier = None
        self.chans = []

    def new_chan(self, sem):
        c = Chan(sem)
        self.chans.append(c)
        return c

    def _dep(self, op, other):
        if other is None or other is op:
            return
        if (not other.is_dma) and (not op.is_dma) and other.eng == op.eng and op.eng == "pe":
            return
        if other not in op.deps:
            op.deps.append(other)

    def op(self, eng, fn, reads=(), writes=(), chan=None):
        is_dma = chan is not None
        o = Op(eng, fn, is_dma)
        if self.cur_barrier is not None:
            o.deps.append(self.cur_barrier)
        for b in reads:
            self._dep(o, b.w)
        for b in writes:
            self._dep(o, b.w)
            for r in b.r:
                self._dep(o, r)
        for b in reads:
            b.r.append(o)
        for b in writes:
            b.w = o
            b.r = []
        if is_dma:
            chan.n += 1
            o.chan = chan
            o.chan_n = chan.n
        self.ops[eng].append(o)
        self.all_ops.append(o)
        return o

    def barrier(self, fn, chan):
        o = Op("sp", fn, True)
        for e in self.ENGS:
            for prev in reversed(self.ops[e]):
                if not prev.is_dma:
                    o.deps.append(prev)
                    break
        o.chan_waits = [(c, c.n) for c in self.chans if c.n > 0]
        chan.n += 1
        o.chan = chan
        o.chan_n = chan.n
        self.ops["sp"].append(o)
        self.all_ops.append(o)
        self.cur_barrier = o
        return o

    def emit(self, sems):
        nc = self.nc
        for o in self.all_ops:
            for d in o.deps:
                if not d.is_dma:
                    d.sig = True
        for e in self.ENGS:
            n = 0
            for o in self.ops[e]:
                if o.sig and not o.is_dma:
                    n += 1
                    o.signo = n
        all_chans = self.chans

        def run_engine(ename, eng):
            seen = {}
            for o in self.ops[ename]:
                need = {}
                for d in o.deps:
                    if d.is_dma:
                        key = ("c", id(d.chan))
                        val = 16 * d.chan_n
                        sem = d.chan.sem
                    else:
                        key = ("e", d.eng)
                        val = d.signo
                        sem = sems[d.eng]
                    if val > need.get(key, (0, None))[0]:
                        need[key] = (val, sem)
                if o.chan_waits:
                    for c, n in o.chan_waits:
                        key = ("c", id(c))
                        if 16 * n > need.get(key, (0, None))[0]:
                            need[key] = (16 * n, c.sem)
                for key, (val, sem) in need.items():
                    if seen.get(key, 0) >= val:
                        continue
                    seen[key] = val
                    eng.wait_ge(sem, val)
                ins = o.fn(eng)
                if o.is_dma:
                    ins.then_inc(o.chan.sem, 16)
                elif o.sig:
                    ins.then_inc(sems[ename], 1)
            if ename == "sp":
                for c in all_chans:
                    if c.n > 0 and seen.get(("c", id(c)), 0) < 16 * c.n:
                        eng.wait_ge(c.sem, 16 * c.n)
                for e2 in self.ENGS:
                    last = 0
                    for o2 in self.ops[e2]:
                        if o2.sig and not o2.is_dma:
                            last = o2.signo
                    if last > 0:
                        eng.wait_ge(sems[e2], last)

        with nc.Block() as block:
            @block.tensor
            def _(eng):
                run_engine("pe", eng)

            @block.scalar
            def _(eng):
                run_engine("act", eng)

            @block.vector
            def _(eng):
                run_engine("dve", eng)

            @block.gpsimd
            def _(eng):
                run_engine("pool", eng)

            @block.sync
            def _(eng):
                run_engine("sp", eng)


def _t5_bucket(d):
    n = np.maximum(d, 0)
    nf = np.maximum(n, 1).astype(np.float32)
    large = 16 + (np.log(nf / np.float32(16.0)) / np.float32(np.log(8.0)) * np.float32(16.0)).astype(np.int32)
    large = np.minimum(large, 31)
    return np.where(n < 16, n, large)


def _onehot_rows():
    oh = np.zeros((33, LEN_F), np.float32)

    def fill(off, length, i0, wmax):
        d = np.arange(length) - i0
        masked = d < 0
        if wmax is not None:
            masked |= d >= wmax
        bk = _t5_bucket(d)
        for i in range(length):
            if masked[i]:
                oh[32, off + i] = 1.0
            else:
                oh[bk[i], off + i] = 1.0

    fill(0, LEN_S, 511, None)
    fill(OFF_W, LEN_W, 511, 512)
    fill(OFF_C, LEN_C, 527, None)
    return oh


_CONST_CACHE = {}


def _shared_consts():
    if "c" in _CONST_CACHE:
        return _CONST_CACHE["c"]
    ident = np.eye(128, dtype=np.float32).astype(NPBF)
    antiI = np.eye(128, dtype=np.float32)[::-1].copy().astype(NPBF)
    L = np.arange(SEQ)
    eall = np.zeros((64, SEQ), np.float32)
    eall[(L // 64) % 64, L] = 1.0
    c = np.arange(512)
    j = np.arange(128)
    ov = ((16 * c[:, None] < 64 * j[None, :] + 64) & (16 * c[:, None] + 32 > 64 * j[None, :])).astype(np.float32)
    ov[511, :] = 0.0
    ov = ov.reshape(4, 128, 128).transpose(1, 0, 2).copy()
    sel = np.zeros((67, 3, 64), np.float32)
    for br in range(3):
        sel[64 + br, br, :] = 1.0
    d = dict(ident=ident, antiI=antiI, eall=eall.astype(NPBF), ov_base=ov, oh=_onehot_rows(), selm=sel)
    _CONST_CACHE["c"] = d
    return d


def _core_consts(i):
    pad = 1536 - 512 * i
    jmin = pad // 64
    kinv = (np.arange(1536) < pad).astype(np.float32)[None].astype(NPBF)
    cinv = (16 * np.arange(128) < pad).astype(np.float32)[None].astype(NPBF)
    vcm = np.zeros((4, 128, 4, 128), np.float32)
    addm = np.zeros((4, 128, 4, 128), np.float32)
    jj = np.arange(128)[None, :]
    for s in range(4):
        T = 3 + 4 * s
        for u in range(4):
            tL = 512 * T + 128 * u + np.arange(128)
            blk = (tL // 64)[:, None]
            valid = jj >= jmin
            vc = valid & (jj <= blk)
            forced = ((jj == jmin) | (jj == blk) | (jj == blk - 1)) & vc
            vcm[s, :, u, :] = vc
            addm[s, :, u, :] = (vc.astype(np.float32) - 1.0) + 2000.0 * forced
    cval = ((16 * np.arange(512) >= pad) & (np.arange(512) <= 510)).astype(np.float32)
    cval_pc = cval.reshape(4, 128).T
    ov = _shared_consts()["ov_base"].copy()
    ov[:, :, 127] = 1.0
    ov = ov * cval_pc[:, :, None]
    kvw = np.zeros((128, 32), np.float32)
    for s in range(4):
        for wi in range(8):
            kt = 8 + 16 * s + wi
            kvw[:, 8 * s + wi] = (128 * kt + np.arange(128)) >= pad
    return dict(vcm=vcm, addm=addm, ov=ov.astype(NPBF), kvw=kvw, cvc=np.ascontiguousarray(cval_pc))


P1 = 0
O_KAUG = 0
O_VS = 32768
O_KWIN = 50176
O_VW = 66560
O_KCT = 75264
O_VC = 77312
P2 = 78464
P3 = P2 + 57344
P3_SIZE = 72704
PC = P3 + P3_SIZE
ARENA_BYTES = PC + 3584


def build_program(stop_after="D"):
    nc = bass.Bass("TRN2", target_bir_lowering=False)

    def din(name, shape, dt=F32):
        return nc.dram_tensor(name, list(shape), dt, kind="ExternalInput").ap()

    x_d = din("x", [SEQ, DM])
    win_d = din("w_in", [DM, NCOLS])
    wgate_d = din("w_gate", [DM, 24])
    nw_d = din("nw", [128, 8])
    w1k_d = din("w1k", [64, 32, 128])
    w1v_d = din("w1v", [64, 32, 128])
    w2k_d = din("w2k", [128, 64])
    w2v_d = din("w2v", [128, 64])
    pek_d = din("pekT", [64, 32])
    pev_d = din("pevT", [64, 32])
    cw_d = din("cw", [128, 12])
    wout_d = din("w_out", [DM, DM])
    relb_d = din("rel_bias", [32, 8])
    fnw_d = din("fnw", [1, DM])
    ident_d = din("ident", [128, 128], BF16)
    anti_d = din("antiI", [128, 128], BF16)
    eall_d = din("eall", [64, SEQ], BF16)
    ov_d = din("ov", [128, 4, 128], BF16)
    oh_d = din("oh", [33, LEN_F])
    selm_d = din("selm", [67, 3, 64])
    kvw_d = din("kvw", [128, 32])
    cvc_d = din("cvc", [128, 4])
    vcm_d = din("vcm", [4, 128, 4, 128])
    addm_d = din("addm", [4, 128, 4, 128])
    out_d = nc.dram_tensor("out", [2048, DM], F32, kind="ExternalOutput").ap()
    fd_d = nc.dram_tensor("fd_scr", [8, LEN_F], BF16, kind="Internal").ap()
    bar_d = nc.dram_tensor("bar_scr", [1, 64], F32, kind="Internal").ap()
    dbg = {}

    def dbg_out(name, shape, dt):
        t = nc.dram_tensor("dbg_" + name, list(shape), dt, kind="ExternalOutput").ap()
        dbg[name] = t
        return t

    with ExitStack() as es:
        arena = es.enter_context(nc.sbuf_tensor("arena", [128, ARENA_BYTES // 2], BF16))
        ps_sc = [es.enter_context(nc.psum_tensor("ps_sc%d" % k, [128, 1024], F32)) for k in range(2)]
        ps_m = [es.enter_context(nc.psum_tensor("ps_m%d" % k, [128, 512], F32)) for k in range(4)]
        sems = {e: es.enter_context(nc.semaphore("s_" + e)) for e in Sched.ENGS}
        S = Sched(nc)

        def newchan(name):
            return S.new_chan(es.enter_context(nc.semaphore(name)))

        bar_chan = newchan("c_bar")

        def carve(off, shape, dt, p0=0, p1=128):
            n = 1
            for d_ in shape:
                n *= d_
            nb = n * (4 if dt == F32 else 2)
            assert off % 4 == 0 and off + nb <= ARENA_BYTES, (off, nb)
            v = arena[p0:p1, off // 2: (off + nb) // 2]
            if dt == F32:
                v = v.bitcast(F32)
            if len(shape) == 2:
                v = v.rearrange("p (a b) -> p a b", a=shape[0])
            elif len(shape) == 3:
                v = v.rearrange("p (a b c) -> p a b c", a=shape[0], b=shape[1])
            return v

        def barrier():
            S.barrier(lambda e: e.dma_start(out=bar_d[0:1, 0:16], in_=fnw_d[0:1, 0:16]), bar_chan)

        def mm(out, lhsT, rhs, start, stop, reads=(), writes=()):
            return S.op("pe", lambda e: e.matmul(out, lhsT=lhsT, rhs=rhs, start=start, stop=stop), reads, writes)

        def tr(out, in_, reads=(), writes=()):
            return S.op("pe", lambda e: e.transpose(out=out, in_=in_, identity=IDENT), reads, writes)

        def act(out, in_, func, reads=(), writes=(), bias=None, scale=None, accum_out=None):
            kw = {}
            if bias is not None:
                kw["bias"] = bias
            if scale is not None:
                kw["scale"] = scale
            if accum_out is not None:
                kw["accum_out"] = accum_out
            return S.op("act", lambda e: e.activation(out=out, in_=in_, func=func, **kw), reads, writes)

        def vcopy(eng, out, in_, reads=(), writes=()):
            return S.op(eng, lambda e: e.tensor_copy(out=out, in_=in_), reads, writes)

        def tt(eng, out, in0, in1, op, reads=(), writes=()):
            return S.op(eng, lambda e: e.tensor_tensor(out=out, in0=in0, in1=in1, op=op), reads, writes)

        def ts(eng, out, in0, s1, op0, s2=None, op1=None, reads=(), writes=()):
            if op1 is None:
                return S.op(eng, lambda e: e.tensor_scalar(out=out, in0=in0, scalar1=s1, scalar2=None, op0=op0), reads, writes)
            return S.op(eng, lambda e: e.tensor_scalar(out=out, in0=in0, scalar1=s1, scalar2=s2, op0=op0, op1=op1), reads, writes)

        def stt(eng, out, in0, scalar, in1, op0, op1, reads=(), writes=()):
            return S.op(eng, lambda e: e.scalar_tensor_tensor(out=out, in0=in0, scalar=scalar, in1=in1, op0=op0, op1=op1), reads, writes)

        def memset(eng, ap, val, reads=(), writes=()):
            return S.op(eng, lambda e: e.memset(ap, val), reads, writes)

        def dma(eng, out, in_, chan, reads=(), writes=()):
            return S.op(eng, lambda e: e.dma_start(out=out, in_=in_), reads, writes, chan=chan)

        KAUG = carve(O_KAUG, [2, 8192], BF16)
        VS = carve(O_VS, [64, 2, 68], BF16)
        KWIN = carve(O_KWIN, [2, 4096], BF16)
        VW = carve(O_VW, [32, 2, 68], BF16)
        KCT = carve(O_KCT, [2, 512], BF16)
        VC = carve(O_VC, [4, 2, 68], BF16)
        QTP = carve(P2 + 0, [4, 2048], BF16)
        SZP = carve(P2 + 16384, [4, 2048], BF16)
        MC = carve(P2 + 32768, [4, 2048], BF16)
        G24 = carve(P2 + 49152, [2048], F32, 0, 24)
        IDENT = carve(PC + 0, [128], BF16)
        ANTI = carve(PC + 256, [128], BF16)
        NW = carve(PC + 512, [8], F32)
        B31 = carve(PC + 544, [8], F32)
        OV = carve(PC + 576, [4, 128], BF16)
        SELM = carve(PC + 1600, [3, 64], F32, 0, 67)
        CW = carve(PC + 2368, [12], F32)
        HIDB = carve(PC + 2416, [2], F32)
        RELB = carve(PC + 2432, [8], F32, 0, 33)
        RB31 = carve(PC + 2464, [8], F32, 0, 32)
        ONES = carve(PC + 2496, [2], BF16)
        SMALL = carve(PC + 2560, [64], F32)
        c_const = newchan("c_const")
        c_dbg = newchan("c_dbg")
        Bps = [Buf("psm%d" % k) for k in range(4)]
        Bsc = [Buf("pssc%d" % k) for k in range(2)]
        Bsc4 = [Buf("pssch%d" % k) for k in range(4)]
        Bc = Buf("consts")
        Bsmall = Buf("small")

        def finish():
            S.emit(sems)

        def dump(name, ap, dt):
            t = dbg_out(name, list(ap.shape), dt)
            idx = tuple(slice(None) for _ in ap.shape)
            dma("sp", t[idx], ap, c_dbg)

        KVW = carve(PC + 2816, [32], F32)
        CVC = carve(PC + 2944, [4], F32)
        b31_src = bass.AP(relb_d.tensor, 31 * 8, [[0, 128], [1, 8]])
        rb31_src = bass.AP(relb_d.tensor, 31 * 8, [[0, 32], [1, 8]])
        const_loads = [(IDENT, ident_d[:, :]), (ANTI, anti_d[:, :]), (NW, nw_d[:, :]), (OV, ov_d[:, :, :]),
                       (SELM, selm_d[:, :, :]), (CW, cw_d[:, :]), (RELB[0:32, :], relb_d[:, :]),
                       (B31, b31_src), (RB31, rb31_src), (KVW, kvw_d[:, :]), (CVC, cvc_d[:, :]),
                       (KAUG[64:128, 0, :], eall_d[:, :]), (KAUG[64:128, 1, :], eall_d[:, :])]
        for i_, (dst, src_) in enumerate(const_loads):
            dma("sp", dst, src_, c_const, writes=[Bc] if i_ == len(const_loads) - 1 else [])
        Bvi = Buf("vind")
        memset("pool", ONES, 1.0, writes=[Bvi])
        for (V_, one) in ((VS, 65), (VW, 66), (VC, 64)):
            memset("pool", V_[:, :, :, 64:68], 0.0, writes=[Bvi])
            memset("pool", V_[:, :, :, one:one + 1], 1.0, writes=[Bvi])
        memset("pool", VS[:, :, :, 64:65], 0.5, writes=[Bvi])
        memset("pool", VW[:, :, :, 64:65], 0.5, writes=[Bvi])
        for g in range(2):
            tt("dve", VW[:, :, g, 64], VW[:, :, g, 64], KVW, ALU.mult, reads=[Bc, Bvi], writes=[Bvi])
            tt("dve", VW[:, :, g, 66], VW[:, :, g, 66], KVW, ALU.mult, reads=[Bc, Bvi], writes=[Bvi])
            tt("dve", VC[:, :, g, 64], VC[:, :, g, 64], CVC, ALU.mult, reads=[Bc, Bvi], writes=[Bvi])
        KCMPT = carve(P2 + 0, [8192], BF16)
        VCMPT = carve(P2 + 16384, [8192], BF16)
        XNTOWN = carve(P3 + 0, [8, 4, 514], BF16)
        XNTW = [carve(P3 + 32896 + 8192 * k, [8, 512], BF16) for k in range(2)]
        WKV = carve(P3 + 49280, [8, 1024], BF16)
        WSTG = [carve(P3 + 65664 + 2048 * k, [8, 64], F32) for k in range(2)]
        win_v = win_d.rearrange("(c p) n -> p c n", p=128)
        c_w = [newchan("c_w%d" % k) for k in range(2)]
        Bwstg = [Buf("wstg%d" % k) for k in range(2)]
        Bwkv = Buf("wkv")
        wcnt = [0]

        def load_w_block(dst_fn, src_v, col0, ncol, stg, Bdst):
            k_ = wcnt[0] % 2
            wcnt[0] += 1
            dma("sp", stg[k_][:, :, 0:ncol], src_v[:, :, col0:col0 + ncol], c_w[k_], writes=[Bwstg[k_]])
            for c in range(8):
                ts("dve", dst_fn(c), stg[k_][:, c, 0:ncol], NW[:, c:c + 1], ALU.mult,
                   reads=[Bwstg[k_], Bc], writes=[Bdst] if c in (0, 7) else [])

        XT = [carve(P2 + 32768 + 4096 * k, [1024], F32) for k in range(4)]
        Bxt = [Buf("xt%d" % k) for k in range(4)]
        c_x = [newchan("c_x%d" % k) for k in range(4)]
        wmap = [(0, 512, 128), (128, 640, 128), (6 * 128, 896, 128), (7 * 128, 1152, 128)]
        wmap.append((2 * 128, 768, 128))
        wmap.append((3 * 128, 768, 128))
        for g in range(2):
            for dup in range(2):
                wmap.append(((4 + g) * 128 + 64 * dup, 1024 + 64 * g, 64))
        for c in range(8):
            k_ = c % 4
            dma("sp", XT[k_][:, 0:768], win_d[128 * c:128 * c + 128, 512:1280], c_x[k_], writes=[Bxt[k_]])
            for wi_, (dcol, scol, n) in enumerate(wmap):
                ts("dve", WKV[:, c, dcol:dcol + n], XT[k_][:, scol - 512:scol - 512 + n], NW[:, c:c + 1], ALU.mult,
                   reads=[Bxt[k_], Bc], writes=[Bwkv] if (c == 7 and wi_ == len(wmap) - 1) else [])

        OHS = carve(P3 + 0, [LEN_F], F32, 0, 33)
        FROW = carve(P3 + 20992, [LEN_F], BF16, 0, 8)
        Boh = Buf("oh")
        Brelb = Buf("relb")
        c_oh = newchan("c_oh")
        dma("sp", OHS, oh_d[:, :], c_oh, writes=[Boh])
        memset("dve", RELB[32:33, :], NEGBIG, reads=[Bc], writes=[Brelb])
        tt("dve", RELB[0:32, :], RELB[0:32, :], RB31, ALU.subtract, reads=[Bc, Brelb], writes=[Brelb])
        Bfrow = Buf("frow")
        c0 = 0
        k = 0
        while c0 < LEN_F:
            n = min(512, LEN_F - c0)
            bank = k % 2
            mm(ps_m[bank][0:8, 0:n], RELB[0:33, :], OHS[:, c0:c0 + n], True, True, reads=[Brelb, Boh], writes=[Bps[bank]])
            vcopy("dve", FROW[:, c0:c0 + n], ps_m[bank][0:8, 0:n], reads=[Bps[bank]], writes=[Bfrow])
            c0 += n
            k += 1
        Bfd = Buf("fd")
        dma("sp", fd_d[:, :], FROW, c_const, reads=[Bfrow], writes=[Bfd])
        if DEBUG == "0":
            dump("frow", FROW, BF16)
            dump("kaug", KAUG[:, :, 0:2048], BF16)
            finish()
            return nc, dbg

        if DEBUG == "w":
            dump("wkv", WKV, BF16)
            finish()
            return nc, dbg
        XN = [carve(P2 + 49152 + 2048 * k, [1024], BF16) for k in range(4)]
        JUNK = carve(P3 + 69760, [1024], BF16)
        Bxn = [Buf("xn%d" % k) for k in range(4)]
        Bss = [Buf("ss%d" % k) for k in range(8)]
        Bxntw = [Buf("xntw%d" % k) for k in range(2)]
        Bxown = [Buf("xown%d" % k) for k in range(4)]

        SLST = [carve(P3 + 67712 + 1024 * k, [512], BF16) for k in range(2)]
        Bsl = [Buf("sl%d" % k) for k in range(2)]
        c_sl = [newchan("c_sl%d" % k) for k in range(2)]

        def tile_dst(n):
            T = n // 4
            if T % 4 == 3:
                return XNTOWN[:, :, T // 4, 2:514], Bxown[T // 4]
            return XNTW[T % 2], Bxntw[T % 2]

        def stA(n):
            kx = n % 4
            ss = SMALL[:, (n % 8) * 2:(n % 8) * 2 + 1]
            rs = SMALL[:, (n % 8) * 2 + 1:(n % 8) * 2 + 2]
            dma("sp", XT[kx], x_d[128 * n:128 * n + 128, :], c_x[kx], writes=[Bxt[kx]])
            act(JUNK, XT[kx], AF.Square, reads=[Bxt[kx]], writes=[Bss[n % 8]], accum_out=ss)
            ts("dve", rs, ss, 1.0 / DM, ALU.mult, EPS, ALU.add, reads=[Bss[n % 8]], writes=[Bss[n % 8]])

        def stB(n):
            rs = SMALL[:, (n % 8) * 2 + 1:(n % 8) * 2 + 2]
            act(rs, rs, AF.Sqrt, reads=[Bss[n % 8]], writes=[Bss[n % 8]])
            S.op("dve", lambda e, rs=rs: e.reciprocal(out=rs, in_=rs), reads=[Bss[n % 8]], writes=[Bss[n % 8]])

        def stC(n):
            kx = n % 4
            sub = n % 4
            rs = SMALL[:, (n % 8) * 2 + 1:(n % 8) * 2 + 2]
            dstT, dstB = tile_dst(n)
            act(XN[kx], XT[kx], AF.Copy, reads=[Bxt[kx], Bss[n % 8]], writes=[Bxn[kx]], scale=rs)
            pb = 2 + (n % 2)
            pst = ps_m[pb][:, 0:512].bitcast(BF16)
            for c in range(8):
                tr(pst[:, c * 128:(c + 1) * 128], XN[kx][:, c * 128:(c + 1) * 128], reads=[Bxn[kx], Bc], writes=[Bps[pb]])
            vcopy("dve", dstT[:, :, sub * 128:(sub + 1) * 128], pst.rearrange("p (c n) -> p c n", c=8),
                  reads=[Bps[pb]] + ([Bfd] if (n // 4) % 4 == 3 else []), writes=[dstB])

        def projA(T):
            XT_T, BT = tile_dst(4 * T)
            if T % 4 == 2:
                vcopy("pool", XNTOWN[:, :, T // 4, 0:2], XT_T[:, :, 510:512], reads=[BT, Bfd], writes=[Bxown[T // 4]])
            cols = slice(512 * T, 512 * T + 512)
            kcv = KCMPT.rearrange("p (r c) -> p r c", r=16)[:, :, 32 * T:32 * T + 32]
            vcv = VCMPT.rearrange("p (r c) -> p r c", r=16)[:, :, 32 * T:32 * T + 32]
            fm = [(0, ("cm", kcv), 128), (1, ("cm", vcv), 128), (2, None, 128)]
            if T % 4 >= 2:
                w0_ = 512 * (2 * (T // 4) + (T % 4 - 2))
                fm += [(4, KWIN[:, 0, w0_:w0_ + 512], 128), (5, KWIN[:, 1, w0_:w0_ + 512], 128)]
            for ii, (blk, dst, m) in enumerate(fm):
                pb = ii % 4
                pso = ps_sc[pb // 2][0:m, (pb % 2) * 512:(pb % 2) * 512 + 512]
                for c in range(8):
                    mm(pso, WKV[:, c, blk * 128:blk * 128 + m], XT_T[:, c, :], c == 0, c == 7,
                       reads=[BT, Bwkv], writes=[Bsc4[pb]])
                if isinstance(dst, tuple):
                    vcopy("dve", dst[1], pso.rearrange("p (c r) -> p r c", r=16), reads=[Bsc4[pb]])
                elif dst is not None:
                    vcopy("dve", dst, pso, reads=[Bsc4[pb]])
                else:
                    ks = T % 2
                    vcopy("dve", KAUG[0:64, 0, cols], pso[0:64, :], reads=[Bsc4[pb]])
                    vcopy("dve", SLST[ks][64:128, :], pso[64:128, :], reads=[Bsc4[pb]], writes=[Bsl[ks]])
                    dma("pool", KAUG[0:64, 1, cols], SLST[ks][64:128, :], c_sl[ks], reads=[Bsl[ks]])
            for sub in range(4):
                kt = 4 * T + sub
                pb = sub % 2
                pso = ps_m[pb][:, 0:256]
                for c in range(8):
                    mm(pso, XT_T[:, c, sub * 128:(sub + 1) * 128], WKV[:, c, 768:1024], c == 0, c == 7,
                       reads=[BT, Bwkv], writes=[Bps[pb]])
                vcopy("dve", VS[:, kt, :, 0:64], pso[:, 0:128].rearrange("p (g d) -> p g d", g=2), reads=[Bps[pb]])
                if T % 4 >= 2:
                    slot = 8 * (T // 4) + 4 * (T % 4 - 2) + sub
                    vcopy("dve", VW[:, slot, :, 0:64], pso[:, 128:256].rearrange("p (g d) -> p g d", g=2),
                          reads=[Bps[pb]])

        for n in range(4):
            stA(n)
            stB(n)
        for T in range(16):
            for n in range(4 * T, 4 * T + 4):
                stC(n)
            if T + 1 < 16:
                for n in range(4 * T + 4, 4 * T + 8):
                    stA(n)
                    stB(n)
            projA(T)
        W1SX = carve(P2 + 32768, [32, 128], F32)
        W2S = carve(P3 + 66176, [2, 64], F32)
        PETS = carve(P3 + 66816, [2, 32], F32)
        c_a2 = newchan("c_a2")
        c_a2b = newchan("c_a2b")
        Bw1s = Buf("w1s")
        Bw1sx = Buf("w1sx")
        for hf in range(2):
            dma("sp", W1SX[64 * hf:64 * hf + 64, :, :], w1k_d[:, :, :], c_a2b, writes=Bxt + [Bw1sx])
        for kv, (w2d, ped) in enumerate(((w2k_d, pek_d), (w2v_d, pev_d))):
            for hf in range(2):
                dma("sp", PETS[64 * hf:64 * hf + 64, kv, :], ped[:, :], c_a2b, writes=[Bw1sx])
            dma("sp", W2S[:, kv, :], w2d[:, :], c_a2b, writes=[Bw1sx])
        barrier()
        if DEBUG == "A1":
            dump("kaug", KAUG[:, :, 0:2048], BF16)
            dump("vs", VS, BF16)
            dump("kwin", KWIN, BF16)
            dump("vw", VW, BF16)
            finish()
            return nc, dbg

        W1 = [carve(P3 + 32896 + 8192 * k, [32, 128], BF16) for k in range(2)]
        W1S = carve(P3 + 49280, [32, 128], F32)
        W2 = carve(P3 + 65664, [2, 128], BF16)
        PET = carve(P3 + 66688, [2, 32], BF16)
        HSIL = [carve(P3 + 67072 + 1024 * k, [512], BF16) for k in range(2)]
        Bw1 = Buf("w1")
        Bhs = [Buf("hs0"), Buf("hs1")]
        for hf in range(2):
            dma("sp", W1S[64 * hf:64 * hf + 64, :, :], w1v_d[:, :, :], c_a2, writes=[Bw1s])
        for kv in range(2):
            stg, Bstg = (W1SX, Bw1sx) if kv == 0 else (W1S, Bw1s)
            vcopy("dve", W1[kv], stg, reads=[Bstg], writes=[Bw1])
            vcopy("pool", PET[:, kv, :], PETS[:, kv, :], reads=[Bw1sx], writes=[Bw1])
            vcopy("pool", W2[:, kv, 0:64], W2S[:, kv, :], reads=[Bw1sx], writes=[Bw1])
            vcopy("pool", W2[:, kv, 64:128], W2S[:, kv, :], reads=[Bw1sx], writes=[Bw1])
            for j in range(32):
                mm(ps_m[2][:, 0:1], W1[kv][0:64, j, :], PET[0:64, kv, j:j + 1], j == 0, j == 31, reads=[Bw1], writes=[Bps[2]])
            vcopy("dve", HIDB[:, kv:kv + 1], ps_m[2][:, 0:1], reads=[Bps[2]], writes=[Bw1])
        Bvc = Buf("vc")
        memset("pool", HSIL[0][:, 511:512], 0.0, writes=[Bhs[0]])
        memset("pool", HSIL[1][:, 511:512], 0.0, writes=[Bhs[1]])
        for g in range(2):
            memset("pool", KCT[:, g, 511:512], 0.0)
            for kv in range(2):
                src = (KCMPT if kv == 0 else VCMPT)
                pr = slice(64 * g, 64 * g + 64)
                hid = ps_sc[kv][:, 0:511]
                srcv = src[pr, :].rearrange("p (r c) -> p r c", r=16)
                for j in range(32):
                    mm(hid, W1[kv][pr, j, :], srcv[:, j % 16, j // 16:j // 16 + 511], j == 0, j == 31,
                       reads=[Bw1], writes=[Bsc[kv]])
                act(HSIL[kv][:, 0:511], hid, AF.Silu, reads=[Bsc[kv], Bw1], writes=[Bhs[kv]], bias=HIDB[:, kv:kv + 1])
            mm(ps_m[0][:, 0:511], W2[:, 0, :], HSIL[0][:, 0:511], True, True, reads=[Bhs[0], Bw1], writes=[Bps[0]])
            vcopy("dve", KCT[:, g, 0:511], ps_m[0][:, 0:511], reads=[Bps[0]])
            for ct in range(4):
                mm(ps_m[1][:, ct * 64:(ct + 1) * 64], HSIL[1][:, ct * 128:(ct + 1) * 128], W2[:, 1, 0:64], True, True,
                   reads=[Bhs[1], Bw1], writes=[Bps[1]])
            vcopy("dve", VC[:, :, g, 0:64], ps_m[1][:, 0:256].rearrange("p (c d) -> p c d", c=4),
                  reads=[Bps[1]], writes=[Bvc])
            for ct in range(4):
                ts("dve", VC[:, ct, g, 0:64], VC[:, ct, g, 0:64], CVC[:, ct:ct + 1], ALU.mult,
                   reads=[Bvc, Bc], writes=[Bvc])
        barrier()
        if DEBUG == "A":
            dump("kaug", KAUG[:, :, 0:2048], BF16)
            dump("vs", VS, BF16)
            dump("kwin", KWIN, BF16)
            dump("vw", VW, BF16)
            dump("kct", KCT, BF16)
            dump("vc", VC, BF16)
            finish()
            return nc, dbg

        WSTG2 = [carve(P3 + 32896 + 4096 * k, [8, 128], F32) for k in range(2)]
        WG = [carve(P3 + 41088 + 2048 * k, [8, 128], BF16) for k in range(6)]
        HSB = carve(P3 + 53376, [514], F32)
        UU = carve(P3 + 55440, [514], F32)
        YY = carve(P3 + 57504, [512], F32)
        SZC = carve(P3 + 59552, [512], F32)
        BS = carve(P3 + 61600, [512], F32)
        Bwg = [Buf("wg%d" % k) for k in range(6)]
        Bout = Buf("projout")
        Btmp = {n: Buf(n) for n in ("hsb", "uu", "yy", "szc", "bs")}
        wgate_v = wgate_d.rearrange("(c p) n -> p c n", p=128)

        def own_cols(s):
            return XNTOWN[:, :, s, 2:514]

        def proj4(wg, nm, dst_fn):
            for s in range(4):
                pb = s % 4
                pso = ps_sc[pb // 2][0:nm, (pb % 2) * 512:(pb % 2) * 512 + 512]
                for c in range(8):
                    mm(pso, WG[wg][:, c, 0:nm], XNTOWN[:, c, s, 2:514], c == 0, c == 7, reads=[Bwg[wg]], writes=[Bsc4[pb]])
                dst_fn(s, pso, Bsc4[pb])

        wgi = 0
        for qb in range(4):
            w_ = wgi % 6
            wgi += 1
            load_w_block(lambda c, w_=w_: WG[w_][:, c, :], win_v, 128 * qb, 128, WSTG2, Bwg[w_])
            proj4(w_, 128, lambda s, pso, B_, qb=qb: act(QTP[:, qb, s * 512:(s + 1) * 512], pso, AF.Copy,
                                                       reads=[B_], scale=0.125))
        for qb in range(4):
            w_ = wgi % 6
            wgi += 1
            load_w_block(lambda c, w_=w_: WG[w_][:, c, :], win_v, 1304 + 128 * qb, 128, WSTG2, Bwg[w_])
            proj4(w_, 128, lambda s, pso, B_, qb=qb: act(SZP[:, qb, s * 512:(s + 1) * 512], pso, AF.Silu,
                                                       reads=[B_]))
        w_ = wgi % 6
        wgi += 1
        load_w_block(lambda c, w_=w_: WG[w_][:, c, 0:24], wgate_v, 0, 24, WSTG2, Bwg[w_])
        proj4(w_, 24, lambda s, pso, B_: act(G24[:, s * 512:(s + 1) * 512], pso, AF.Sigmoid, reads=[B_]))
        for ch in range(4):
            ws = []
            for (nm_, c0_) in (("h", 1816), ("b", 2328), ("c", 2840), ("z", 3352)):
                w_ = wgi % 6
                wgi += 1
                load_w_block(lambda c, w_=w_: WG[w_][:, c, :], win_v, c0_ + 128 * ch, 128, WSTG2, Bwg[w_])
                ws.append(w_)
            wh, wb, wc_, wz = ws
            for s in range(4):
                qcols = slice(s * 512, (s + 1) * 512)
                ps_h = ps_sc[0][:, 0:512]
                ps_c = ps_sc[0][:, 512:1024]
                ps_z = ps_sc[1][:, 0:512]
                ps_b = ps_sc[1][:, 512:1024]
                ps_h2 = ps_m[0][:, 0:2]
                ps_c2 = ps_m[1][:, 0:2]
                for c in range(8):
                    mm(ps_h, WG[wh][:, c, :], XNTOWN[:, c, s, 2:514], c == 0, c == 7, reads=[Bwg[wh]], writes=[Bsc4[0]])
                for c in range(8):
                    mm(ps_h2, WG[wh][:, c, :], XNTOWN[:, c, s, 0:2], c == 0, c == 7, reads=[Bwg[wh]], writes=[Bps[0]])
                for c in range(8):
                    mm(ps_c, WG[wc_][:, c, :], XNTOWN[:, c, s, 2:514], c == 0, c == 7, reads=[Bwg[wc_]], writes=[Bsc4[1]])
                for c in range(8):
                    mm(ps_c2, WG[wc_][:, c, :], XNTOWN[:, c, s, 0:2], c == 0, c == 7, reads=[Bwg[wc_]], writes=[Bps[1]])
                for c in range(8):
                    mm(ps_z, WG[wz][:, c, :], XNTOWN[:, c, s, 2:514], c == 0, c == 7, reads=[Bwg[wz]], writes=[Bsc4[2]])
                for c in range(8):
                    mm(ps_b, WG[wb][:, c, :], XNTOWN[:, c, s, 2:514], c == 0, c == 7, reads=[Bwg[wb]], writes=[Bsc4[3]])
                act(HSB[:, 2:514], ps_h, AF.Copy, reads=[Bsc4[0]], writes=[Btmp["hsb"]])
                vcopy("dve", HSB[:, 0:2], ps_h2, reads=[Bps[0]], writes=[Btmp["hsb"]])
                tt("dve", UU[:, 2:514], ps_c, HSB[:, 2:514], ALU.mult, reads=[Bsc4[1], Btmp["hsb"]], writes=[Btmp["uu"]])
                tt("dve", UU[:, 0:2], ps_c2, HSB[:, 0:2], ALU.mult, reads=[Bps[1], Btmp["hsb"]], writes=[Btmp["uu"]])
                ts("dve", YY, UU[:, 2:514], CW[:, 3 * ch + 2:3 * ch + 3], ALU.mult, reads=[Btmp["uu"], Bc], writes=[Btmp["yy"]])
                stt("dve", YY, UU[:, 1:513], CW[:, 3 * ch + 1:3 * ch + 2], YY, ALU.mult, ALU.add,
                    reads=[Btmp["uu"], Btmp["yy"], Bc], writes=[Btmp["yy"]])
                stt("dve", YY, UU[:, 0:512], CW[:, 3 * ch:3 * ch + 1], YY, ALU.mult, ALU.add,
                    reads=[Btmp["uu"], Btmp["yy"], Bc], writes=[Btmp["yy"]])
                act(SZC, ps_z, AF.Silu, reads=[Bsc4[2]], writes=[Btmp["szc"]])
                tt("dve", BS, ps_b, YY, ALU.mult, reads=[Bsc4[3], Btmp["yy"]], writes=[Btmp["bs"]])
                tt("pool", MC[:, ch, qcols], BS, SZC, ALU.mult, reads=[Btmp["bs"], Btmp["szc"]])
        barrier()
        if DEBUG == "B":
            dump("qtp", QTP, BF16)
            dump("szp", SZP, BF16)
            dump("mc", MC, BF16)
            dump("g24", G24, F32)
            finish()
            return nc, dbg

        def c3(off, shape, dt, p0=0, p1=128):
            return carve(P3 + off, shape, dt, p0, p1)
        AC = [c3(4096 * k, [4, 512], BF16, 0, 64) for k in range(2)]
        MT0 = [c3(1024 * k, [512], BF16, 64, 128) for k in range(2)]
        MT1 = [c3(2048 + 1024 * k, [512], BF16, 64, 128) for k in range(2)]
        ZSW1 = c3(4096, [512], F32, 64, 65)
        CF1H = c3(6144, [512], BF16, 64, 65)
        CF1L = c3(7168, [512], BF16, 64, 65)
        X0 = [c3(8192 + 1024 * k, [512], BF16) for k in range(2)]
        X1 = [c3(10240 + 1024 * k, [512], BF16) for k in range(2)]
        HS = [c3(12288 + 2048 * k, [1024], BF16) for k in range(2)]
        HW = [c3(16384 + 2816 * k, [1408], BF16) for k in range(2)]
        HC = c3(22016, [8, 512], BF16)
        IMP = c3(30208, [4, 128], F32)
        VCM = c3(32256, [4, 128], F32)
        ADDM = c3(34304, [4, 128], F32)
        PB = [c3(65728 + 2048 * k, [1024], BF16) for k in range(3)]
        OSB_S = c3(40448, [512], F32, 0, 67)
        OSB_W = c3(42496, [512], F32, 0, 67)
        OCSB = c3(44544, [512], F32, 0, 67)
        COEF = c3(46592, [512], F32, 64, 67)
        ZSW = c3(48640, [512], F32, 64, 67)
        SZO = [c3(50688 + 2048 * k, [512], BF16, 0, 64) for k in range(2)]
        MXO = [c3(50688 + 2048 * k + 1024, [512], BF16, 0, 64) for k in range(2)]
        GST = [c3(50688 + 2048 * k, [512], F32, 64, 67) for k in range(2)]
        GST1 = c3(54784, [512], F32, 64, 65)
        SCORE = c3(56832, [4, 128], F32)
        WORK = c3(58880, [4, 128], F32)
        TOP = c3(60928, [4, 16], F32)
        MNB_N = c3(61184, [4, 128], BF16)
        MNB_S = c3(62208, [4, 128], BF16)
        RZT = c3(63232, [4], F32)
        COEFH = c3(63296, [512], BF16, 64, 67)
        COEFL = c3(64320, [512], BF16, 64, 67)
        SELMB = c3(65344, [3, 64], BF16, 0, 67)

        SC = [ps_sc[0], ps_sc[1]]
        OCP = ps_m[2][0:67, :]
        IMPP = ps_m[3][:, :]
        Bsc2 = [Buf("sc0"), Buf("sc1")]
        Bocp = Buf("ocp")
        Bimpp = Buf("impp")

        Bcc = Buf("cconst")
        c_c = newchan("c_cconst")
        vcopy("pool", SELMB, SELM, reads=[Bc], writes=[Bcc])
        Bhc = Buf("hc")
        c_hc = newchan("c_hc")
        for h in range(8):
            dma("sp", HC[:, h, :], bass.AP(fd_d.tensor, h * LEN_F + OFF_C, [[16, 128], [1, 512]]), c_hc, writes=[Bhc])

        Bp = [Buf("p%d" % k) for k in range(3)]
        Bx = [Buf("x%d" % k) for k in range(2)]
        Bhsw = [Buf("hsw%d" % k) for k in range(2)]
        Bimp = Buf("imp")
        Bmask = Buf("mask")
        Bmt = [Buf("mt0"), Buf("mt1")]
        Bac = [[Buf("ac%d_%d" % (p_, k)) for k in range(4)] for p_ in range(2)]
        Bgst = [Buf("gst%d" % k) for k in range(2)]
        Bgst1 = Buf("gst1")
        Bszo = [Buf("szo%d" % k) for k in range(2)]
        Bmxo = [Buf("mxo%d" % k) for k in range(2)]
        Bosb = {n: Buf(n) for n in ("s", "w", "c", "coef", "coefh", "coefl", "zsw", "zsw1", "cf1h", "cf1l",
                                    "score", "work", "top", "mnb", "mnbs", "rzt")}
        Bszp = [Buf("szp%d" % k) for k in range(8)]
        c_tab = [newchan("c_tab%d" % k) for k in range(2)]
        c_mask = newchan("c_mask")
        c_gst = [newchan("c_gst%d" % k) for k in range(2)]
        c_gst1 = newchan("c_gstone")
        c_x2 = [newchan("c_xq%d" % k) for k in range(2)]
        c_szo = [newchan("c_szo%d" % k) for k in range(2)]
        c_mxo = [newchan("c_mxo%d" % k) for k in range(2)]

        class Pipe:
            def __init__(self):
                self.q = []
                self.tick = 0

            def _run_due(self):
                due = [f for (t_, f) in self.q if t_ <= self.tick]
                self.q = [(t_, f) for (t_, f) in self.q if t_ > self.tick]
                for f in due:
                    f()

            def push(self, qk_fn, later=()):
                self.tick += 1
                qk_fn()
                self._run_due()
                for (d_, f) in later:
                    self.q.append((self.tick + d_, f))

            def flush(self):
                while self.q:
                    self.tick += 1
                    self._run_due()

        pipe = Pipe()
        gctr = [0]

        pctr = [0]

        def push_group(it):
            grp = it["group"]
            n = gctr[0]
            gctr[0] += 1
            sc = SC[n % 2]
            Bs_ = Bsc2[n % 2]
            pn = pctr[0]
            pctr[0] += 1
            P_ = PB[pn % 3]
            Bp_ = Bp[pn % 3]
            h = grp[0]["h"]
            width = 512 * len(grp)

            rng = [tl.get("cols", (0, 512)) for tl in grp]
            if len(grp) == 2:
                assert rng[0][1] == 512 and rng[1][0] == 0, rng
            e0 = rng[0][0]
            e1 = 512 * (len(grp) - 1) + rng[-1][1]

            def qk():
                for idx, tl in enumerate(grp):
                    lo, hi = rng[idx]
                    o = sc[:, idx * 512 + lo:idx * 512 + hi]
                    nm = len(tl["mms"])
                    for mi, (l_, r_) in enumerate(tl["mms"]):
                        mm(o, l_, r_[:, lo:hi], mi == 0, mi == nm - 1, reads=tl["reads"], writes=[Bs_])
                act(P_[:, e0:e1], sc[:, e0:e1], AF.Exp, reads=[Bs_, Bc], writes=[Bp_], bias=B31[:, h:h + 1])

            def pv():
                for idx, tl in enumerate(grp):
                    lo, hi = rng[idx]
                    tl["pv"](P_[:, idx * 512 + lo:idx * 512 + hi], Bp_, lo, hi)
            pipe.push(qk, [(1, pv)] + list(it.get("after", [])))

        def special(fn):
            n = gctr[0]
            gctr[0] += 1
            fn(SC[n % 2], Bsc2[n % 2])

        def to_groups(tiles, after):
            out = [dict(group=tiles[i:i + 2]) for i in range(0, len(tiles), 2)]
            out[-1]["after"] = after
            return out

        def run_seq(seq):
            for it in seq:
                if "call" in it:
                    it["call"]()
                else:
                    push_group(it)

        def merge(seq3, inserts):
            out = []
            ti = 0
            inserts = sorted(inserts, key=lambda x: x[0])
            ii = 0
            for it in seq3:
                while ii < len(inserts) and "call" not in it and inserts[ii][0] <= ti:
                    out.extend(inserts[ii][1])
                    ii += 1
                out.append(it)
                if "call" not in it:
                    ti += 1
            while ii < len(inserts):
                out.extend(inserts[ii][1])
                ii += 1
            return out

        insts = [(g, s) for g in range(2) for s in range(4)]
        hctr = [0]

        def stage1_head(idx, r, delays=(1, 3, 8), gst=None):
            GST1_, Bgst1_, c_gst1_ = gst if gst is not None else (GST1, Bgst1, c_gst1)
            g, s = insts[idx]
            p_ = idx % 2
            h = 4 * g + r
            q = h // 2
            pr = slice(64 * (h % 2), 64 * (h % 2) + 64)
            qcols = slice(s * 512, (s + 1) * 512)
            QT_h = QTP[pr, q, qcols]
            items = [dict(call=lambda: dma("sp", GST1_, G24[3 * h:3 * h + 1, qcols], c_gst1_, writes=[Bgst1_]))]
            tiles = []
            pend = []
            for ct in range(s + 1):
                mms = [(KCT[pr, g, ct * 128:(ct + 1) * 128], QT_h)]
                if ct == s:
                    mms.append((ANTI, HC[:, h, :]))

                def pvf(Ph, Bp_, lo, hi, ct=ct):
                    mm(OCP, VC[:, ct, g, 0:67], Ph, ct == 0, ct == s, reads=[Bp_], writes=[Bocp])
                    pend.append((ct, Ph, Bp_))
                    if ct == s:
                        for u in range(4):
                            for (c2, Ph2, Bp2) in pend:
                                mm(IMPP[:, u * 128:(u + 1) * 128], Ph2[:, u * 128:(u + 1) * 128], OV[:, c2, :],
                                   c2 == 0, c2 == s, reads=[Bp2], writes=[Bimpp])
                tiles.append(dict(mms=mms, reads=[Bhc], pv=pvf, h=h))

            def fin1():
                IMPRAW = WORK.rearrange("p u j -> p (u j)")
                vcopy("dve", OCSB, OCP, reads=[Bocp], writes=[Bosb["c"]])
                vcopy("dve", IMPRAW, IMPP, reads=[Bimpp], writes=[Bosb["work"]])
                ts("dve", RZT, WORK[:, :, 127], 1e-30, ALU.max, reads=[Bosb["work"]], writes=[Bosb["rzt"]])
                ts("dve", ZSW1, OCSB[64:65, :], 1e-30, ALU.max, reads=[Bosb["c"]], writes=[Bosb["zsw1"]])
                S.op("dve", lambda e: e.reciprocal(out=RZT, in_=RZT), reads=[Bosb["rzt"]], writes=[Bosb["rzt"]])
                for u in range(4):
                    if r == 0:
                        ts("dve", IMP[:, u, :], WORK[:, u, :], RZT[:, u:u + 1], ALU.mult,
                           reads=[Bosb["work"], Bosb["rzt"]], writes=[Bimp])
                    else:
                        stt("dve", IMP[:, u, :], WORK[:, u, :], RZT[:, u:u + 1], IMP[:, u, :],
                            ALU.mult, ALU.add, reads=[Bosb["work"], Bosb["rzt"], Bimp], writes=[Bimp])

            def fin1b():
                act(ZSW1, ZSW1, AF.Ln, reads=[Bosb["zsw1"]], writes=[Bosb["zsw1"]])
                act(ZSW1, ZSW1, AF.Exp, reads=[Bosb["zsw1"]], writes=[Bosb["zsw1"]], scale=-1.0)
                tt("dve", ZSW1, ZSW1, GST1_, ALU.mult, reads=[Bosb["zsw1"], Bgst1_], writes=[Bosb["zsw1"]])
                vcopy("dve", CF1H, ZSW1, reads=[Bosb["zsw1"]], writes=[Bosb["cf1h"]])
                tt("dve", CF1L, ZSW1, CF1H, ALU.subtract, reads=[Bosb["zsw1"], Bosb["cf1h"]], writes=[Bosb["cf1l"]])

            def fin2():
                bcp = ps_m[3][0:64, :]
                mm(bcp, SELMB[64:65, 0, :], CF1H, True, False, reads=[Bosb["cf1h"], Bcc], writes=[Bimpp])
                mm(bcp, SELMB[64:65, 0, :], CF1L, False, True, reads=[Bosb["cf1l"], Bcc], writes=[Bimpp])
                tt("dve", AC[p_][:, r, :], OCSB[0:64, :], bcp, ALU.mult,
                   reads=[Bosb["c"], Bimpp], writes=[Bac[p_][r]])
            items += to_groups(tiles, [(delays[0], fin1), (delays[1], fin1b), (delays[2], fin2)])
            return items

        def stage2_dve(idx):
            g, s = insts[idx]
            dma("sp", VCM, vcm_d[s, :, :, :], c_mask, writes=[Bmask])
            dma("sp", ADDM, addm_d[s, :, :, :], c_mask, writes=[Bmask])
            tt("dve", SCORE, IMP, VCM, ALU.mult, reads=[Bimp, Bmask], writes=[Bosb["score"]])
            tt("dve", SCORE, SCORE, ADDM, ALU.add, reads=[Bosb["score"], Bmask], writes=[Bosb["score"]])
            Btop = [Buf("top%d" % u) for u in range(4)]
            Bwk = [Buf("wk%d" % u) for u in range(4)]
            for u in range(4):
                S.op("dve", lambda e, u=u: e.max(out=TOP[:, u, 0:8], in_=SCORE[:, u, :]),
                     reads=[Bosb["score"], Bosb["work"], Bosb["top"]], writes=[Btop[u]])
            for u in range(4):
                S.op("dve", lambda e, u=u: e.match_replace(out=WORK[:, u, :], in_to_replace=TOP[:, u, 0:8],
                                                         in_values=SCORE[:, u, :], imm_value=-1e30),
                     reads=[Bosb["score"], Btop[u]], writes=[Bwk[u]])
            for u in range(4):
                S.op("dve", lambda e, u=u: e.max(out=TOP[:, u, 8:16], in_=WORK[:, u, :]),
                     reads=[Bwk[u]], writes=[Btop[u]])
            for u in range(4):
                ts("dve", WORK[:, u, :], SCORE[:, u, :], TOP[:, u, 15:16], ALU.is_ge,
                   reads=[Bosb["score"], Btop[u], Bwk[u]], writes=[Bwk[u]])
            tt("dve", WORK, WORK, VCM, ALU.mult, reads=Bwk + [Bosb["work"], Bmask], writes=Bwk + [Bosb["work"], Bosb["top"]])
            ts("dve", MNB_N, WORK, -1.0, ALU.add, -NEGBIG, ALU.mult, reads=[Bosb["work"]], writes=[Bosb["mnb"]])
            ts("dve", MNB_S[:, :, 0:64], WORK[:, :, 64:128], -1.0, ALU.add, -NEGBIG, ALU.mult,
               reads=[Bosb["work"]], writes=[Bosb["mnbs"]])
            ts("dve", MNB_S[:, :, 64:128], WORK[:, :, 0:64], -1.0, ALU.add, -NEGBIG, ALU.mult,
               reads=[Bosb["work"]], writes=[Bosb["mnbs"]])

        def stage2_pe(idx):
            p_ = idx % 2
            pst = ps_m[2][:, 0:512].bitcast(BF16)
            for u in range(4):
                tr(pst[:, u * 128:(u + 1) * 128], MNB_N[:, u, :], reads=[Bosb["mnb"], Bc], writes=[Bocp])
                tr(pst[:, 512 + u * 128:512 + (u + 1) * 128], MNB_S[:, u, :], reads=[Bosb["mnbs"], Bc], writes=[Bocp])
            vcopy("dve", MT1[p_], pst[64:128, 0:512], reads=[Bocp], writes=[Bmt[p_]])
            vcopy("dve", MT0[p_], pst[64:128, 512:1024], reads=[Bocp], writes=[Bmt[p_]])

        def stage3_head(idx, r):
            g, s = insts[idx]
            p_ = idx % 2
            T = 3 + 4 * s
            h = 4 * g + r
            q = h // 2
            half = h % 2
            pr = slice(64 * half, 64 * half + 64)
            qcols = slice(s * 512, (s + 1) * 512)
            k_ = hctr[0] % 2
            hctr[0] += 1
            QT_h = QTP[pr, q, qcols]

            def setup():
                dma("sp", GST[k_], G24[3 * h:3 * h + 3, qcols], c_gst[k_], writes=[Bgst[k_]])
                dma("sp", HS[k_], bass.AP(fd_d.tensor, h * LEN_F, [[1, 128], [1, 1024]]), c_tab[k_], writes=[Bhsw[k_]])
                dma("sp", HW[k_], bass.AP(fd_d.tensor, h * LEN_F + OFF_W, [[1, 128], [1, 1408]]), c_tab[k_], writes=[Bhsw[k_]])
                if half == 0:
                    vcopy("dve", X0[k_][0:64, :], QTP[0:64, q, qcols], writes=[Bx[k_]])
                    vcopy("dve", X1[k_][0:64, :], QTP[0:64, q, qcols], writes=[Bx[k_]])
                else:
                    dma("pool", X0[k_][0:64, :], QTP[64:128, q, qcols], c_x2[k_], writes=[Bx[k_]])
                    dma("pool", X1[k_][0:64, :], QTP[64:128, q, qcols], c_x2[k_], writes=[Bx[k_]])
                    dma("pool", SZO[k_], SZP[64:128, q, qcols], c_szo[k_], reads=[Bszp[h]], writes=[Bszo[k_]])
                vcopy("dve", X0[k_][64:128, :], MT0[p_], reads=[Bmt[p_]], writes=[Bx[k_]])
                vcopy("dve", X1[k_][64:128, :], MT1[p_], reads=[Bmt[p_]], writes=[Bx[k_]])
            items = [dict(call=setup)]
            nsel = 16 * (s + 1)

            def sel_tile(kt, first, last):
                Xs = X0[k_] if kt < 32 else X1[k_]
                mms = [(KAUG[:, g, kt * 128:(kt + 1) * 128], Xs)]
                delta = 128 * kt - 512 * T
                cols = (0, 512)
                if delta >= -128:
                    col0 = 384 - delta
                    mms.append((ANTI, HS[k_][:, col0:col0 + 512]))
                    if delta > 0:
                        cols = (delta, 512)

                def pvf(Ph, Bp_, lo, hi, kt=kt):
                    mm(ps_m[0][0:67, lo:hi], VS[:, kt, g, 0:67], Ph, first, last, reads=[Bp_], writes=[Bps[0]])
                return dict(mms=mms, reads=[Bx[k_], Bhsw[k_]], pv=pvf, h=h, cols=cols)

            def win_tile(delta, first, last):
                kt = (512 * T + delta) // 128
                slot = 8 * s + (kt - (4 * T - 4))
                col0 = 384 - delta
                mms = [(KWIN[pr, g, slot * 128:(slot + 1) * 128], QT_h), (ANTI, HW[k_][:, col0:col0 + 512])]
                lo = max(0, delta)
                hi = min(512, ((delta + 638) // 128 + 1) * 128)
                hi = min(512, max(hi, 128))

                def pvf(Ph, Bp_, lo_, hi_, slot=slot):
                    mm(ps_m[1][0:67, lo_:hi_], VW[:, slot, g, 0:67], Ph, first, last, reads=[Bp_], writes=[Bps[1]])
                return dict(mms=mms, reads=[Bhsw[k_]], pv=pvf, h=h, cols=(lo, hi))

            far = list(range(0, 4 * T - 1))
            order = [far[0], far[1], 4 * T + 1, far[2], 4 * T + 2, far[3], 4 * T + 3, far[4], 4 * T - 1, 4 * T] + far[5:]
            assert len(order) == nsel and order[-1] in far and len(set(order)) == nsel
            tiles = [sel_tile(kt, i_ == 0, i_ == nsel - 1) for i_, kt in enumerate(order)]
            worder = [0, -256, 256, -384, 384, -512, 128, -128]
            tiles += [win_tile(d_, i_ == 0, i_ == 7) for i_, d_ in enumerate(worder)]

            def fin1():
                vcopy("dve", OSB_S, ps_m[0][0:67, :], reads=[Bps[0]], writes=[Bosb["s"]])
                vcopy("dve", OSB_W, ps_m[1][0:67, :], reads=[Bps[1]], writes=[Bosb["w"]])
                tt("dve", ZSW, OSB_S[64:67, :], OSB_W[64:67, :], ALU.add, reads=[Bosb["s"], Bosb["w"]], writes=[Bosb["zsw"]])
                S.op("dve", lambda e: e.reciprocal(out=ZSW, in_=ZSW), reads=[Bosb["zsw"]], writes=[Bosb["zsw"]])
                tt("dve", COEF, ZSW, GST[k_], ALU.mult, reads=[Bosb["zsw"], Bgst[k_]], writes=[Bosb["coef"]])
                vcopy("dve", COEFH, COEF, reads=[Bosb["coef"]], writes=[Bosb["coefh"]])
                tt("dve", COEFL, COEF, COEFH, ALU.subtract, reads=[Bosb["coef"], Bosb["coefh"]], writes=[Bosb["coefl"]])

            def bc_mm(br):
                bcp = ps_m[3][0:64, :]
                mm(bcp, SELMB[64:67, br, :], COEFH, True, False, reads=[Bosb["coefh"], Bcc], writes=[Bimpp])
                mm(bcp, SELMB[64:67, br, :], COEFL, False, True, reads=[Bosb["coefl"], Bcc], writes=[Bimpp])

            def fin2():
                bc_mm(1)
                tt("dve", OSB_S[0:64, :], OSB_S[0:64, :], ps_m[3][0:64, :], ALU.mult,
                   reads=[Bosb["s"], Bimpp], writes=[Bosb["s"]])

            def fin3():
                bc_mm(2)
                tt("dve", OSB_W[0:64, :], OSB_W[0:64, :], ps_m[3][0:64, :], ALU.mult,
                   reads=[Bosb["w"], Bimpp], writes=[Bosb["w"]])
                tt("dve", OSB_S[0:64, :], OSB_S[0:64, :], OSB_W[0:64, :], ALU.add,
                   reads=[Bosb["s"], Bosb["w"]], writes=[Bosb["s"]])
                tt("dve", OSB_S[0:64, :], OSB_S[0:64, :], AC[p_][:, r, :], ALU.add,
                   reads=[Bosb["s"], Bac[p_][r]], writes=[Bosb["s"]])
                if half == 0:
                    tt("dve", SZP[0:64, q, qcols], OSB_S[0:64, :], SZP[0:64, q, qcols], ALU.mult,
                       reads=[Bosb["s"], Bszp[h]], writes=[Bszp[h]])
                else:
                    tt("dve", MXO[k_], OSB_S[0:64, :], SZO[k_], ALU.mult, reads=[Bosb["s"], Bszo[k_]], writes=[Bmxo[k_]])
                    dma("pool", SZP[64:128, q, qcols], MXO[k_], c_mxo[k_], reads=[Bmxo[k_]], writes=[Bszp[h]])
            items += to_groups(tiles, [(1, fin1), (10, fin2), (12, fin3)])
            return items

        seq = []
        for r in range(4):
            seq += stage1_head(0, r, delays=(1, 1, 2), gst=(GST[r % 2][0:1, :], Bgst[r % 2], c_gst[r % 2]))
        run_seq(seq)
        pipe.flush()
        stage2_dve(0)
        stage2_pe(0)
        for idx in range(8):
            seq3 = []
            for r in range(4):
                seq3 += stage3_head(idx, r)
            if idx + 1 < 8:
                n3 = sum(1 for it in seq3 if "call" not in it)
                ins = []
                step = max(11, int(0.12 * n3))
                for r in range(4):
                    ins.append((1 + r * step, stage1_head(idx + 1, r)))
                p_dve = max(int(0.60 * n3), 1 + 3 * step + 2 + 4)
                ins.append((p_dve, [dict(call=lambda idx=idx: stage2_dve(idx + 1))]))
                ins.append((p_dve + 8, [dict(call=lambda idx=idx: stage2_pe(idx + 1))]))
                seq3 = merge(seq3, ins)
            run_seq(seq3)
        pipe.flush()
        barrier()
        if DEBUG == "C":
            dump("szp", SZP, BF16)
            finish()
            return nc, dbg

        WOA = c3(0, [4, 1024], BF16)
        WOC = c3(8192, [4, 1024], BF16)
        WOS = [c3(16384 + 4096 * k, [1024], F32) for k in range(4)]
        XR = [c3(32768 + 4096 * k, [1024], F32) for k in range(3)]
        YT = [c3(45056 + 4096 * k, [1024], F32) for k in range(4)]
        FNW = c3(61440, [1024], F32)
        JUNKF = c3(65536, [1024], F32)
        c_wo = [newchan("c_wo%d" % k) for k in range(4)]
        c_xr = [newchan("c_xr%d" % k) for k in range(3)]
        c_out = [newchan("c_out%d" % k) for k in range(4)]
        c_fnw = newchan("c_fnw")
        Bwos = [Buf("wos%d" % k) for k in range(4)]
        Bwo = Buf("wo")
        Bxr = [Buf("xr%d" % k) for k in range(3)]
        Byt = [Buf("yt%d" % k) for k in range(4)]
        Bssd = [Buf("ssd%d" % k) for k in range(8)]
        Bfnw = Buf("fnw")
        dma("sp", FNW, bass.AP(fnw_d.tensor, 0, [[0, 128], [1, 1024]]), c_fnw, writes=[Bfnw])
        for rc in range(8):
            k_ = rc % 4
            dma("sp", WOS[k_], wout_d[128 * rc:128 * rc + 128, :], c_wo[k_], writes=[Bwos[k_]])
            dst = WOA[:, rc, :] if rc < 4 else WOC[:, rc - 4, :]
            vcopy("dve", dst, WOS[k_], reads=[Bwos[k_]], writes=[Bwo] if rc in (0, 7) else [])

        def d_st1(n):
            s, sub = n // 4, n % 4
            kp, kx, ky = n % 2, n % 3, n % 4
            tc0 = s * 512 + sub * 128
            L0 = 512 * (3 + 4 * s) + 128 * sub
            dma("pool", XR[kx], x_d[L0:L0 + 128, :], c_xr[kx], writes=[Bxr[kx]])
            for hf in range(2):
                pso = ps_sc[kp][:, hf * 512:(hf + 1) * 512]
                for q in range(4):
                    mm(pso, SZP[:, q, tc0:tc0 + 128], WOA[:, q, hf * 512:(hf + 1) * 512], q == 0, False,
                       reads=[Bwo], writes=[Bsc[kp]])
                for ch in range(4):
                    mm(pso, MC[:, ch, tc0:tc0 + 128], WOC[:, ch, hf * 512:(hf + 1) * 512], False, ch == 3,
                       reads=[Bwo], writes=[Bsc[kp]])
            tt("dve", YT[ky], ps_sc[kp][:, :], XR[kx], ALU.add, reads=[Bsc[kp], Bxr[kx]], writes=[Byt[ky]])

        def d_sq(n):
            ss = SMALL[:, (n % 8) * 2:(n % 8) * 2 + 1]
            act(JUNKF, YT[n % 4], AF.Square, reads=[Byt[n % 4]], writes=[Bssd[n % 8]], accum_out=ss)

        def d_st2a(n):
            ss = SMALL[:, (n % 8) * 2:(n % 8) * 2 + 1]
            rs = SMALL[:, (n % 8) * 2 + 1:(n % 8) * 2 + 2]
            ts("dve", rs, ss, 1.0 / DM, ALU.mult, EPS, ALU.add, reads=[Bssd[n % 8]], writes=[Bssd[n % 8]])
            act(rs, rs, AF.Sqrt, reads=[Bssd[n % 8]], writes=[Bssd[n % 8]])

        def d_st2b(n):
            s, sub = n // 4, n % 4
            ky = n % 4
            tc0 = s * 512 + sub * 128
            rs = SMALL[:, (n % 8) * 2 + 1:(n % 8) * 2 + 2]
            S.op("dve", lambda e, rs=rs: e.reciprocal(out=rs, in_=rs), reads=[Bssd[n % 8]], writes=[Bssd[n % 8]])
            stt("dve", YT[ky], YT[ky], rs, FNW, ALU.mult, ALU.mult, reads=[Byt[ky], Bssd[n % 8], Bfnw], writes=[Byt[ky]])
            dma("sp", out_d[tc0:tc0 + 128, :], YT[ky], c_out[ky], reads=[Byt[ky]])

        d_st1(0)
        d_sq(0)
        for n in range(16):
            if n + 1 < 16:
                d_st1(n + 1)
            d_st2a(n)
            if n + 1 < 16:
                d_sq(n + 1)
            d_st2b(n)
        finish()
    return nc, dbg


_PROG = {}


def _get_program():
    if "p" not in _PROG:
        _PROG["p"] = build_program()
    return _PROG["p"]


def make_in_maps(x, norm_w, w_in, w_ck1, w_ck2, pe_k, w_cv1, w_cv2, pe_v, conv_w, w_out, rel_bias, final_norm_w):
    f32 = np.float32
    x = np.asarray(x, f32)
    w_in0 = np.ascontiguousarray(np.asarray(w_in, f32)[0])
    shared = dict(_shared_consts())
    shared.pop("ov_base")
    gate_cols = [1280 + br * 8 + h for h in range(8) for br in range(3)]
    shared.update(
        w_in=w_in0,
        w_gate=np.ascontiguousarray(w_in0[:, gate_cols]),
        nw=np.ascontiguousarray(np.asarray(norm_w, f32)[0].reshape(8, 128).T),
        w1k=np.ascontiguousarray(np.asarray(w_ck1, f32)[0].transpose(1, 0, 2)),
        w1v=np.ascontiguousarray(np.asarray(w_cv1, f32)[0].transpose(1, 0, 2)),
        w2k=np.ascontiguousarray(np.asarray(w_ck2, f32)[0]),
        w2v=np.ascontiguousarray(np.asarray(w_cv2, f32)[0]),
        pekT=np.ascontiguousarray(np.asarray(pe_k, f32)[0].T),
        pevT=np.ascontiguousarray(np.asarray(pe_v, f32)[0].T),
        cw=np.ascontiguousarray(np.asarray(conv_w, f32)[0].T.reshape(4, 128, 3).transpose(1, 0, 2).reshape(128, 12)),
        w_out=np.ascontiguousarray(np.asarray(w_out, f32)[0]),
        rel_bias=np.ascontiguousarray(np.asarray(rel_bias, f32)),
        fnw=np.ascontiguousarray(np.asarray(final_norm_w, f32)[None, :]),
    )
    in_maps = []
    for core in range(8):
        b, i = core // 4, core % 4
        pad = 1536 - 512 * i
        xc = np.zeros((SEQ, DM), f32)
        xc[pad:] = x[b, :SEQ - pad]
        m = dict(shared)
        m.update(_core_consts(i))
        m["x"] = xc
        in_maps.append(m)
    return in_maps


def kernel(x, norm_w, w_in, w_ck1, w_ck2, pe_k, w_cv1, w_cv2, pe_v, conv_w, w_out, rel_bias, final_norm_w):
    nc, _ = _get_program()
    in_maps = make_in_maps(x, norm_w, w_in, w_ck1, w_ck2, pe_k, w_cv1, w_cv2, pe_v, conv_w, w_out, rel_bias,
                           final_norm_w)
    res = run_bass_kernel_spmd(nc, in_maps, core_ids=list(range(8)))
    out = np.zeros((2, SEQ, DM), np.float32)
    for core in range(8):
        b, i = core // 4, core % 4
        o = res.results[core]["out"]
        for s in range(4):
            a = 4 * s + i
            out[b, 512 * a:512 * a + 512] = o[512 * s:512 * s + 512]
    return out
```
